# Optimizing a Trainium2 kernel written in Bass

```python
import math
import jax, jax.numpy as jnp
from jax import lax
import numpy as np

D_MODEL = 1024
BATCH = 4
SEQ = 4096
DEPTH = 4

GRID_W = 64
CTX_LEN = 256
N_EVEN = (DEPTH + 1) // 2
N_ODD = DEPTH // 2
N_MOD = 6
NORM_EPS = 1e-6
W_A = D_MODEL // 2
HY_ORDER = 2
HY_SHORT = 3
HY_BANDS = 16
HY_EMB = 2 * HY_BANDS + 1
HY_FFN = 64
HY_TARGET = 1e-2
HY_FAST_PCT = 0.3
HY_SLOW_PCT = 1.5
W_B = D_MODEL // 2
SGU_GROUPS = 4
SGU_DH = W_B // SGU_GROUPS
CHUNK = 128
D_RNN = ((4 * D_MODEL // 3 + 127) // 128) * 128
RG_HEADS = 16
RG_DH = D_RNN // RG_HEADS
RG_CONV = 4
RG_C = 8.0
D_FF = ((8 * D_MODEL // 3 + 255) // 256) * 256

kernel_name = 'hybrid_hyena_sgu_rglru_dit_block'


def rmsnorm(x, g):
    xf = x.astype(jnp.float32)
    xf = xf * lax.rsqrt(jnp.mean(xf * xf, axis=-1, keepdims=True) + NORM_EPS)
    return xf.astype(x.dtype) * g


def layernorm(x, g):
    xf = x.astype(jnp.float32)
    xc = xf - jnp.mean(xf, axis=-1, keepdims=True)
    xf = xc * lax.rsqrt(jnp.mean(xc * xc, axis=-1, keepdims=True) + NORM_EPS)
    return xf.astype(x.dtype) * g


def modulate(h, shift, scale):
    return h * (1 + scale) + shift


def depthwise_conv(x, w, b, left):
    k_w, L = w.shape[0], x.shape[1]
    xp = jnp.pad(x, ((0, 0), (left, k_w - 1 - left), (0, 0)))
    y = b
    for k in range(k_w):
        y = y + xp[:, k:k + L] * w[k]
    return y


def to_col_major(x):
    b, L, d = x.shape
    rows = L // GRID_W
    return x.reshape(b, rows, GRID_W, d).transpose(0, 2, 1, 3).reshape(b, L, d)


def to_row_major(x):
    b, L, d = x.shape
    rows = L // GRID_W
    return x.reshape(b, GRID_W, rows, d).transpose(0, 2, 1, 3).reshape(b, L, d)


def swiglu(h, w_in, w_out):
    z = h @ w_in
    return (jax.nn.silu(z[..., :D_FF]) * z[..., D_FF:]) @ w_out


def hyena_filter_spectrum(L, f1_w, f1_b, f2_w, f2_b, f3_w, f3_b, sin_freq):
    t = jnp.linspace(0.0, 1.0, L, dtype=jnp.float32)[:, None]
    w = 2.0 * math.pi * jnp.arange(L, dtype=jnp.float32)[:, None] / L
    f = jnp.linspace(1e-4, HY_BANDS - 1, HY_BANDS, dtype=jnp.float32)[None, :]
    emb = jnp.concatenate([t, jnp.cos(f * w), -jnp.sin(f * w)], axis=-1)
    hdn = jnp.sin(sin_freq * (emb @ f1_w + f1_b))
    hdn = jnp.sin(sin_freq * (hdn @ f2_w + f2_b))
    h = (hdn @ f3_w + f3_b).astype(jnp.float32).reshape(L, HY_ORDER, 2, W_A)
    deltas = jnp.abs(jnp.linspace(math.log(HY_TARGET) / HY_SLOW_PCT,
                                  math.log(HY_TARGET) / HY_FAST_PCT, W_A, dtype=jnp.float32))
    h = h * jnp.exp(-t[:, :, None, None] * deltas)
    k = jnp.concatenate([h[:, :, 0], jnp.zeros((1, HY_ORDER, W_A), jnp.float32), h[:0:-1, :, 1]], axis=0)
    k = k * lax.rsqrt(jnp.sum(k * k, axis=0, keepdims=True) + NORM_EPS)
    return jnp.fft.rfft(k, n=2 * L, axis=0)


def long_conv(z, kf):
    L = z.shape[1]
    zf = jnp.fft.rfft(z.astype(jnp.float32), n=2 * L, axis=1)
    return jnp.fft.irfft(zf * kf, n=2 * L, axis=1)[:, :L].astype(z.dtype)


def hyena_sgu_mixer(h, w_in, w_out, hy_conv_w, hy_conv_b, hy_f1_w, hy_f1_b, hy_f2_w, hy_f2_b,
                    hy_f3_w, hy_f3_b, hy_sin_freq, hy_skip, sgu_ln_g, sgu_w, sgu_b):
    b, L, _ = h.shape
    z = h @ w_in
    za = depthwise_conv(z[..., :3 * W_A], hy_conv_w, hy_conv_b, left=1)
    v, g1, g2 = za[..., :W_A], za[..., W_A:2 * W_A], za[..., 2 * W_A:]
    kf = hyena_filter_spectrum(L, hy_f1_w, hy_f1_b, hy_f2_w, hy_f2_b, hy_f3_w, hy_f3_b, hy_sin_freq)
    y = g1 * (long_conv(v, kf[:, 0]) + hy_skip[0] * v)
    y_a = g2 * (long_conv(y, kf[:, 1]) + hy_skip[1] * y)
    zb = jax.nn.gelu(z[..., 3 * W_A:])
    u, vb = zb[..., :W_B], zb[..., W_B:]
    vb = layernorm(vb, sgu_ln_g).reshape(b, L // CHUNK, CHUNK, SGU_GROUPS, SGU_DH)
    s = jnp.einsum('gpq,bnqgd->bnpgd', sgu_w, vb) + sgu_b.T[None, None, :, :, None]
    y_b = u * s.reshape(b, L, W_B)
    return jnp.concatenate([y_a, y_b], axis=-1) @ w_out


def rglru_coeffs(xb, w_a, b_a, w_x, b_x, lam):
    b, L, _ = xb.shape
    xh = xb.reshape(b, L, RG_HEADS, RG_DH)
    r = jax.nn.sigmoid(jnp.einsum('blhd,hde->blhe', xh, w_a).reshape(b, L, D_RNN) + b_a)
    gi = jax.nn.sigmoid(jnp.einsum('blhd,hde->blhe', xh, w_x).reshape(b, L, D_RNN) + b_x)
    log_a = (-RG_C * r.astype(jnp.float32)) * jax.nn.softplus(-lam.astype(jnp.float32))
    a = jnp.exp(log_a)
    bx = jnp.sqrt(-jnp.expm1(2.0 * log_a)) * (gi * xb).astype(jnp.float32)
    return a, bx


def linear_scan(a, bx, h0, reverse):
    if h0 is not None:
        edge = -1 if reverse else 0
        bx = bx.at[:, edge].add(a[:, edge] * h0)

    def combine(e1, e2):
        a1, b1 = e1
        a2, b2 = e2
        return a1 * a2, a2 * b1 + b2

    _, h = lax.associative_scan(combine, (a, bx), reverse=reverse, axis=1)
    return h


def bidir_rglru_mixer(h_lat, h_ctx, col_major, ctx_out, w_in, conv_w, conv_b, wa, ba, wx, bx, lam, w_out):
    xc = depthwise_conv(h_ctx @ w_in[:, D_RNN:], conv_w, conv_b, left=2)
    af, bf = rglru_coeffs(xc, wa[0], ba[0], wx[0], bx[0], lam[0])
    ab, bb = rglru_coeffs(xc, wa[1], ba[1], wx[1], bx[1], lam[1])
    hf_c = linear_scan(af, bf, None, False)
    hb_c = linear_scan(ab, bb, None, True)
    if col_major:
        h_lat = to_col_major(h_lat)
    z = h_lat @ w_in
    gate = jax.nn.gelu(z[..., :D_RNN])
    xl = depthwise_conv(z[..., D_RNN:], conv_w, conv_b, left=2)
    af, bf = rglru_coeffs(xl, wa[0], ba[0], wx[0], bx[0], lam[0])
    ab, bb = rglru_coeffs(xl, wa[1], ba[1], wx[1], bx[1], lam[1])
    hf = linear_scan(af, bf, hf_c[:, -1], False)
    hb = linear_scan(ab, bb, hb_c[:, 0], True)
    y_lat = (gate * (hf + hb).astype(gate.dtype)) @ w_out
    if col_major:
        y_lat = to_row_major(y_lat)
    if ctx_out:
        gate_c = jax.nn.gelu(h_ctx @ w_in[:, :D_RNN])
        y_ctx = (gate_c * (hf_c + hb_c).astype(gate_c.dtype)) @ w_out
        return y_lat, y_ctx
    return y_lat, None


def setup_inputs(seed: int = 0) -> dict:
    key = jax.random.key(seed)
    keys = jax.random.split(key, 40)
    counter = [0]

    def nxt():
        k = keys[counter[0]]
        counter[0] += 1
        return k

    def nrm(shape, scale):
        return jax.random.normal(nxt(), shape, jnp.float32) * scale

    def gain(shape):
        return 1.0 + nrm(shape, 0.05)

    D = D_MODEL
    u = jax.random.uniform(nxt(), (N_ODD, 2, D_RNN), jnp.float32, minval=0.9, maxval=0.999)
    sig = u ** (1.0 / RG_C)
    rg_lam = jnp.log(sig / (1.0 - sig))
    return {
        'x': nrm((BATCH, SEQ, D), 1.0),
        'c': nrm((BATCH, D), 1.0),
        'ctx': nrm((BATCH, CTX_LEN, D), 1.0),
        'c_ctx': nrm((D,), 1.0),
        'w_ada': nrm((DEPTH, D, N_MOD * D), 0.5 * D ** -0.5),
        'b_ada': nrm((DEPTH, N_MOD * D), 0.02),
        'norm_mix_g': gain((DEPTH, D)),
        'norm_ffn_g': gain((DEPTH, D)),
        'w_in_even': nrm((N_EVEN, D, 3 * W_A + 2 * W_B), D ** -0.5),
        'w_out_even': nrm((N_EVEN, W_A + W_B, D), (W_A + W_B) ** -0.5),
        'hy_conv_w': nrm((N_EVEN, HY_SHORT, 3 * W_A), HY_SHORT ** -0.5),
        'hy_conv_b': nrm((N_EVEN, 3 * W_A), 0.02),
        'hy_f1_w': nrm((N_EVEN, HY_EMB, HY_FFN), HY_EMB ** -0.5),
        'hy_f1_b': nrm((N_EVEN, HY_FFN), 0.1),
        'hy_f2_w': nrm((N_EVEN, HY_FFN, HY_FFN), HY_FFN ** -0.5),
        'hy_f2_b': nrm((N_EVEN, HY_FFN), 0.1),
        'hy_f3_w': nrm((N_EVEN, HY_FFN, HY_ORDER * 2 * W_A), HY_FFN ** -0.5),
        'hy_f3_b': nrm((N_EVEN, HY_ORDER * 2 * W_A), 0.02),
        'hy_sin_freq': gain((N_EVEN, HY_FFN)),
        'hy_skip': nrm((N_EVEN, HY_ORDER, W_A), 0.5),
        'sgu_ln_g': gain((N_EVEN, W_B)),
        'sgu_w': nrm((N_EVEN, SGU_GROUPS, CHUNK, CHUNK), CHUNK ** -0.5),
        'sgu_b': 1.0 + nrm((N_EVEN, SGU_GROUPS, CHUNK), 0.1),
        'w_in_odd': nrm((N_ODD, D, 2 * D_RNN), D ** -0.5),
        'rg_conv_w': nrm((N_ODD, RG_CONV, D_RNN), RG_CONV ** -0.5),
        'rg_conv_b': nrm((N_ODD, D_RNN), 0.02),
        'rg_wa': nrm((N_ODD, 2, RG_HEADS, RG_DH, RG_DH), RG_DH ** -0.5),
        'rg_ba': nrm((N_ODD, 2, D_RNN), 0.02),
        'rg_wx': nrm((N_ODD, 2, RG_HEADS, RG_DH, RG_DH), RG_DH ** -0.5),
        'rg_bx': nrm((N_ODD, 2, D_RNN), 0.02),
        'rg_lam': rg_lam,
        'w_out_odd': nrm((N_ODD, D_RNN, D), D_RNN ** -0.5),
        'w_ffn_in': nrm((DEPTH, D, 2 * D_FF), D ** -0.5),
        'w_ffn_out': nrm((DEPTH, D_FF, D), D_FF ** -0.5),
        'final_norm_g': gain((D,)),
    }


def reference(x, c, ctx, c_ctx, w_ada, b_ada, norm_mix_g, norm_ffn_g,
              w_in_even, w_out_even, hy_conv_w, hy_conv_b, hy_f1_w, hy_f1_b, hy_f2_w, hy_f2_b,
              hy_f3_w, hy_f3_b, hy_sin_freq, hy_skip, sgu_ln_g, sgu_w, sgu_b,
              w_in_odd, rg_conv_w, rg_conv_b, rg_wa, rg_ba, rg_wx, rg_bx, rg_lam, w_out_odd,
              w_ffn_in, w_ffn_out, final_norm_g):
    silu_c = jax.nn.silu(c)
    silu_cc = jax.nn.silu(c_ctx)
    for l in range(DEPTH):
        run_ctx = l < DEPTH - 1
        is_rec = l % 2 == 1
        i = l // 2
        mod_x = (silu_c @ w_ada[l] + b_ada[l]).reshape(-1, 1, N_MOD, D_MODEL)
        hx = modulate(rmsnorm(x, norm_mix_g[l]), mod_x[:, :, 0], mod_x[:, :, 1])
        if run_ctx or is_rec:
            mod_c = (silu_cc @ w_ada[l] + b_ada[l]).reshape(1, 1, N_MOD, D_MODEL)
            hc = modulate(rmsnorm(ctx, norm_mix_g[l]), mod_c[:, :, 0], mod_c[:, :, 1])
        if is_rec:
            mix_x, mix_c = bidir_rglru_mixer(hx, hc, i % 2 == 1, run_ctx, w_in_odd[i], rg_conv_w[i],
                                             rg_conv_b[i], rg_wa[i], rg_ba[i], rg_wx[i], rg_bx[i],
                                             rg_lam[i], w_out_odd[i])
        else:
            ep = (w_in_even[i], w_out_even[i], hy_conv_w[i], hy_conv_b[i], hy_f1_w[i], hy_f1_b[i],
                  hy_f2_w[i], hy_f2_b[i], hy_f3_w[i], hy_f3_b[i], hy_sin_freq[i], hy_skip[i],
                  sgu_ln_g[i], sgu_w[i], sgu_b[i])
            mix_x = hyena_sgu_mixer(hx, *ep)
            mix_c = hyena_sgu_mixer(hc, *ep) if run_ctx else None
        x = x + mod_x[:, :, 2] * mix_x
        hx = modulate(rmsnorm(x, norm_ffn_g[l]), mod_x[:, :, 3], mod_x[:, :, 4])
        x = x + mod_x[:, :, 5] * swiglu(hx, w_ffn_in[l], w_ffn_out[l])
        if run_ctx:
            ctx = ctx + mod_c[:, :, 2] * mix_c
            hc = modulate(rmsnorm(ctx, norm_ffn_g[l]), mod_c[:, :, 3], mod_c[:, :, 4])
            ctx = ctx + mod_c[:, :, 5] * swiglu(hc, w_ffn_in[l], w_ffn_out[l])
    return rmsnorm(x, final_norm_g)
```

```python
import contextlib
import math
import numpy as np
import ml_dtypes
import concourse.bass as bass
import concourse.mybir as mybir
from concourse.bass_utils import run_bass_kernel_spmd

F32 = mybir.dt.float32
BF16 = mybir.dt.bfloat16
AF = mybir.ActivationFunctionType
ALU = mybir.AluOpType
AX = mybir.AxisListType

D = 1024
KD = 8
T = 4096
TC = 256
TA = T + TC
DFF = 2816
DR = 1408
NH = 16
HD = 88
EPS = 1e-6
NF = 33
NFC = 3


class Buf:
    __slots__ = ("w", "r", "name")

    def __init__(self, name=""):
        self.w = []
        self.r = []
        self.name = name


class KB:
    def __init__(self):
        self.nc = bass.Bass("TRN2", target_bir_lowering=False)
        self.st = contextlib.ExitStack()
        self.E = ["pe", "act", "dve", "pool", "sp"]
        self.sem = {e: self.st.enter_context(self.nc.semaphore("sem_" + e)) for e in self.E}
        self.cnt = {e: 0 for e in self.E}
        self.ops = {e: [] for e in self.E}
        self.waited = {e: {} for e in self.E}
        self.dq = {}
        self.dqi = {}
        self.dqlast = {}
        for q, n in (("sp", 16), ("pool", 16), ("act", 8)):
            self.dq[q] = [self.st.enter_context(self.nc.semaphore("dq_%s%d" % (q, i))) for i in range(n)]
            self.dqi[q] = 0
        self.nops = 0
        self.dummy = self.st.enter_context(self.nc.sbuf_tensor("kbdummy", [128, 8], F32))
        self.op("dve", lambda eh: eh.memset(self.dummy[:, 0:8], 0.0), (), ())

    def _collect(self, e, reads, writes):
        deps = {}

        def add(lst):
            for (sm, v) in lst:
                if deps.get(sm, 0) < v:
                    deps[sm] = v
        for b in reads:
            add(b.w)
        for b in writes:
            add(b.w)
            add(b.r)
        res = []
        for sm, v in deps.items():
            if e == "pe" and sm == self.sem["pe"]:
                continue
            if self.waited[e].get(sm, 0) >= v:
                continue
            self.waited[e][sm] = v
            res.append((sm, v))
        return res

    def _update(self, dep, reads, writes):
        for b in writes:
            b.w = [dep]
            b.r = []
        for b in reads:
            if b in writes:
                continue
            b.r = [d for d in b.r if d[0] != dep[0]] + [dep]

    def op(self, e, fn, reads=(), writes=()):
        waits = self._collect(e, reads, writes)
        self.cnt[e] += 1
        dep = (self.sem[e], self.cnt[e])
        self.ops[e].append((waits, fn, (self.sem[e], 1)))
        self._update(dep, reads, writes)
        self.nops += 1

    def dma(self, q, out, in_, reads=(), writes=()):
        waits = self._collect(q, reads, writes)
        i = self.dqi[q]
        self.dqi[q] += 1
        n = len(self.dq[q])
        slot = self.dq[q][i % n]
        tgt = 16 * (i // n + 1)
        if i >= n:
            prev = 16 * (i // n)
            if self.waited[q].get(slot, 0) < prev:
                self.waited[q][slot] = prev
                waits.append((slot, prev))
        self.ops[q].append((waits, lambda eh: eh.dma_start(out=out, in_=in_), (slot, 16)))
        self.dqlast[slot] = tgt
        self._update((slot, tgt), reads, writes)
        self.nops += 1

    def barrier(self):
        self._barrier1()
        d = self.dummy
        self.op("act", lambda eh: eh.activation(out=d[:, 1:2], in_=d[:, 0:1], func=AF.Copy), (), ())
        self.op("dve", lambda eh: eh.memset(d[:, 2:3], 0.0), (), ())
        self.op("pool", lambda eh: eh.memset(d[:, 3:4], 0.0), (), ())
        self._barrier1()

    def _barrier1(self):
        deps = [(self.sem[e], self.cnt[e]) for e in self.E if self.cnt[e] > 0]
        deps += list(self.dqlast.items())
        for e in self.E:
            waits = []
            for sm, v in deps:
                if sm == self.sem[e]:
                    continue
                if self.waited[e].get(sm, 0) >= v:
                    continue
                self.waited[e][sm] = v
                waits.append((sm, v))
            if waits:
                self.ops[e].append((waits, None, None))

    def finish(self):
        self.barrier()
        with self.nc.Block() as block:
            def emit(e):
                def f(eh):
                    for waits, fn, inc in self.ops[e]:
                        for sm, v in waits:
                            eh.wait_ge(sm, v)
                        if fn is not None:
                            fn(eh).then_inc(*inc)
                return f
            block.tensor(emit("pe"))
            block.scalar(emit("act"))
            block.vector(emit("dve"))
            block.gpsimd(emit("pool"))
            block.sync(emit("sp"))
        self.st.close()

    def mm(self, out, lhsT, rhs, start, stop, reads, writes):
        self.op("pe", lambda eh: eh.matmul(out, lhsT, rhs, start=start, stop=stop), reads, writes)

    def tr(self, out, in_, ident, reads, writes):
        self.op("pe", lambda eh: eh.transpose(out, in_, ident), reads, writes)

    def act(self, out, in_, func, reads, writes, bias=None, scale=None, accum_out=None):
        kw = {}
        if bias is not None:
            kw["bias"] = bias
        if scale is not None:
            kw["scale"] = scale
        if accum_out is not None:
            kw["accum_out"] = accum_out
        self.op("act", lambda eh: eh.activation(out=out, in_=in_, func=func, **kw), reads, writes)

    def tt(self, e, out, in0, in1, op, reads, writes):
        self.op(e, lambda eh: eh.tensor_tensor(out=out, in0=in0, in1=in1, op=op), reads, writes)

    def ts(self, e, out, in0, s1, s2, op0, op1, reads, writes):
        if s2 is None:
            self.op(e, lambda eh: eh.tensor_scalar(out=out, in0=in0, scalar1=s1, scalar2=None, op0=op0), reads, writes)
        else:
            self.op(e, lambda eh: eh.tensor_scalar(out=out, in0=in0, scalar1=s1, scalar2=s2, op0=op0, op1=op1), reads, writes)

    def stt(self, e, out, in0, scalar, in1, op0, op1, reads, writes):
        self.op(e, lambda eh: eh.scalar_tensor_tensor(out=out, in0=in0, scalar=scalar, in1=in1, op0=op0, op1=op1), reads, writes)

    def copy(self, e, out, in_, reads, writes):
        self.op(e, lambda eh: eh.tensor_copy(out=out, in_=in_), reads, writes)

    def memset(self, e, ap, val, writes):
        self.op(e, lambda eh: eh.memset(ap, val), (), writes)


class Arena:
    def __init__(self, k, nbytes):
        self.k = k
        self.cap = nbytes
        self.t = k.st.enter_context(k.nc.sbuf_tensor("arena", [128, nbytes // 4], F32))
        self.top = 0

    def alloc(self, free_shape, dtype, parts=128):
        es = 4 if dtype == F32 else 2
        n = int(np.prod(free_shape))
        nb = (n * es + 63) // 64 * 64
        off = self.top
        self.top += nb
        assert self.top <= self.cap, "SBUF arena overflow %d > %d" % (self.top, self.cap)
        assert (n * es) % 4 == 0
        ap = self.t[0:parts, off // 4: off // 4 + (n * es) // 4]
        if dtype == BF16:
            ap = ap.bitcast(BF16)
        fs = list(free_shape)
        if len(fs) == 2:
            ap = ap.rearrange("p (a b) -> p a b", a=fs[0])
        elif len(fs) == 3:
            ap = ap.rearrange("p (a b c) -> p a b c", a=fs[0], b=fs[1])
        elif len(fs) == 4:
            ap = ap.rearrange("p (a b c d) -> p a b c d", a=fs[0], b=fs[1], c=fs[2])
        return ap


class Ring:
    def __init__(self, arena, n, free_shape, dtype, name, parts=128):
        self.aps = [arena.alloc(free_shape, dtype, parts) for _ in range(n)]
        self.bufs = [Buf("%s%d" % (name, i)) for i in range(n)]
        self.i = 0

    def get(self):
        j = self.i % len(self.aps)
        self.i += 1
        return self.aps[j], self.bufs[j]


def build(nl=4, dbg=False, stop=10**9):
    k = KB()
    phc = [0]

    def go(name):
        phc[0] += 1
        import os
        skip = [int(v) for v in os.environ.get('SKIPPH', '').split(',') if v]
        ok = phc[0] <= stop and phc[0] not in skip
        if ok:
            print('phase', phc[0], name, flush=True)
        return ok
    nc = k.nc

    def din(name, shape, dt=F32):
        return nc.dram_tensor(name, list(shape), dt, kind="ExternalInput").ap()

    def dscr(name, shape, dt=F32):
        return nc.dram_tensor(name, list(shape), dt).ap()

    I = {}
    for name, shape, dt in INPUT_SPECS:
        I[name] = din(name, shape, dt)
    out_d = nc.dram_tensor("out", [T, D], F32, kind="ExternalOutput").ap()
    if dbg:
        dbg_d = nc.dram_tensor("dbg", [TA, D], F32, kind="ExternalOutput").ap()

    X = dscr("Xs", [TA, D])
    XP = dscr("XPs", [T, D])
    ZA = dscr("ZAs", [128, 12, TA])
    G1 = dscr("G1s", [128, 4, TA])
    G2 = dscr("G2s", [128, 4, TA])
    YAd = dscr("YAs", [128, 4, TA], BF16)
    YBd = dscr("YBs", [128, 4, TA], BF16)
    KF = dscr("KFs", [2, 2, NF, 128, 2, 512])
    ZRd = dscr("ZRs", [HD, NH, TA])
    GTd = dscr("GTs", [HD, NH, TA], BF16)
    YRd = dscr("YRs", [HD, NH, TA], BF16)
    gX = [Buf("X%d" % i) for i in range(TA // 128)]
    gXP = [Buf("XP%d" % i) for i in range(T // 128)]
    gZA, gG1, gG2, gYA, gYB, gKF, gZR, gGT, gYR = (Buf(n) for n in
                                                   ("ZA", "G1", "G2", "YA", "YB", "KF", "ZR", "GT", "YR"))

    ar = Arena(k, 212480)
    ps_t = [k.st.enter_context(nc.psum_tensor("ps%d" % i, [128, 512], F32)) for i in range(8)]
    ps_b = [Buf("ps%d" % i) for i in range(8)]
    ps_i = [0]

    ps_lim = [0, 8]

    def ps():
        lo, hi = ps_lim
        j = lo + ps_i[0] % (hi - lo)
        ps_i[0] += 1
        return ps_t[j][:, :], ps_b[j]

    ident = ar.alloc([128], F32)
    g_ident = Buf("ident")
    k.dma("sp", ident, I["ident"], (), (g_ident,))
    cst = ar.alloc([8], F32)
    g_cst = Buf("cst")
    k.memset("dve", cst[:, 0:1], EPS, (g_cst,))
    k.memset("dve", cst[:, 1:2], 1.0, (g_cst,))
    k.memset("dve", cst[:, 2:3], 0.0, (g_cst,))
    ones_f = ar.alloc([128], F32)
    g_ones = Buf("ones")
    k.memset("dve", ones_f, 1.0, (g_ones,))
    cc_f = ar.alloc([KD, 2], F32)
    scT = ar.alloc([KD, 2], BF16)
    screp = ar.alloc([KD, 2, 128], BF16)
    g_sc = Buf("sc")
    k.dma("sp", cc_f, I["ccT"], (), (g_sc,))
    sc_f = ar.alloc([KD, 2], F32)
    k.act(sc_f, cc_f, AF.Silu, (g_sc,), (g_sc,))
    k.copy("dve", scT, sc_f, (g_sc,), (g_sc,))
    for kk in range(KD):
        for w in range(2):
            k.act(screp[:, kk, w, :], ones_f, AF.Identity, (g_ones, g_sc), (g_sc,), scale=sc_f[:, kk, w:w + 1])
    modf = ar.alloc([2, 6, KD], F32)
    AB = ar.alloc([2, 4, KD], F32)
    gbc = ar.alloc([2, 2, D], F32)
    g_mod = Buf("mod")
    g_gbc = Buf("gbc")
    stage = Ring(ar, 4, [512], F32, "stage")
    persist_top = ar.top

    def xrows(buf_ap, perm, pos0):
        if pos0 >= T or not perm:
            return [(0, 128, buf_ap[pos0:pos0 + 128, :])]
        c0 = pos0 // 64
        v = buf_ap[0:T, :].rearrange("(r c) d -> c r d", c=64)
        return [(0, 64, v[c0]), (64, 128, v[c0 + 1])]

    def xguard(which, pos0):
        return (gXP if which == "XP" else gX)[pos0 // 128]

    def load_x(q, tile_ap, tile_g, src, perm, pos0):
        buf_ap = XP if src == "XP" else X
        for (p0, p1, rows) in xrows(buf_ap, perm, pos0):
            k.dma(q, tile_ap[p0:p1, :], rows, (xguard(src, pos0),) if not perm or pos0 >= T else tuple(gX[:T // 128]), (tile_g,))

    def store_x(q, tile_ap, tile_g, dst, pos0):
        buf_ap = XP if dst == "XP" else X
        k.dma(q, buf_ap[pos0:pos0 + 128, :], tile_ap, (tile_g,), (xguard(dst, pos0),))

    def tile_list(n, with_ctx=True):
        res = [(t0, n) for t0 in range(0, T, n)]
        if with_ctx:
            res += [(T + t0, min(n, TC)) for t0 in range(0, TC, n)]
        return res

    class WT:
        def __init__(self, ap, nk, ncol, step=512):
            self.ap = ap
            self.step = step
            self.nk = nk
            self.ncol = ncol
            self.g = [[Buf("w") for _ in range((ncol + step - 1) // step)] for _ in range(nk)]

        def gs(self, kk, c0, c1):
            return tuple(self.g[kk][j] for j in range(c0 // self.step, (c1 - 1) // self.step + 1))

    def load_cast(wt, src_view, parts=128):
        for kk in range(wt.nk):
            for c0 in range(0, wt.ncol, wt.step):
                c1 = min(wt.ncol, c0 + wt.step)
                st, sg = stage.get()
                k.dma("sp", st[0:parts, 0:c1 - c0], src_view[:, kk, c0:c1], (), (sg,))
                k.copy("pool", wt.ap[:, kk, c0:c1], st[0:parts, 0:c1 - c0], (sg,), (wt.g[kk][c0 // wt.step],))

    def wview(src_ap, p=128):
        return src_ap.rearrange("(k p) n -> p k n", p=p)

    def adaln(l):
        mark = ar.top
        wts = [WT(ar.alloc([KD, D], BF16), KD, D) for _ in range(2)]
        bT = ar.alloc([48], F32)
        bbc = ar.alloc([2, D], F32)
        gm = ar.alloc([2, KD], F32)
        g_b = Buf("adab")
        k.dma("sp", bT, I["b_adaT"][l], (), (g_b,))
        k.dma("sp", gm[:, 0, :], I["gmT"][l], (), (g_b,))
        k.dma("sp", gm[:, 1, :], I["gfT"][l], (), (g_b,))
        for jj, j in enumerate((2, 5)):
            k.dma("sp", bbc[:, jj, :], I["b_ada"][l, j * D:(j + 1) * D].partition_broadcast(128), (), (g_b,))
        wv = I["w_ada"][l].rearrange("(k p) n -> p k n", p=128)
        for j in range(6):
            wtt = wts[j % 2]
            wt = wtt.ap
            load_cast(wtt, wv[:, :, j * D:(j + 1) * D])
            if j in (0, 1, 3, 4):
                pt, pg = ps()
                pv = pt[:, 0:16].rearrange("p (a b) -> p a b", a=KD)
                for oc in range(KD):
                    for kk in range(KD):
                        k.mm(pv[:, oc, :], wt[:, kk, oc * 128:(oc + 1) * 128], scT[:, kk, :], kk == 0, kk == KD - 1,
                             wtt.gs(kk, oc * 128, (oc + 1) * 128) + (g_sc,), (pg,))
                for w in range(2):
                    k.tt("dve", modf[:, w, j, :], pv[:, :, w], bT[:, j * KD:(j + 1) * KD], ALU.add,
                         (pg, g_b), (g_mod,))
            else:
                jj = 0 if j == 2 else 1
                for w in range(2):
                    for hf in range(2):
                        pt, pg = ps()
                        for kk in range(KD):
                            k.mm(pt, screp[:, kk, w, :], wt[:, kk, hf * 512:(hf + 1) * 512], kk == 0, kk == KD - 1,
                                 wtt.gs(kk, hf * 512, (hf + 1) * 512) + (g_sc,), (pg,))
                        k.tt("dve", gbc[:, w, jj, hf * 512:(hf + 1) * 512], pt, bbc[:, jj, hf * 512:(hf + 1) * 512],
                             ALU.add, (pg, g_b), (g_gbc,))
        for w in range(2):
            k.stt("dve", AB[:, w, 0, :], modf[:, w, 1, :], 1.0, gm[:, 0, :], ALU.add, ALU.mult, (g_mod, g_b), (g_mod,))
            k.copy("dve", AB[:, w, 1, :], modf[:, w, 0, :], (g_mod,), (g_mod,))
            k.stt("dve", AB[:, w, 2, :], modf[:, w, 4, :], 1.0, gm[:, 1, :], ALU.add, ALU.mult, (g_mod, g_b), (g_mod,))
            k.copy("dve", AB[:, w, 3, :], modf[:, w, 3, :], (g_mod,), (g_mod,))
        k.barrier()
        ar.top = mark

    class NormRes:
        def __init__(self, nxs=4):
            self.junk = ar.alloc([D], BF16)
            self.g_junk = Buf("junk")
            self.ss = Ring(ar, 8, [4], F32, "ss")
            self.xs = Ring(ar, nxs, [D], F32, "xs")

    def norm_sub(nr, x_ap, x_g, which, ab0, hxT, hx_g, col0):
        ss, ssg = nr.ss.get()
        k.memset("pool", ss[:, 0:1], 0.0, (ssg,))
        k.act(nr.junk, x_ap, AF.Square, (x_g, ssg), (nr.g_junk, ssg), accum_out=ss[:, 0:1])
        k.act(ss[:, 1:2], ss[:, 0:1], AF.Sqrt, (ssg, g_cst), (ssg,), bias=cst[:, 0:1], scale=1.0 / D)
        k.op("dve", lambda eh: eh.reciprocal(out=ss[:, 2:3], in_=ss[:, 1:2]), (ssg,), (ssg,))
        xs, xsg = nr.xs.get()
        k.ts("dve", xs, x_ap, ss[:, 2:3], None, ALU.mult, None, (x_g, ssg), (xsg,))
        return xs, xsg, ss, ssg

    def norm_tile(nr, xsubs, which, ab0, hxT, hx_g):
        ns = len(xsubs)
        normed = []
        for j, (xa, xg) in enumerate(xsubs):
            xs, xsg, _, _ = norm_sub(nr, xa, xg, which, ab0, hxT, hx_g, j * 128)
            normed.append((xs, xsg))
        for kk in range(KD):
            pt, pg = ps()
            for j, (xs, xsg) in enumerate(normed):
                k.tr(pt[:, j * 128:(j + 1) * 128], xs[:, kk * 128:(kk + 1) * 128], ident, (xsg, g_ident), (pg,))
            k.act(hxT[:, kk, 0:ns * 128], pt[:, 0:ns * 128], AF.Identity, (pg, g_mod), (hx_g,),
                  bias=AB[:, which, ab0 + 1, kk:kk + 1], scale=AB[:, which, ab0, kk:kk + 1])

    def outproj_resid(l, yload, kparts, wtt, src, perm, dst, with_ctx):
        xr = Ring(ar, 4, [D], F32, "xo")
        tmpr = Ring(ar, 2, [D], F32, "tmpo")
        nk = len(kparts)
        for (t0, n) in tile_list(512, with_ctx):
            w = 1 if t0 >= T else 0
            ya, yg = yload(t0, n)
            for j in range(n // 128):
                xa, xg = xr.get()
                load_x("sp", xa, xg, src, perm, t0 + j * 128)
                tmp, tg = tmpr.get()
                for hf in range(2):
                    pt, pg = ps()
                    for kk in range(nk):
                        kp = kparts[kk]
                        k.mm(pt, ya[0:kp, kk, j * 128:(j + 1) * 128], wtt.ap[0:kp, kk, hf * 512:(hf + 1) * 512],
                             kk == 0, kk == nk - 1, (yg,) + wtt.gs(kk, hf * 512, (hf + 1) * 512), (pg,))
                    k.tt("dve", tmp[:, hf * 512:(hf + 1) * 512], pt, gbc[:, w, 0, hf * 512:(hf + 1) * 512], ALU.mult,
                         (pg, g_gbc), (tg,))
                k.tt("pool", xa, xa, tmp, ALU.add, (xg, tg), (xg,))
                store_x("sp", xa, xg, dst if t0 < T else "X", t0 + j * 128)

    def ffn(l, buf, with_ctx):
        mark = ar.top
        w1t = WT(ar.alloc([KD, 2 * DFF], BF16), KD, 2 * DFF)
        w2t = WT(ar.alloc([22, D], BF16), 22, D)
        w1, w2 = w1t.ap, w2t.ap
        load_cast(w1t, wview(I["w_ffn_in"][l]))
        load_cast(w2t, wview(I["w_ffn_out"][l]))
        nr = NormRes(2)
        xr = Ring(ar, 3, [D], F32, "xf")
        hxr = Ring(ar, 2, [KD, 256], BF16, "hxf")
        hr = Ring(ar, 1, [22, 256], BF16, "hf")
        sgr = Ring(ar, 2, [256], F32, "sg")
        tmpr = Ring(ar, 1, [512], F32, "tmpf")
        for (t0, n) in tile_list(256, with_ctx):
            w = 1 if t0 >= T else 0
            src = buf if t0 < T else "X"
            xs = []
            for j in range(n // 128):
                xa, xg = xr.get()
                load_x("sp", xa, xg, src, False, t0 + j * 128)
                xs.append((xa, xg))
            hxT, hxg = hxr.get()
            norm_tile(nr, xs, w, 2, hxT, hxg)
            hT, hg = hr.get()
            for c in range(22):
                p1, g1 = ps()
                p2, g2 = ps()
                for kk in range(KD):
                    k.mm(p1[:, 0:n], w1[:, kk, c * 128:(c + 1) * 128], hxT[:, kk, 0:n], kk == 0, kk == KD - 1,
                         (hxg,) + w1t.gs(kk, c * 128, (c + 1) * 128), (g1,))
                for kk in range(KD):
                    k.mm(p2[:, 0:n], w1[:, kk, DFF + c * 128:DFF + (c + 1) * 128], hxT[:, kk, 0:n], kk == 0,
                         kk == KD - 1, (hxg,) + w1t.gs(kk, DFF + c * 128, DFF + (c + 1) * 128), (g2,))
                sg, sgg = sgr.get()
                k.act(sg[:, 0:n], p1[:, 0:n], AF.Silu, (g1,), (sgg,))
                k.tt("dve", hT[:, c, 0:n], p2[:, 0:n], sg[:, 0:n], ALU.mult, (g2, sgg), (hg,))
            for j in range(n // 128):
                xa, xg = xs[j]
                for hf in range(2):
                    pt, pg = ps()
                    for c in range(22):
                        k.mm(pt, hT[:, c, j * 128:(j + 1) * 128], w2[:, c, hf * 512:(hf + 1) * 512], c == 0, c == 21,
                             (hg,) + w2t.gs(c, hf * 512, (hf + 1) * 512), (pg,))
                    tmp, tg = tmpr.get()
                    k.tt("dve", tmp, pt, gbc[:, w, 1, hf * 512:(hf + 1) * 512], ALU.mult, (pg, g_gbc), (tg,))
                    k.tt("pool", xa[:, hf * 512:(hf + 1) * 512], xa[:, hf * 512:(hf + 1) * 512], tmp, ALU.add,
                         (xg, tg), (xg,))
                store_x("sp", xa, xg, src, t0 + j * 128)
        k.barrier()
        ar.top = mark

    def fwd_ring(L, nbuf):
        return Ring(ar, nbuf, [L // 128, 128], BF16, "tbl%d" % L)

    def inv_ring(L):
        return Ring(ar, 6, [2, 512 if L == T else 256], BF16, "gr%d" % L)

    def dft_forward(L, Pin, Qin, g_in, epilogue, tb):
        ntc = L // 128
        nfc = NF if L == T else NFC
        tc_name, ts_name = ("TBLc", "TBLs") if L == T else ("TBLc_c", "TBLs_c")
        for fc in range(nfc):
            ct, cg = tb.get()
            k.dma("sp", ct, I[tc_name][fc][:, 0:ntc, :], (), (cg,))
            st_, sg_ = tb.get()
            k.dma("act" if L == T else "sp", st_, I[ts_name][fc][:, 0:ntc, :], (), (sg_,))
            pc, gc = ps()
            for tc in range(ntc):
                k.mm(pc, ct[:, tc, :], Pin[:, tc, :], tc == 0, tc == ntc - 1, (cg, g_in), (gc,))
            pS, gS = ps()
            for tc in range(ntc):
                k.mm(pS, st_[:, tc, :], Qin[:, tc, :], tc == 0, tc == ntc - 1, (sg_, g_in), (gS,))
            epilogue(fc, pc, gc, pS, gS)

    def dft_inverse(L, Ysp, g_y, epilogue, tb):
        nfc = NF if L == T else NFC
        tw = 512 if L == T else 256
        gc_name, gs_name = ("GRc", "GRs") if L == T else ("GRc_c", "GRs_c")
        for tg in range(L // tw):
            pss = [(ps_t[j][:, :], ps_b[j]) for j in range(4)]
            ps_lim[0] = 4
            for fc in range(nfc):
                tt_, tg_ = tb.get()
                k.dma("sp", tt_[:, 0, :], I[gc_name][fc * 128:(fc + 1) * 128, tg * tw:(tg + 1) * tw], (), (tg_,))
                k.dma("sp", tt_[:, 1, :], I[gs_name][fc * 128:(fc + 1) * 128, tg * tw:(tg + 1) * tw], (), (tg_,))
                for cc in range(4):
                    pt, pg = pss[cc]
                    k.mm(pt[:, 0:tw], Ysp[:, fc, 0, cc * 128:(cc + 1) * 128], tt_[:, 0, :], fc == 0, False,
                         (g_y, tg_), (pg,))
                    k.mm(pt[:, 0:tw], Ysp[:, fc, 1, cc * 128:(cc + 1) * 128], tt_[:, 1, :], False, fc == nfc - 1,
                         (g_y, tg_), (pg,))
            for cc in range(4):
                epilogue(cc, tg, pss[cc][0], pss[cc][1])
        ps_lim[0] = 0

    def kf_gen(i, L, li):
        mark = ar.top
        ntc = L // 128
        nfc = NF if L == T else NFC
        sfx = "" if L == T else "_c"
        emb = ar.alloc([L], F32, parts=33)
        f1w = ar.alloc([64], F32, parts=33)
        f2w = ar.alloc([64], F32, parts=64)
        f3w = ar.alloc([2048], F32, parts=65)
        vec = ar.alloc([4], F32, parts=64)
        sfb = ar.alloc([2], F32, parts=64)
        h1 = ar.alloc([L], F32, parts=64)
        h2 = ar.alloc([L], F32, parts=65)
        tcol = ar.alloc([ntc], F32)
        dlt = ar.alloc([512], F32)
        wN = ar.alloc([nfc], F32)
        skb = ar.alloc([2, 512], F32)
        g_c = Buf("kfc")
        g_h1 = Buf("h1")
        g_h2 = Buf("h2")
        k.dma("sp", emb, I["embT" + sfx], (), (g_c,))
        k.dma("sp", f1w, I["f1w"][i], (), (g_c,))
        k.dma("sp", f2w, I["f2w"][i], (), (g_c,))
        k.dma("sp", f3w, I["f3wa"][i], (), (g_c,))
        k.dma("sp", vec[:, 0:1], I["f1bT"][i], (), (g_c,))
        k.dma("sp", vec[:, 1:2], I["f2bT"][i], (), (g_c,))
        k.dma("sp", vec[:, 2:3], I["sfT"][i], (), (g_c,))
        k.dma("sp", tcol, I["tcol" + sfx], (), (g_c,))
        k.dma("sp", dlt, I["deltas"].partition_broadcast(128), (), (g_c,))
        k.dma("sp", wN, I["wN" + sfx], (), (g_c,))
        for o in range(2):
            k.dma("sp", skb[:, o, :], I["hy_skip"][i, o].partition_broadcast(128), (), (g_c,))
        k.tt("dve", sfb[:, 0:1], vec[:, 0:1], vec[:, 2:3], ALU.mult, (g_c,), (g_c,))
        k.tt("dve", sfb[:, 1:2], vec[:, 1:2], vec[:, 2:3], ALU.mult, (g_c,), (g_c,))
        k.memset("dve", h2[64:65, :], 1.0, (g_h2,))
        argr = Ring(ar, 2, [512], F32, "arg", parts=64)
        wrr = Ring(ar, 2, [512], F32, "wrp", parts=64)
        cw = min(512, L)

        def sin_layer(src_lhsT, kp, rhs_full, g_rhs, bcol, dst, g_dst):
            for c0 in range(0, L, cw):
                pt, pg = ps()
                k.mm(pt[0:64, 0:cw], src_lhsT, rhs_full[0:kp, c0:c0 + cw], True, True, (g_c, g_rhs), (pg,))
                a, ag = argr.get()
                k.ts("dve", a[:, 0:cw], pt[0:64, 0:cw], vec[:, 2:3], sfb[:, bcol:bcol + 1], ALU.mult, ALU.add,
                     (pg, g_c), (ag,))
                w_, wg_ = wrr.get()
                k.ts("dve", w_[:, 0:cw], a[:, 0:cw], math.pi, -2.0 * math.pi, ALU.is_gt, ALU.mult, (ag,), (wg_,))
                k.tt("dve", a[:, 0:cw], a[:, 0:cw], w_[:, 0:cw], ALU.add, (ag, wg_), (ag,))
                k.ts("dve", w_[:, 0:cw], a[:, 0:cw], -math.pi, 2.0 * math.pi, ALU.is_lt, ALU.mult, (ag,), (wg_,))
                k.tt("dve", a[:, 0:cw], a[:, 0:cw], w_[:, 0:cw], ALU.add, (ag, wg_), (ag,))
                k.act(dst[0:64, c0:c0 + cw], a[:, 0:cw], AF.Sin, (ag,), (g_dst,))
        sin_layer(f1w[0:33, :], 33, emb, g_c, 0, h1, g_h1)
        sin_layer(f2w[0:64, :], 64, h1, g_h1, 1, h2, g_h2)

        PQ = ar.alloc([2, ntc, 512], BF16)
        g_pq = Buf("pq")
        acc = ar.alloc([2, 512], F32)
        g_acc = Buf("acc")
        rs = ar.alloc([512], F32)
        g_rs = Buf("rs")
        dec_r = Ring(ar, 1, [512], F32, "dec")
        hh_r = Ring(ar, 1, [2, 512], F32, "hh")
        sq_r = Ring(ar, 1, [2, 512], F32, "sq")
        kfo_r = Ring(ar, 1, [2, 512], F32, "kfo")
        tbf = fwd_ring(L, 2)
        for o in range(2):
            k.memset("pool", acc, 0.0, (g_acc,))
            for tc in range(ntc):
                dec, dg = dec_r.get()
                k.ts("pool", dec, dlt, tcol[:, tc:tc + 1], None, ALU.mult, None, (g_c,), (dg,))
                k.act(dec, dec, AF.Exp, (dg,), (dg,), scale=-1.0)
                hh, hg_ = hh_r.get()
                for dr in range(2):
                    pt, pg = ps()
                    k.mm(pt, h2[0:65, tc * 128:(tc + 1) * 128], f3w[0:65, o * 1024 + dr * 512:o * 1024 + (dr + 1) * 512],
                         True, True, (g_h2, g_c), (pg,))
                    k.tt("dve", hh[:, dr, :], pt, dec, ALU.mult, (pg, dg), (hg_,))
                if tc == 0:
                    k.memset("dve", hh[0:1, 1, :], 0.0, (hg_,))
                sq, sqg = sq_r.get()
                k.tt("pool", sq, hh, hh, ALU.mult, (hg_,), (sqg,))
                k.tt("pool", acc, acc, sq, ALU.add, (sqg, g_acc), (g_acc,))
                k.tt("dve", PQ[:, 0, tc, :], hh[:, 0, :], hh[:, 1, :], ALU.add, (hg_,), (g_pq,))
                k.tt("dve", PQ[:, 1, tc, :], hh[:, 1, :], hh[:, 0, :], ALU.subtract, (hg_,), (g_pq,))
            k.tt("pool", acc[:, 0, :], acc[:, 0, :], acc[:, 1, :], ALU.add, (g_acc,), (g_acc,))
            pt, pg = ps()
            k.mm(pt, ones_f, acc[:, 0, :], True, True, (g_ones, g_acc), (pg,))
            k.act(rs, pt, AF.Sqrt, (pg, g_cst), (g_rs,), bias=cst[:, 0:1], scale=1.0)
            k.op("dve", lambda eh: eh.reciprocal(out=rs, in_=rs), (g_rs,), (g_rs,))

            def epi(fc, pc, gc, pS, gS, o=o):
                kt, kg = kfo_r.get()
                k.tt("dve", kt[:, 0, :], pc, rs, ALU.mult, (gc, g_rs), (kg,))
                k.tt("dve", kt[:, 0, :], kt[:, 0, :], skb[:, o, :], ALU.add, (kg, g_c), (kg,))
                k.tt("dve", kt[:, 1, :], pS, rs, ALU.mult, (gS, g_rs), (kg,))
                k.ts("pool", kt, kt, wN[:, fc:fc + 1], None, ALU.mult, None, (kg, g_c), (kg,))
                k.dma("sp", KF[li, o, fc], kt, (kg,), (gKF,))
            dft_forward(L, PQ[:, 0], PQ[:, 1], g_pq, epi, tbf)
        k.barrier()
        ar.top = mark

    def even_mixer(l):
        i = l // 2
        run_ctx = l < 3
        def e1():
            mark = ar.top
            wint = WT(ar.alloc([KD, 2560], BF16), KD, 2560)
            win = wint.ap
            load_cast(wint, wview(I["w_in_even"][i]))
            swT = ar.alloc([4, 128], BF16)
            g_sw = Buf("sw")
            st_, sg_ = stage.get()
            k.dma("sp", st_.rearrange("p (g q) -> p g q", g=4), I["sgu_wT"][i].rearrange("g q p -> q g p"), (), (sg_,))
            k.copy("pool", swT, st_.rearrange("p (g q) -> p g q", g=4), (sg_,), (g_sw,))
            lng = ar.alloc([512], F32)
            sbb = ar.alloc([4, 128], F32)
            k.dma("sp", lng, I["sgu_lng"][i].partition_broadcast(128), (), (g_sw,))
            k.dma("sp", sbb, I["sgu_b"][i].rearrange("g p -> (g p)").partition_broadcast(128), (), (g_sw,))
            nr = NormRes()
            xr = Ring(ar, 5, [D], F32, "xe")
            hxr = Ring(ar, 2, [KD, 512], BF16, "hxe")
            zar = Ring(ar, 2, [12, 512], F32, "zae")
            ur = Ring(ar, 2, [4, 512], BF16, "ue")
            vbr = Ring(ar, 2, [512], F32, "vbe")
            vnr = Ring(ar, 2, [512], BF16, "vne")
            str_ = Ring(ar, 4, [4], F32, "ste")
            ybr = Ring(ar, 2, [4, 512], BF16, "ybe")
            tsr = Ring(ar, 2, [512], F32, "tse")
            for (t0, n) in tile_list(512, run_ctx):
                w = 1 if t0 >= T else 0
                xs = []
                for j in range(n // 128):
                    xa, xg = xr.get()
                    load_x("sp", xa, xg, "X", False, t0 + j * 128)
                    xs.append((xa, xg))
                hxT, hxg = hxr.get()
                norm_tile(nr, xs, w, 0, hxT, hxg)
                za, zg = zar.get()
                for c in range(12):
                    pt, pg = ps()
                    for kk in range(KD):
                        k.mm(pt[:, 0:n], win[:, kk, c * 128:(c + 1) * 128], hxT[:, kk, 0:n], kk == 0, kk == KD - 1,
                             (hxg,) + wint.gs(kk, c * 128, (c + 1) * 128), (pg,))
                    k.act(za[:, c, 0:n], pt[:, 0:n], AF.Copy, (pg,), (zg,))
                k.dma("act", ZA[:, :, t0:t0 + n], za[:, :, 0:n], (zg,), (gZA,))
                u, ug = ur.get()
                for c in range(4):
                    pt, pg = ps()
                    for kk in range(KD):
                        k.mm(pt[:, 0:n], win[:, kk, 1536 + c * 128:1536 + (c + 1) * 128], hxT[:, kk, 0:n], kk == 0,
                             kk == KD - 1, (hxg,) + wint.gs(kk, 1536 + c * 128, 1536 + (c + 1) * 128), (pg,))
                    k.act(u[:, c, 0:n], pt[:, 0:n], AF.Gelu_apprx_tanh, (pg,), (ug,))
                yb, ybg = ybr.get()
                for j in range(n // 128):
                    pt, pg = ps()
                    for kk in range(KD):
                        k.mm(pt, hxT[:, kk, j * 128:(j + 1) * 128], win[:, kk, 2048:2560], kk == 0, kk == KD - 1,
                             (hxg,) + wint.gs(kk, 2048, 2560), (pg,))
                    vb, vbg = vbr.get()
                    k.act(vb, pt, AF.Gelu_apprx_tanh, (pg,), (vbg,))
                    st_, stg = str_.get()
                    k.op("dve", lambda eh, st_=st_, vb=vb: eh.reduce_sum(out=st_[:, 0:1], in_=vb, axis=AX.X), (vbg,), (stg,))
                    k.ts("dve", st_[:, 1:2], st_[:, 0:1], 1.0 / 512, None, ALU.mult, None, (stg,), (stg,))
                    k.ts("dve", vb, vb, st_[:, 1:2], None, ALU.subtract, None, (vbg, stg), (vbg,))
                    k.memset("pool", st_[:, 2:3], 0.0, (stg,))
                    k.act(nr.junk[:, 0:512], vb, AF.Square, (vbg, stg), (nr.g_junk, stg), accum_out=st_[:, 2:3])
                    k.act(st_[:, 3:4], st_[:, 2:3], AF.Sqrt, (stg, g_cst), (stg,), bias=cst[:, 0:1], scale=1.0 / 512)
                    k.op("dve", lambda eh, st_=st_: eh.reciprocal(out=st_[:, 3:4], in_=st_[:, 3:4]), (stg,), (stg,))
                    vn, vng = vnr.get()
                    k.stt("dve", vn, vb, st_[:, 3:4], lng, ALU.mult, ALU.mult, (vbg, stg, g_sw), (vng,))
                    p2, g2 = ps()
                    for g in range(4):
                        k.mm(p2[:, g * 128:(g + 1) * 128], vn[:, g * 128:(g + 1) * 128], swT[:, g, :], True, True,
                             (vng, g_sw), (g2,))
                    tsb, tsg = tsr.get()
                    k.tt("dve", tsb, p2, sbb.rearrange("p g q -> p (g q)"), ALU.add, (g2, g_sw), (tsg,))
                    k.tt("pool", yb[:, :, j * 128:(j + 1) * 128], tsb.rearrange("p (g q) -> p g q", g=4),
                         u[:, :, j * 128:(j + 1) * 128], ALU.mult, (tsg, ug), (ybg,))
                k.dma("sp", YBd[:, :, t0:t0 + n], yb[:, :, 0:n], (ybg,), (gYB,))
            k.barrier()
            ar.top = mark
        if go('E1'):
            e1()
        if go('kf_gen T'):
            kf_gen(i, T, 0)
        if run_ctx and go('kf_gen TC'):
            kf_gen(i, TC, 1)
        for (L, pos0, li) in ((T, 0, 0), (TC, T, 1)):
            if li == 1 and not run_ctx:
                continue
            if go('hyena_seq %d' % L):
                hyena_seq(i, L, pos0, li)
        if not go('E4'):
            return
        mark = ar.top
        import os
        if os.environ.get('PADWO'):
            ar.alloc([int(os.environ['PADWO']) * 256], F32)
        wot = WT(ar.alloc([KD, D], BF16), KD, D)
        load_cast(wot, wview(I["w_out_even"][i]))
        yr_ = Ring(ar, 2, [KD, 512], BF16, "ycat")

        def yload(t0, n):
            ya, yg = yr_.get()
            k.dma("sp", ya[:, 0:4, 0:n], YAd[:, :, t0:t0 + n], (gYA,), (yg,))
            k.dma("sp", ya[:, 4:8, 0:n], YBd[:, :, t0:t0 + n], (gYB,), (yg,))
            return ya, yg
        outproj_resid(l, yload, [128] * 8, wot, "X", False, "X", run_ctx)
        k.barrier()
        ar.top = mark

    def hyena_seq(i, L, pos0, li):
        ntc = L // 128
        nfc = NF if L == T else NFC
        tw = 512 if L == T else 256
        mark = ar.top
        vT = ar.alloc([ntc, 512], BF16)
        g_vT = Buf("vT")
        cw_ = ar.alloc([12, 3], F32)
        cb_ = ar.alloc([12], F32)
        g_cw = Buf("cw")
        k.dma("sp", cw_, I["hy_cwT"][i], (), (g_cw,))
        k.dma("sp", cb_, I["hy_cbT"][i], (), (g_cw,))
        mark2 = ar.top
        zin_r = Ring(ar, 2, [L + 2], F32, "zin")
        zo_r = Ring(ar, 2, [L], F32, "zo")
        for j in range(2):
            k.memset("pool", zin_r.aps[j][:, 0:1], 0.0, (zin_r.bufs[j],))
            k.memset("pool", zin_r.aps[j][:, L + 1:L + 2], 0.0, (zin_r.bufs[j],))
        for c in range(12):
            zi, zig = zin_r.get()
            k.dma("sp", zi[:, 1:L + 1], ZA[:, c, pos0:pos0 + L], (gZA,), (zig,))
            zo, zog = zo_r.get()
            k.act(zo, zi[:, 1:L + 1], AF.Identity, (zig, g_cw), (zog,), bias=cb_[:, c:c + 1], scale=cw_[:, c, 1:2])
            k.stt("dve", zo, zi[:, 0:L], cw_[:, c, 0:1], zo, ALU.mult, ALU.add, (zig, g_cw, zog), (zog,))
            k.stt("dve", zo, zi[:, 2:L + 2], cw_[:, c, 2:3], zo, ALU.mult, ALU.add, (zig, g_cw, zog), (zog,))
            if c < 4:
                for tc in range(ntc):
                    pt, pg = ps()
                    k.tr(pt[:, 0:128], zo[:, tc * 128:(tc + 1) * 128], ident, (zog, g_ident), (pg,))
                    k.act(vT[:, tc, c * 128:(c + 1) * 128], pt[:, 0:128], AF.Copy, (pg,), (g_vT,))
            elif c < 8:
                k.dma("sp", G1[:, c - 4, pos0:pos0 + L], zo, (zog,), (gG1,))
            else:
                k.dma("sp", G2[:, c - 8, pos0:pos0 + L], zo, (zog,), (gG2,))
        k.barrier()
        ar.top = mark2
        Ysp = ar.alloc([nfc, 2, 512], BF16)
        g_Y = Buf("Ysp")
        kf_r = Ring(ar, 2, [2, 512], F32, "kft")
        t_r = Ring(ar, 2, [4, 512], F32, "ytmp")
        g_r = Ring(ar, 3, [tw], F32, "gt")
        y_r = Ring(ar, 3, [tw], F32, "y1")
        yo_r = Ring(ar, 3, [tw], BF16, "yo")
        tbf = fwd_ring(L, 3)
        tbi = inv_ring(L)
        for o in range(2):
            def epi_f(fc, pc, gc, pS, gS, o=o):
                kt, kg = kf_r.get()
                k.dma("sp", kt, KF[li, o, fc], (gKF,), (kg,))
                tm, tmg = t_r.get()
                k.tt("dve", tm[:, 0, :], pc, kt[:, 0, :], ALU.mult, (gc, kg), (tmg,))
                k.tt("dve", tm[:, 1, :], pS, kt[:, 1, :], ALU.mult, (gS, kg), (tmg,))
                k.tt("dve", tm[:, 2, :], pS, kt[:, 0, :], ALU.mult, (gS, kg), (tmg,))
                k.tt("dve", tm[:, 3, :], pc, kt[:, 1, :], ALU.mult, (gc, kg), (tmg,))
                k.tt("pool", Ysp[:, fc, 0, :], tm[:, 0, :], tm[:, 1, :], ALU.add, (tmg,), (g_Y,))
                k.tt("pool", Ysp[:, fc, 1, :], tm[:, 2, :], tm[:, 3, :], ALU.subtract, (tmg,), (g_Y,))
            dft_forward(L, vT, vT, g_vT, epi_f, tbf)

            def epi_i(cc, tg, pt, pg, o=o):
                gt, gg = g_r.get()
                gsrc = G1 if o == 0 else G2
                k.dma("sp", gt, gsrc[:, cc, pos0 + tg * tw:pos0 + (tg + 1) * tw], (gG1 if o == 0 else gG2,), (gg,))
                if o == 0:
                    y1, y1g = y_r.get()
                    k.tt("dve", y1, pt[:, 0:tw], gt, ALU.mult, (pg, gg), (y1g,))
                    for s in range(tw // 128):
                        p2, g2 = ps()
                        k.tr(p2[:, 0:128], y1[:, s * 128:(s + 1) * 128], ident, (y1g, g_ident), (g2,))
                        k.act(vT[:, tg * (tw // 128) + s, cc * 128:(cc + 1) * 128], p2[:, 0:128], AF.Copy, (g2,), (g_vT,))
                else:
                    yo, yog = yo_r.get()
                    k.tt("dve", yo, pt[:, 0:tw], gt, ALU.mult, (pg, gg), (yog,))
                    k.dma("sp", YAd[:, cc, pos0 + tg * tw:pos0 + (tg + 1) * tw], yo, (yog,), (gYA,))
            dft_inverse(L, Ysp, g_Y, epi_i, tbi)
        k.barrier()
        ar.top = mark

    def odd_mixer(l):
        i = l // 2
        run_ctx = l < 3
        perm = (i % 2 == 1)
        dst = "XP" if perm else "X"
        mark = ar.top
        wint = WT(ar.alloc([KD, 2 * DR], BF16), KD, 2 * DR)
        win = wint.ap
        load_cast(wint, wview(I["w_in_odd"][i]))
        nr = NormRes()
        xr = Ring(ar, 5, [D], F32, "xo1")
        hxr = Ring(ar, 2, [KD, 512], BF16, "hxo")
        gtr = Ring(ar, 1, [NH, 512], BF16, "gto", parts=HD)
        zrr = Ring(ar, 1, [NH, 512], F32, "zro", parts=HD)
        for (t0, n) in tile_list(512, True):
            w = 1 if t0 >= T else 0
            xs = []
            for j in range(n // 128):
                xa, xg = xr.get()
                load_x("sp", xa, xg, "X", perm, t0 + j * 128)
                xs.append((xa, xg))
            hxT, hxg = hxr.get()
            norm_tile(nr, xs, w, 0, hxT, hxg)
            gt, gtg = gtr.get()
            zr, zrg = zrr.get()
            for h in range(NH):
                for part in range(2):
                    c0 = part * DR + h * HD
                    pt, pg = ps()
                    for kk in range(KD):
                        k.mm(pt[0:HD, 0:n], win[:, kk, c0:c0 + HD], hxT[:, kk, 0:n], kk == 0, kk == KD - 1,
                             (hxg,) + wint.gs(kk, c0, c0 + HD), (pg,))
                    if part == 0:
                        k.act(gt[:, h, 0:n], pt[0:HD, 0:n], AF.Gelu_apprx_tanh, (pg,), (gtg,))
                    else:
                        k.copy("dve", zr[:, h, 0:n], pt[0:HD, 0:n], (pg,), (zrg,))
            k.dma("act", GTd[:, :, t0:t0 + n], gt[:, :, 0:n], (gtg,), (gGT,))
            k.dma("sp", ZRd[:, :, t0:t0 + n], zr[:, :, 0:n], (zrg,), (gZR,))
        k.barrier()
        ar.top = mark
        wa = ar.alloc([2, 2, NH, HD], BF16, parts=HD)
        g_wa = Buf("wa")
        for d in range(2):
            for ax, nm in enumerate(("rg_wa", "rg_wx")):
                sv = I[nm][i, d].rearrange("h d e -> d h e")
                for h0 in range(0, NH, 4):
                    st_, sg_ = stage.get()
                    sview = st_[0:HD, 0:4 * HD].rearrange("p (h e) -> p h e", h=4)
                    k.dma("sp", sview, sv[:, h0:h0 + 4, :], (), (sg_,))
                    k.copy("pool", wa[:, d, ax, h0:h0 + 4, :], sview, (sg_,), (g_wa,))
        vecs = ar.alloc([2, 3, NH], F32, parts=HD)
        csp = ar.alloc([2, NH], F32, parts=HD)
        cwv = ar.alloc([NH, 4], F32, parts=HD)
        cbv = ar.alloc([NH], F32, parts=HD)
        g_v = Buf("ovec")
        for d in range(2):
            k.dma("sp", vecs[:, d, 0, :], I["rg_baT"][i, d], (), (g_v,))
            k.dma("sp", vecs[:, d, 1, :], I["rg_bxT"][i, d], (), (g_v,))
            k.dma("sp", vecs[:, d, 2, :], I["rg_lamT"][i, d], (), (g_v,))
        k.dma("sp", cwv, I["rg_cwT"][i], (), (g_v,))
        k.dma("sp", cbv, I["rg_cbT"][i], (), (g_v,))
        for d in range(2):
            k.act(csp[:, d, :], vecs[:, d, 2, :], AF.Exp, (g_v,), (g_v,), scale=-1.0)
            k.act(csp[:, d, :], csp[:, d, :], AF.Ln, (g_v, g_cst), (g_v,), bias=cst[0:HD, 1:2], scale=1.0)
            k.ts("dve", csp[:, d, :], csp[:, d, :], -8.0, None, ALU.mult, None, (g_v,), (g_v,))
        LL = (TC, T)
        zl_r = Ring(ar, 2, [T + 3], F32, "zl", parts=HD)
        zc_r = Ring(ar, 2, [TC + 3], F32, "zc", parts=HD)
        for rr, Lx in ((zl_r, T), (zc_r, TC)):
            for j in range(2):
                k.memset("pool", rr.aps[j][:, 0:2], 0.0, (rr.bufs[j],))
                k.memset("pool", rr.aps[j][:, Lx + 2:Lx + 3], 0.0, (rr.bufs[j],))
        xl_r = Ring(ar, 2, [TA], F32, "xl", parts=HD)
        xb_r = Ring(ar, 2, [TA], BF16, "xb", parts=HD)
        hs_r = Ring(ar, 2, [TA], F32, "hs", parts=HD)
        hb_r = Ring(ar, 2, [512], F32, "hb", parts=HD)
        gl_r = Ring(ar, 1, [TA], BF16, "gl", parts=HD)
        yo_r = Ring(ar, 1, [TA], BF16, "yro", parts=HD)
        e_r = [Ring(ar, 2, [512], F32, "e%d" % j, parts=HD) for j in range(5)]
        st_r = Ring(ar, 4, [2], F32, "sto", parts=HD)
        for h in range(NH):
            zl, zlg = zl_r.get()
            zc, zcg = zc_r.get()
            k.dma("sp", zl[:, 2:T + 2], ZRd[:, h, 0:T], (gZR,), (zlg,))
            k.dma("sp", zc[:, 2:TC + 2], ZRd[:, h, T:TA], (gZR,), (zcg,))
            xl, xlg = xl_r.get()
            for (zz, zzg, Lx, o0) in ((zl, zlg, T, 0), (zc, zcg, TC, T)):
                k.act(xl[:, o0:o0 + Lx], zz[:, 2:Lx + 2], AF.Identity, (zzg, g_v), (xlg,), bias=cbv[:, h:h + 1],
                      scale=cwv[:, h, 2:3])
                for (kk, off) in ((0, 0), (1, 1), (3, 3)):
                    k.stt("dve", xl[:, o0:o0 + Lx], zz[:, off:off + Lx], cwv[:, h, kk:kk + 1], xl[:, o0:o0 + Lx],
                          ALU.mult, ALU.add, (zzg, g_v, xlg), (xlg,))
            xb, xbg = xb_r.get()
            k.copy("pool", xb, xl, (xlg,), (xbg,))
            hs, hsg = hs_r.get()
            for d in range(2):
                segs = []
                for (o0, Lx) in ((T, TC), (0, T)):
                    cs = list(range(o0, o0 + Lx, 512))
                    if d == 1:
                        cs = cs[::-1]
                    segs.append([(c0, min(512, Lx)) for c0 in cs])
                prev = None
                for si, seg in enumerate(segs):
                    for (c0, n) in seg:
                        pa, pag = ps()
                        k.mm(pa[0:HD, 0:n], wa[:, d, 0, h, :], xb[:, c0:c0 + n], True, True, (g_wa, xbg), (pag,))
                        px, pxg = ps()
                        k.mm(px[0:HD, 0:n], wa[:, d, 1, h, :], xb[:, c0:c0 + n], True, True, (g_wa, xbg), (pxg,))
                        r_, rg = e_r[0].get()
                        k.act(r_[:, 0:n], pa[0:HD, 0:n], AF.Sigmoid, (pag, g_v), (rg,), bias=vecs[:, d, 0, h:h + 1])
                        gi, gig = e_r[1].get()
                        k.act(gi[:, 0:n], px[0:HD, 0:n], AF.Sigmoid, (pxg, g_v), (gig,), bias=vecs[:, d, 1, h:h + 1])
                        a_, ag = e_r[2].get()
                        k.act(a_[:, 0:n], r_[:, 0:n], AF.Exp, (rg, g_v), (ag,), scale=csp[:, d, h:h + 1])
                        s_, sg_ = e_r[3].get()
                        k.tt("pool", s_[:, 0:n], a_[:, 0:n], a_[:, 0:n], ALU.mult, (ag,), (sg_,))
                        k.act(s_[:, 0:n], s_[:, 0:n], AF.Sqrt, (sg_, g_cst), (sg_,), bias=cst[0:HD, 1:2], scale=-1.0)
                        bx, bxg = e_r[4].get()
                        k.tt("pool", bx[:, 0:n], gi[:, 0:n], xl[:, c0:c0 + n], ALU.mult, (gig, xlg), (bxg,))
                        k.tt("dve", bx[:, 0:n], bx[:, 0:n], s_[:, 0:n], ALU.mult, (bxg, sg_), (bxg,))
                        if d == 0:
                            dst_ap, dst_g = hs[:, c0:c0 + n], hsg
                        else:
                            hbt, hbg = hb_r.get()
                            dst_ap, dst_g = hbt[:, 0:n], hbg
                        init = 0.0 if prev is None else prev[0]
                        rd = (ag, bxg) + (() if prev is None else (prev[1],))
                        if d == 0:
                            k.op("dve", lambda eh, o=dst_ap, a=a_[:, 0:n], b=bx[:, 0:n], ini=init:
                                 eh.tensor_tensor_scan(out=o, data0=a, data1=b, initial=ini, op0=ALU.mult, op1=ALU.add),
                                 rd, (dst_g,))
                            prev = (hs[:, c0 + n - 1:c0 + n], hsg)
                        else:
                            k.op("dve", lambda eh, o=dst_ap, a=a_[:, 0:n], b=bx[:, 0:n], ini=init:
                                 eh.tensor_tensor_scan(out=o[:, ::-1], data0=a[:, ::-1], data1=b[:, ::-1], initial=ini,
                                                       op0=ALU.mult, op1=ALU.add), rd, (dst_g,))
                            st_, stg = st_r.get()
                            k.copy("dve", st_[:, 0:1], dst_ap[:, 0:1], (dst_g,), (stg,))
                            prev = (st_[:, 0:1], stg)
                            k.tt("pool", hs[:, c0:c0 + n], hs[:, c0:c0 + n], dst_ap, ALU.add, (hsg, dst_g), (hsg,))
            gl, glg = gl_r.get()
            k.dma("sp", gl, GTd[:, h, :], (gGT,), (glg,))
            yo, yog = yo_r.get()
            k.tt("pool", yo, hs, gl, ALU.mult, (hsg, glg), (yog,))
            k.dma("sp", YRd[:, h, :], yo, (yog,), (gYR,))
        k.barrier()
        ar.top = mark
        wot = WT(ar.alloc([NH, D], BF16, parts=HD), NH, D)
        load_cast(wot, I["w_out_odd"][i].rearrange("(h e) n -> e h n", e=HD), parts=HD)
        yr_ = Ring(ar, 2, [NH, 512], BF16, "yro3", parts=HD)

        def yload(t0, n):
            ya, yg = yr_.get()
            k.dma("sp", ya[:, :, 0:n], YRd[:, :, t0:t0 + n], (gYR,), (yg,))
            return ya, yg
        outproj_resid(l, yload, [HD] * NH, wot, "X", perm, dst, run_ctx)
        k.barrier()
        ar.top = mark

    def final_norm(src, perm):
        mark = ar.top
        gfb = ar.alloc([D], F32)
        g_gf = Buf("gf")
        k.dma("sp", gfb, I["gfin"].partition_broadcast(128), (), (g_gf,))
        nr = NormRes()
        xr = Ring(ar, 4, [D], F32, "xfn")
        orr = Ring(ar, 3, [D], F32, "ofn")
        for pos0 in range(0, T, 128):
            xa, xg = xr.get()
            load_x("sp", xa, xg, src, False, pos0)
            ss, ssg = nr.ss.get()
            k.memset("pool", ss[:, 0:1], 0.0, (ssg,))
            k.act(nr.junk, xa, AF.Square, (xg, ssg), (nr.g_junk, ssg), accum_out=ss[:, 0:1])
            k.act(ss[:, 1:2], ss[:, 0:1], AF.Sqrt, (ssg, g_cst), (ssg,), bias=cst[:, 0:1], scale=1.0 / D)
            k.op("dve", lambda eh, ss=ss: eh.reciprocal(out=ss[:, 2:3], in_=ss[:, 1:2]), (ssg,), (ssg,))
            oa, og = orr.get()
            k.stt("dve", oa, xa, ss[:, 2:3], gfb, ALU.mult, ALU.mult, (xg, ssg, g_gf), (og,))
            for (p0, p1, rows) in xrows(out_d, perm, pos0):
                k.dma("sp", rows, oa[p0:p1, :], (og,), ())
        k.barrier()
        ar.top = mark

    for r0 in range(0, T, 512):
        k.dma("sp", X[r0:r0 + 512, :], I["x"][r0:r0 + 512, :], (), tuple(gX[r0 // 128:r0 // 128 + 4]))
    k.dma("sp", X[T:TA, :], I["ctx"], (), tuple(gX[T // 128:]))
    k.barrier()
    cur = "X"
    for l in range(nl):
        if go('adaln'):
            adaln(l)
        if l % 2 == 0:
            even_mixer(l)
        else:
            odd_mixer(l)
            if (l // 2) % 2 == 1:
                cur = "XP"
        if go('ffn'):
            ffn(l, cur, l < 3)
    if go('final'):
        final_norm(cur, cur == "XP")
    if dbg:
        k.dma("sp", dbg_d[0:T, :], (XP if cur == "XP" else X)[0:T, :], (), ())
        k.dma("sp", dbg_d[T:TA, :], X[T:TA, :], (), ())
    k.finish()
    return nc, k


def _tbl(L):
    N = 2 * L
    nf = (L + 1 + 127) // 128 * 128
    idx = np.arange(nf, dtype=np.int64)
    prod = (idx[:, None] * idx[None, :]) % N
    ang = prod.astype(np.float64) * (2.0 * np.pi / N)
    valid = (idx <= L)
    m = (valid[:, None] & valid[None, :])
    Gc = np.where(m, np.cos(ang), 0.0)
    Gs = np.where(m, np.sin(ang), 0.0)
    nb = nf // 128

    def colblk(G):
        return np.ascontiguousarray(G.reshape(nb, 128, nb, 128).transpose(2, 1, 0, 3)).astype(ml_dtypes.bfloat16)
    wN = np.where(valid, 2.0, 0.0)
    wN[0] = 1.0
    wN[L] = 1.0
    wN = (wN / N).astype(np.float32)
    wNT = np.ascontiguousarray(wN.reshape(nb, 128).T)
    return (colblk(Gc), colblk(Gs), Gc.astype(ml_dtypes.bfloat16), Gs.astype(ml_dtypes.bfloat16), wNT)


def _emb(L):
    t = np.linspace(0.0, 1.0, L, dtype=np.float32)[:, None]
    w = (2.0 * np.float32(math.pi) * np.arange(L, dtype=np.float32)[:, None] / np.float32(L)).astype(np.float32)
    f = np.linspace(1e-4, 15, 16, dtype=np.float32)[None, :]
    emb = np.concatenate([t, np.cos(f * w), -np.sin(f * w)], axis=-1).astype(np.float32)
    tcol = np.ascontiguousarray(t[:, 0].reshape(L // 128, 128).T)
    return np.ascontiguousarray(emb.T), tcol


_CONST = {}


def _consts():
    if _CONST:
        return _CONST
    c = {}
    c["ident"] = np.eye(128, dtype=np.float32)
    c["TBLc"], c["TBLs"], c["GRc"], c["GRs"], c["wN"] = _tbl(T)
    c["TBLc_c"], c["TBLs_c"], c["GRc_c"], c["GRs_c"], c["wN_c"] = _tbl(TC)
    c["embT"], c["tcol"] = _emb(T)
    c["embT_c"], c["tcol_c"] = _emb(TC)
    c["deltas"] = np.abs(np.linspace(math.log(1e-2) / 1.5, math.log(1e-2) / 0.3, 512, dtype=np.float32)).astype(np.float32)
    _CONST.update(c)
    return _CONST


BF = ml_dtypes.bfloat16
INPUT_SPECS = [
    ("x", (T, D), F32), ("ctx", (TC, D), F32), ("ccT", (128, KD, 2), F32),
    ("w_ada", (4, D, 6 * D), F32), ("b_adaT", (4, 128, 48), F32), ("b_ada", (4, 6 * D), F32),
    ("gmT", (4, 128, KD), F32), ("gfT", (4, 128, KD), F32), ("gfin", (D,), F32),
    ("w_in_even", (2, D, 2560), F32), ("w_out_even", (2, D, D), F32),
    ("hy_cwT", (2, 128, 12, 3), F32), ("hy_cbT", (2, 128, 12), F32),
    ("f1w", (2, 33, 64), F32), ("f1bT", (2, 64, 1), F32), ("sfT", (2, 64, 1), F32),
    ("f2w", (2, 64, 64), F32), ("f2bT", (2, 64, 1), F32), ("f3wa", (2, 65, 2048), F32),
    ("hy_skip", (2, 2, 512), F32), ("sgu_lng", (2, 512), F32), ("sgu_wT", (2, 4, 128, 128), F32),
    ("sgu_b", (2, 4, 128), F32),
    ("w_in_odd", (2, D, 2 * DR), F32), ("rg_cwT", (2, HD, NH, 4), F32), ("rg_cbT", (2, HD, NH), F32),
    ("rg_wa", (2, 2, NH, HD, HD), F32), ("rg_wx", (2, 2, NH, HD, HD), F32),
    ("rg_baT", (2, 2, HD, NH), F32), ("rg_bxT", (2, 2, HD, NH), F32), ("rg_lamT", (2, 2, HD, NH), F32),
    ("w_out_odd", (2, DR, D), F32),
    ("w_ffn_in", (4, D, 2 * DFF), F32), ("w_ffn_out", (4, DFF, D), F32),
    ("ident", (128, 128), F32),
    ("TBLc", (NF, 128, NF, 128), BF16), ("TBLs", (NF, 128, NF, 128), BF16),
    ("GRc", (NF * 128, NF * 128), BF16), ("GRs", (NF * 128, NF * 128), BF16), ("wN", (128, NF), F32),
    ("TBLc_c", (NFC, 128, NFC, 128), BF16), ("TBLs_c", (NFC, 128, NFC, 128), BF16),
    ("GRc_c", (NFC * 128, NFC * 128), BF16), ("GRs_c", (NFC * 128, NFC * 128), BF16), ("wN_c", (128, NFC), F32),
    ("embT", (33, T), F32), ("tcol", (128, T // 128), F32),
    ("embT_c", (33, TC), F32), ("tcol_c", (128, TC // 128), F32),
    ("deltas", (512,), F32),
]


def make_in_maps(inp, cores):
    c = _consts()
    f = lambda a: np.ascontiguousarray(np.asarray(a, dtype=np.float32))
    sh = {}
    sh["w_ada"] = f(inp["w_ada"])
    sh["b_adaT"] = f(np.asarray(inp["b_ada"]).reshape(4, 48, 128).transpose(0, 2, 1))
    sh["b_ada"] = f(inp["b_ada"])
    sh["gmT"] = f(np.asarray(inp["norm_mix_g"]).reshape(4, KD, 128).transpose(0, 2, 1))
    sh["gfT"] = f(np.asarray(inp["norm_ffn_g"]).reshape(4, KD, 128).transpose(0, 2, 1))
    sh["gfin"] = f(inp["final_norm_g"])
    sh["w_in_even"] = f(inp["w_in_even"])
    sh["w_out_even"] = f(inp["w_out_even"])
    sh["hy_cwT"] = f(np.asarray(inp["hy_conv_w"]).reshape(2, 3, 12, 128).transpose(0, 3, 2, 1))
    sh["hy_cbT"] = f(np.asarray(inp["hy_conv_b"]).reshape(2, 12, 128).transpose(0, 2, 1))
    sh["f1w"] = f(inp["hy_f1_w"])
    sh["f1bT"] = f(np.asarray(inp["hy_f1_b"]).reshape(2, 64, 1))
    sh["sfT"] = f(np.asarray(inp["hy_sin_freq"]).reshape(2, 64, 1))
    sh["f2w"] = f(inp["hy_f2_w"])
    sh["f2bT"] = f(np.asarray(inp["hy_f2_b"]).reshape(2, 64, 1))
    sh["f3wa"] = f(np.concatenate([np.asarray(inp["hy_f3_w"]), np.asarray(inp["hy_f3_b"])[:, None, :]], axis=1))
    sh["hy_skip"] = f(inp["hy_skip"])
    sh["sgu_lng"] = f(inp["sgu_ln_g"])
    sh["sgu_wT"] = f(np.asarray(inp["sgu_w"]).transpose(0, 1, 3, 2))
    sh["sgu_b"] = f(inp["sgu_b"])
    sh["w_in_odd"] = f(inp["w_in_odd"])
    sh["rg_cwT"] = f(np.asarray(inp["rg_conv_w"]).reshape(2, 4, NH, HD).transpose(0, 3, 2, 1))
    sh["rg_cbT"] = f(np.asarray(inp["rg_conv_b"]).reshape(2, NH, HD).transpose(0, 2, 1))
    sh["rg_wa"] = f(inp["rg_wa"])
    sh["rg_wx"] = f(inp["rg_wx"])
    sh["rg_baT"] = f(np.asarray(inp["rg_ba"]).reshape(2, 2, NH, HD).transpose(0, 1, 3, 2))
    sh["rg_bxT"] = f(np.asarray(inp["rg_bx"]).reshape(2, 2, NH, HD).transpose(0, 1, 3, 2))
    sh["rg_lamT"] = f(np.asarray(inp["rg_lam"]).reshape(2, 2, NH, HD).transpose(0, 1, 3, 2))
    sh["w_out_odd"] = f(inp["w_out_odd"])
    sh["w_ffn_in"] = f(inp["w_ffn_in"])
    sh["w_ffn_out"] = f(inp["w_ffn_out"])
    for kname in ("ident", "TBLc", "TBLs", "GRc", "GRs", "wN", "TBLc_c", "TBLs_c", "GRc_c", "GRs_c", "wN_c",
                  "embT", "tcol", "embT_c", "tcol_c", "deltas"):
        sh[kname] = c[kname]
    maps = []
    x = np.asarray(inp["x"], dtype=np.float32)
    ctx = np.asarray(inp["ctx"], dtype=np.float32)
    cvec = np.asarray(inp["c"], dtype=np.float32)
    ccx = np.asarray(inp["c_ctx"], dtype=np.float32)
    for b in cores:
        m = dict(sh)
        m["x"] = np.ascontiguousarray(x[b])
        m["ctx"] = np.ascontiguousarray(ctx[b])
        cc = np.stack([cvec[b], ccx], axis=-1)
        m["ccT"] = np.ascontiguousarray(cc.reshape(KD, 128, 2).transpose(1, 0, 2))
        maps.append(m)
    return maps


def kernel(**inputs):
    nc, _ = build(4, False)
    maps = make_in_maps(inputs, range(4))
    res = run_bass_kernel_spmd(nc, maps, core_ids=list(range(4)))
    out = np.stack([np.asarray(r["out"], dtype=np.float32) for r in res.results], axis=0)
    return out
```

```python
import contextlib
import math
import numpy as np
import ml_dtypes
import concourse.bass as bass
import concourse.mybir as mybir
from concourse.bass_utils import run_bass_kernel_spmd

F32 = mybir.dt.float32
BF16 = mybir.dt.bfloat16
AF = mybir.ActivationFunctionType
ALU = mybir.AluOpType
AX = mybir.AxisListType

D = 1024
KD = 8
T = 4096
TC = 256
TA = T + TC
DFF = 2816
DR = 1408
NH = 16
HD = 88
EPS = 1e-6
NF = 33
NFC = 3


class Buf:
    __slots__ = ("w", "r", "name")

    def __init__(self, name=""):
        self.w = []
        self.r = []
        self.name = name


class KB:
    def __init__(self):
        self.nc = bass.Bass("TRN2", target_bir_lowering=False)
        self.st = contextlib.ExitStack()
        self.E = ["pe", "act", "dve", "pool", "sp"]
        self.sem = {e: self.st.enter_context(self.nc.semaphore("sem_" + e)) for e in self.E}
        self.cnt = {e: 0 for e in self.E}
        self.ops = {e: [] for e in self.E}
        self.waited = {e: {} for e in self.E}
        self.dq = {}
        self.dqi = {}
        self.dqlast = {}
        for q, n in (("sp", 16), ("pool", 16), ("act", 8)):
            self.dq[q] = [self.st.enter_context(self.nc.semaphore("dq_%s%d" % (q, i))) for i in range(n)]
            self.dqi[q] = 0
        self.nops = 0
        self.dummy = self.st.enter_context(self.nc.sbuf_tensor("kbdummy", [128, 8], F32))
        self.op("dve", lambda eh: eh.memset(self.dummy[:, 0:8], 0.0), (), ())

    def _collect(self, e, reads, writes):
        deps = {}

        def add(lst):
            for (sm, v) in lst:
                if deps.get(sm, 0) < v:
                    deps[sm] = v
        for b in reads:
            add(b.w)
        for b in writes:
            add(b.w)
            add(b.r)
        res = []
        for sm, v in deps.items():
            if e == "pe" and sm == self.sem["pe"]:
                continue
            if self.waited[e].get(sm, 0) >= v:
                continue
            self.waited[e][sm] = v
            res.append((sm, v))
        return res

    def _update(self, dep, reads, writes):
        for b in writes:
            b.w = [dep]
            b.r = []
        for b in reads:
            if b in writes:
                continue
            b.r = [d for d in b.r if d[0] != dep[0]] + [dep]

    def op(self, e, fn, reads=(), writes=()):
        waits = self._collect(e, reads, writes)
        self.cnt[e] += 1
        dep = (self.sem[e], self.cnt[e])
        self.ops[e].append((waits, fn, (self.sem[e], 1)))
        self._update(dep, reads, writes)
        self.nops += 1

    def dma(self, q, out, in_, reads=(), writes=()):
        waits = self._collect(q, reads, writes)
        i = self.dqi[q]
        self.dqi[q] += 1
        n = len(self.dq[q])
        slot = self.dq[q][i % n]
        tgt = 16 * (i // n + 1)
        if i >= n:
            prev = 16 * (i // n)
            if self.waited[q].get(slot, 0) < prev:
                self.waited[q][slot] = prev
                waits.append((slot, prev))
        self.ops[q].append((waits, lambda eh: eh.dma_start(out=out, in_=in_), (slot, 16)))
        self.dqlast[slot] = tgt
        self._update((slot, tgt), reads, writes)
        self.nops += 1

    def barrier(self):
        self._barrier1()
        d = self.dummy
        self.op("act", lambda eh: eh.activation(out=d[:, 1:2], in_=d[:, 0:1], func=AF.Copy), (), ())
        self.op("dve", lambda eh: eh.memset(d[:, 2:3], 0.0), (), ())
        self.op("pool", lambda eh: eh.memset(d[:, 3:4], 0.0), (), ())
        self._barrier1()

    def _barrier1(self):
        deps = [(self.sem[e], self.cnt[e]) for e in self.E if self.cnt[e] > 0]
        deps += list(self.dqlast.items())
        for e in self.E:
            waits = []
            for sm, v in deps:
                if sm == self.sem[e]:
                    continue
                if self.waited[e].get(sm, 0) >= v:
                    continue
                self.waited[e][sm] = v
                waits.append((sm, v))
            if waits:
                self.ops[e].append((waits, None, None))

    def finish(self):
        self.barrier()
        with self.nc.Block() as block:
            def emit(e):
                def f(eh):
                    for waits, fn, inc in self.ops[e]:
                        for sm, v in waits:
                            eh.wait_ge(sm, v)
                        if fn is not None:
                            fn(eh).then_inc(*inc)
                return f
            block.tensor(emit("pe"))
            block.scalar(emit("act"))
            block.vector(emit("dve"))
            block.gpsimd(emit("pool"))
            block.sync(emit("sp"))
        self.st.close()

    def mm(self, out, lhsT, rhs, start, stop, reads, writes):
        self.op("pe", lambda eh: eh.matmul(out, lhsT, rhs, start=start, stop=stop), reads, writes)

    def tr(self, out, in_, ident, reads, writes):
        self.op("pe", lambda eh: eh.transpose(out, in_, ident), reads, writes)

    def act(self, out, in_, func, reads, writes, bias=None, scale=None, accum_out=None):
        kw = {}
        if bias is not None:
            kw["bias"] = bias
        if scale is not None:
            kw["scale"] = scale
        if accum_out is not None:
            kw["accum_out"] = accum_out
        self.op("act", lambda eh: eh.activation(out=out, in_=in_, func=func, **kw), reads, writes)

    def tt(self, e, out, in0, in1, op, reads, writes):
        self.op(e, lambda eh: eh.tensor_tensor(out=out, in0=in0, in1=in1, op=op), reads, writes)

    def ts(self, e, out, in0, s1, s2, op0, op1, reads, writes):
        if s2 is None:
            self.op(e, lambda eh: eh.tensor_scalar(out=out, in0=in0, scalar1=s1, scalar2=None, op0=op0), reads, writes)
        else:
            self.op(e, lambda eh: eh.tensor_scalar(out=out, in0=in0, scalar1=s1, scalar2=s2, op0=op0, op1=op1), reads, writes)

    def stt(self, e, out, in0, scalar, in1, op0, op1, reads, writes):
        self.op(e, lambda eh: eh.scalar_tensor_tensor(out=out, in0=in0, scalar=scalar, in1=in1, op0=op0, op1=op1), reads, writes)

    def copy(self, e, out, in_, reads, writes):
        self.op(e, lambda eh: eh.tensor_copy(out=out, in_=in_), reads, writes)

    def memset(self, e, ap, val, writes):
        self.op(e, lambda eh: eh.memset(ap, val), (), writes)


class Arena:
    def __init__(self, k, nbytes):
        self.k = k
        self.cap = nbytes
        self.t = k.st.enter_context(k.nc.sbuf_tensor("arena", [128, nbytes // 4], F32))
        self.top = 0

    def alloc(self, free_shape, dtype, parts=128):
        es = 4 if dtype == F32 else 2
        n = int(np.prod(free_shape))
        nb = (n * es + 63) // 64 * 64
        off = self.top
        self.top += nb
        assert self.top <= self.cap, "SBUF arena overflow %d > %d" % (self.top, self.cap)
        assert (n * es) % 4 == 0
        ap = self.t[0:parts, off // 4: off // 4 + (n * es) // 4]
        if dtype == BF16:
            ap = ap.bitcast(BF16)
        fs = list(free_shape)
        if len(fs) == 2:
            ap = ap.rearrange("p (a b) -> p a b", a=fs[0])
        elif len(fs) == 3:
            ap = ap.rearrange("p (a b c) -> p a b c", a=fs[0], b=fs[1])
        elif len(fs) == 4:
            ap = ap.rearrange("p (a b c d) -> p a b c d", a=fs[0], b=fs[1], c=fs[2])
        return ap


class Ring:
    def __init__(self, arena, n, free_shape, dtype, name, parts=128):
        self.aps = [arena.alloc(free_shape, dtype, parts) for _ in range(n)]
        self.bufs = [Buf("%s%d" % (name, i)) for i in range(n)]
        self.i = 0

    def get(self):
        j = self.i % len(self.aps)
        self.i += 1
        return self.aps[j], self.bufs[j]


def build(nl=4, dbg=False, stop=10**9):
    k = KB()
    phc = [0]

    def go(name):
        phc[0] += 1
        import os
        skip = [int(v) for v in os.environ.get('SKIPPH', '').split(',') if v]
        ok = phc[0] <= stop and phc[0] not in skip
        if ok:
            print('phase', phc[0], name, flush=True)
        return ok
    nc = k.nc

    def din(name, shape, dt=F32):
        return nc.dram_tensor(name, list(shape), dt, kind="ExternalInput").ap()

    def dscr(name, shape, dt=F32):
        return nc.dram_tensor(name, list(shape), dt).ap()

    I = {}
    for name, shape, dt in INPUT_SPECS:
        I[name] = din(name, shape, dt)
    out_d = nc.dram_tensor("out", [T, D], F32, kind="ExternalOutput").ap()
    if dbg:
        dbg_d = nc.dram_tensor("dbg", [TA, D], F32, kind="ExternalOutput").ap()

    X = dscr("Xs", [TA, D])
    XP = dscr("XPs", [T, D])
    ZA = dscr("ZAs", [128, 12, TA])
    G1 = dscr("G1s", [128, 4, TA])
    G2 = dscr("G2s", [128, 4, TA])
    YAd = dscr("YAs", [128, 4, TA], BF16)
    YBd = dscr("YBs", [128, 4, TA], BF16)
    KF = dscr("KFs", [2, 2, NF, 128, 2, 512])
    ZRd = dscr("ZRs", [HD, NH, TA])
    GTd = dscr("GTs", [HD, NH, TA], BF16)
    YRd = dscr("YRs", [HD, NH, TA], BF16)
    gX = [Buf("X%d" % i) for i in range(TA // 128)]
    gXP = [Buf("XP%d" % i) for i in range(T // 128)]
    gZA, gG1, gG2, gYA, gYB, gKF, gZR, gGT, gYR = (Buf(n) for n in
                                                   ("ZA", "G1", "G2", "YA", "YB", "KF", "ZR", "GT", "YR"))

    ar = Arena(k, 212480)
    ps_t = [k.st.enter_context(nc.psum_tensor("ps%d" % i, [128, 512], F32)) for i in range(8)]
    ps_b = [Buf("ps%d" % i) for i in range(8)]
    ps_i = [0]

    ps_lim = [0, 8]

    def ps():
        lo, hi = ps_lim
        j = lo + ps_i[0] % (hi - lo)
        ps_i[0] += 1
        return ps_t[j][:, :], ps_b[j]

    ident = ar.alloc([128], F32)
    g_ident = Buf("ident")
    k.dma("sp", ident, I["ident"], (), (g_ident,))
    cst = ar.alloc([8], F32)
    g_cst = Buf("cst")
    k.memset("dve", cst[:, 0:1], EPS, (g_cst,))
    k.memset("dve", cst[:, 1:2], 1.0, (g_cst,))
    k.memset("dve", cst[:, 2:3], 0.0, (g_cst,))
    ones_f = ar.alloc([128], F32)
    g_ones = Buf("ones")
    k.memset("dve", ones_f, 1.0, (g_ones,))
    cc_f = ar.alloc([KD, 2], F32)
    scT = ar.alloc([KD, 2], BF16)
    screp = ar.alloc([KD, 2, 128], BF16)
    g_sc = Buf("sc")
    k.dma("sp", cc_f, I["ccT"], (), (g_sc,))
    sc_f = ar.alloc([KD, 2], F32)
    k.act(sc_f, cc_f, AF.Silu, (g_sc,), (g_sc,))
    k.copy("dve", scT, sc_f, (g_sc,), (g_sc,))
    for kk in range(KD):
        for w in range(2):
            k.act(screp[:, kk, w, :], ones_f, AF.Identity, (g_ones, g_sc), (g_sc,), scale=sc_f[:, kk, w:w + 1])
    modf = ar.alloc([2, 6, KD], F32)
    AB = ar.alloc([2, 4, KD], F32)
    gbc = ar.alloc([2, 2, D], F32)
    g_mod = Buf("mod")
    g_gbc = Buf("gbc")
    stage = Ring(ar, 4, [512], F32, "stage")
    persist_top = ar.top

    def xrows(buf_ap, perm, pos0):
        if pos0 >= T or not perm:
            return [(0, 128, buf_ap[pos0:pos0 + 128, :])]
        c0 = pos0 // 64
        v = buf_ap[0:T, :].rearrange("(r c) d -> c r d", c=64)
        return [(0, 64, v[c0]), (64, 128, v[c0 + 1])]

    def xguard(which, pos0):
        return (gXP if which == "XP" else gX)[pos0 // 128]

    def load_x(q, tile_ap, tile_g, src, perm, pos0):
        buf_ap = XP if src == "XP" else X
        for (p0, p1, rows) in xrows(buf_ap, perm, pos0):
            k.dma(q, tile_ap[p0:p1, :], rows, (xguard(src, pos0),) if not perm or pos0 >= T else tuple(gX[:T // 128]), (tile_g,))

    def store_x(q, tile_ap, tile_g, dst, pos0):
        buf_ap = XP if dst == "XP" else X
        k.dma(q, buf_ap[pos0:pos0 + 128, :], tile_ap, (tile_g,), (xguard(dst, pos0),))

    def tile_list(n, with_ctx=True):
        res = [(t0, n) for t0 in range(0, T, n)]
        if with_ctx:
            res += [(T + t0, min(n, TC)) for t0 in range(0, TC, n)]
        return res

    class WT:
        def __init__(self, ap, nk, ncol, step=512):
            self.ap = ap
            self.step = step
            self.nk = nk
            self.ncol = ncol
            self.g = [[Buf("w") for _ in range((ncol + step - 1) // step)] for _ in range(nk)]

        def gs(self, kk, c0, c1):
            return tuple(self.g[kk][j] for j in range(c0 // self.step, (c1 - 1) // self.step + 1))

    def load_cast(wt, src_view, parts=128):
        for kk in range(wt.nk):
            for c0 in range(0, wt.ncol, wt.step):
                c1 = min(wt.ncol, c0 + wt.step)
                st, sg = stage.get()
                k.dma("sp", st[0:parts, 0:c1 - c0], src_view[:, kk, c0:c1], (), (sg,))
                k.copy("pool", wt.ap[:, kk, c0:c1], st[0:parts, 0:c1 - c0], (sg,), (wt.g[kk][c0 // wt.step],))

    def wview(src_ap, p=128):
        return src_ap.rearrange("(k p) n -> p k n", p=p)

    def adaln(l):
        mark = ar.top
        wts = [WT(ar.alloc([KD, D], BF16), KD, D) for _ in range(2)]
        bT = ar.alloc([48], F32)
        bbc = ar.alloc([2, D], F32)
        gm = ar.alloc([2, KD], F32)
        g_b = Buf("adab")
        k.dma("sp", bT, I["b_adaT"][l], (), (g_b,))
        k.dma("sp", gm[:, 0, :], I["gmT"][l], (), (g_b,))
        k.dma("sp", gm[:, 1, :], I["gfT"][l], (), (g_b,))
        for jj, j in enumerate((2, 5)):
            k.dma("sp", bbc[:, jj, :], I["b_ada"][l, j * D:(j + 1) * D].partition_broadcast(128), (), (g_b,))
        wv = I["w_ada"][l].rearrange("(k p) n -> p k n", p=128)
        for j in range(6):
            wtt = wts[j % 2]
            wt = wtt.ap
            load_cast(wtt, wv[:, :, j * D:(j + 1) * D])
            if j in (0, 1, 3, 4):
                pt, pg = ps()
                pv = pt[:, 0:16].rearrange("p (a b) -> p a b", a=KD)
                for oc in range(KD):
                    for kk in range(KD):
                        k.mm(pv[:, oc, :], wt[:, kk, oc * 128:(oc + 1) * 128], scT[:, kk, :], kk == 0, kk == KD - 1,
                             wtt.gs(kk, oc * 128, (oc + 1) * 128) + (g_sc,), (pg,))
                for w in range(2):
                    k.tt("dve", modf[:, w, j, :], pv[:, :, w], bT[:, j * KD:(j + 1) * KD], ALU.add,
                         (pg, g_b), (g_mod,))
            else:
                jj = 0 if j == 2 else 1
                for w in range(2):
                    for hf in range(2):
                        pt, pg = ps()
                        for kk in range(KD):
                            k.mm(pt, screp[:, kk, w, :], wt[:, kk, hf * 512:(hf + 1) * 512], kk == 0, kk == KD - 1,
                                 wtt.gs(kk, hf * 512, (hf + 1) * 512) + (g_sc,), (pg,))
                        k.tt("dve", gbc[:, w, jj, hf * 512:(hf + 1) * 512], pt, bbc[:, jj, hf * 512:(hf + 1) * 512],
                             ALU.add, (pg, g_b), (g_gbc,))
        for w in range(2):
            k.stt("dve", AB[:, w, 0, :], modf[:, w, 1, :], 1.0, gm[:, 0, :], ALU.add, ALU.mult, (g_mod, g_b), (g_mod,))
            k.copy("dve", AB[:, w, 1, :], modf[:, w, 0, :], (g_mod,), (g_mod,))
            k.stt("dve", AB[:, w, 2, :], modf[:, w, 4, :], 1.0, gm[:, 1, :], ALU.add, ALU.mult, (g_mod, g_b), (g_mod,))
            k.copy("dve", AB[:, w, 3, :], modf[:, w, 3, :], (g_mod,), (g_mod,))
        k.barrier()
        ar.top = mark

    class NormRes:
        def __init__(self, nxs=4):
            self.junk = ar.alloc([D], BF16)
            self.g_junk = Buf("junk")
            self.ss = Ring(ar, 8, [4], F32, "ss")
            self.xs = Ring(ar, nxs, [D], F32, "xs")

    def norm_sub(nr, x_ap, x_g, which, ab0, hxT, hx_g, col0):
        ss, ssg = nr.ss.get()
        k.memset("pool", ss[:, 0:1], 0.0, (ssg,))
        k.act(nr.junk, x_ap, AF.Square, (x_g, ssg), (nr.g_junk, ssg), accum_out=ss[:, 0:1])
        k.act(ss[:, 1:2], ss[:, 0:1], AF.Sqrt, (ssg, g_cst), (ssg,), bias=cst[:, 0:1], scale=1.0 / D)
        k.op("dve", lambda eh: eh.reciprocal(out=ss[:, 2:3], in_=ss[:, 1:2]), (ssg,), (ssg,))
        xs, xsg = nr.xs.get()
        k.ts("dve", xs, x_ap, ss[:, 2:3], None, ALU.mult, None, (x_g, ssg), (xsg,))
        return xs, xsg, ss, ssg

    def norm_tile(nr, xsubs, which, ab0, hxT, hx_g):
        ns = len(xsubs)
        normed = []
        for j, (xa, xg) in enumerate(xsubs):
            xs, xsg, _, _ = norm_sub(nr, xa, xg, which, ab0, hxT, hx_g, j * 128)
            normed.append((xs, xsg))
        for kk in range(KD):
            pt, pg = ps()
            for j, (xs, xsg) in enumerate(normed):
                k.tr(pt[:, j * 128:(j + 1) * 128], xs[:, kk * 128:(kk + 1) * 128], ident, (xsg, g_ident), (pg,))
            k.act(hxT[:, kk, 0:ns * 128], pt[:, 0:ns * 128], AF.Identity, (pg, g_mod), (hx_g,),
                  bias=AB[:, which, ab0 + 1, kk:kk + 1], scale=AB[:, which, ab0, kk:kk + 1])

    def outproj_resid(l, yload, kparts, wtt, src, perm, dst, with_ctx):
        xr = Ring(ar, 4, [D], F32, "xo")
        tmpr = Ring(ar, 2, [D], F32, "tmpo")
        nk = len(kparts)
        for (t0, n) in tile_list(512, with_ctx):
            w = 1 if t0 >= T else 0
            ya, yg = yload(t0, n)
            for j in range(n // 128):
                xa, xg = xr.get()
                load_x("act", xa, xg, src, perm, t0 + j * 128)
                tmp, tg = tmpr.get()
                for hf in range(2):
                    pt, pg = ps()
                    for kk in range(nk):
                        kp = kparts[kk]
                        k.mm(pt, ya[0:kp, kk, j * 128:(j + 1) * 128], wtt.ap[0:kp, kk, hf * 512:(hf + 1) * 512],
                             kk == 0, kk == nk - 1, (yg,) + wtt.gs(kk, hf * 512, (hf + 1) * 512), (pg,))
                    k.tt("dve", tmp[:, hf * 512:(hf + 1) * 512], pt, gbc[:, w, 0, hf * 512:(hf + 1) * 512], ALU.mult,
                         (pg, g_gbc), (tg,))
                k.tt("pool", xa, xa, tmp, ALU.add, (xg, tg), (xg,))
                store_x("sp", xa, xg, dst if t0 < T else "X", t0 + j * 128)

    def ffn(l, buf, with_ctx):
        mark = ar.top
        w1t = WT(ar.alloc([KD, 2 * DFF], BF16), KD, 2 * DFF)
        w2t = WT(ar.alloc([22, D], BF16), 22, D)
        w1, w2 = w1t.ap, w2t.ap
        load_cast(w1t, wview(I["w_ffn_in"][l]))
        load_cast(w2t, wview(I["w_ffn_out"][l]))
        nr = NormRes(2)
        xr = Ring(ar, 3, [D], F32, "xf")
        hxr = Ring(ar, 2, [KD, 256], BF16, "hxf")
        hr = Ring(ar, 1, [22, 256], BF16, "hf")
        sgr = Ring(ar, 2, [256], F32, "sg")
        tmpr = Ring(ar, 1, [512], F32, "tmpf")
        for (t0, n) in tile_list(256, with_ctx):
            w = 1 if t0 >= T else 0
            src = buf if t0 < T else "X"
            xs = []
            for j in range(n // 128):
                xa, xg = xr.get()
                load_x("act", xa, xg, src, False, t0 + j * 128)
                xs.append((xa, xg))
            hxT, hxg = hxr.get()
            norm_tile(nr, xs, w, 2, hxT, hxg)
            hT, hg = hr.get()
            for c in range(22):
                p1, g1 = ps()
                p2, g2 = ps()
                for kk in range(KD):
                    k.mm(p1[:, 0:n], w1[:, kk, c * 128:(c + 1) * 128], hxT[:, kk, 0:n], kk == 0, kk == KD - 1,
                         (hxg,) + w1t.gs(kk, c * 128, (c + 1) * 128), (g1,))
                for kk in range(KD):
                    k.mm(p2[:, 0:n], w1[:, kk, DFF + c * 128:DFF + (c + 1) * 128], hxT[:, kk, 0:n], kk == 0,
                         kk == KD - 1, (hxg,) + w1t.gs(kk, DFF + c * 128, DFF + (c + 1) * 128), (g2,))
                sg, sgg = sgr.get()
                k.act(sg[:, 0:n], p1[:, 0:n], AF.Silu, (g1,), (sgg,))
                k.tt("dve", hT[:, c, 0:n], p2[:, 0:n], sg[:, 0:n], ALU.mult, (g2, sgg), (hg,))
            for j in range(n // 128):
                xa, xg = xs[j]
                for hf in range(2):
                    pt, pg = ps()
                    for c in range(22):
                        k.mm(pt, hT[:, c, j * 128:(j + 1) * 128], w2[:, c, hf * 512:(hf + 1) * 512], c == 0, c == 21,
                             (hg,) + w2t.gs(c, hf * 512, (hf + 1) * 512), (pg,))
                    tmp, tg = tmpr.get()
                    k.tt("dve", tmp, pt, gbc[:, w, 1, hf * 512:(hf + 1) * 512], ALU.mult, (pg, g_gbc), (tg,))
                    k.tt("pool", xa[:, hf * 512:(hf + 1) * 512], xa[:, hf * 512:(hf + 1) * 512], tmp, ALU.add,
                         (xg, tg), (xg,))
                store_x("sp", xa, xg, src, t0 + j * 128)
        k.barrier()
        ar.top = mark

    def fwd_ring(L, nbuf):
        return Ring(ar, nbuf, [L // 128, 128], BF16, "tbl%d" % L)

    def inv_ring(L):
        return Ring(ar, 6, [2, 512 if L == T else 256], BF16, "gr%d" % L)

    def dft_forward(L, Pin, Qin, g_in, epilogue, tb):
        ntc = L // 128
        nfc = NF if L == T else NFC
        tc_name, ts_name = ("TBLc", "TBLs") if L == T else ("TBLc_c", "TBLs_c")
        for fc in range(nfc):
            ct, cg = tb.get()
            k.dma("sp", ct, I[tc_name][fc][:, 0:ntc, :], (), (cg,))
            st_, sg_ = tb.get()
            k.dma("act" if L == T else "sp", st_, I[ts_name][fc][:, 0:ntc, :], (), (sg_,))
            pc, gc = ps()
            for tc in range(ntc):
                k.mm(pc, ct[:, tc, :], Pin[:, tc, :], tc == 0, tc == ntc - 1, (cg, g_in), (gc,))
            pS, gS = ps()
            for tc in range(ntc):
                k.mm(pS, st_[:, tc, :], Qin[:, tc, :], tc == 0, tc == ntc - 1, (sg_, g_in), (gS,))
            epilogue(fc, pc, gc, pS, gS)

    def dft_inverse(L, Ysp, g_y, epilogue, tb):
        nfc = NF if L == T else NFC
        tw = 512 if L == T else 256
        gc_name, gs_name = ("GRc", "GRs") if L == T else ("GRc_c", "GRs_c")
        for tg in range(L // tw):
            pss = [(ps_t[j][:, :], ps_b[j]) for j in range(4)]
            ps_lim[0] = 4
            for fc in range(nfc):
                tt_, tg_ = tb.get()
                k.dma("sp", tt_[:, 0, :], I[gc_name][fc * 128:(fc + 1) * 128, tg * tw:(tg + 1) * tw], (), (tg_,))
                k.dma("act", tt_[:, 1, :], I[gs_name][fc * 128:(fc + 1) * 128, tg * tw:(tg + 1) * tw], (), (tg_,))
                for cc in range(4):
                    pt, pg = pss[cc]
                    k.mm(pt[:, 0:tw], Ysp[:, fc, 0, cc * 128:(cc + 1) * 128], tt_[:, 0, :], fc == 0, False,
                         (g_y, tg_), (pg,))
                    k.mm(pt[:, 0:tw], Ysp[:, fc, 1, cc * 128:(cc + 1) * 128], tt_[:, 1, :], False, fc == nfc - 1,
                         (g_y, tg_), (pg,))
            for cc in range(4):
                epilogue(cc, tg, pss[cc][0], pss[cc][1])
        ps_lim[0] = 0

    def kf_gen(i, L, li):
        mark = ar.top
        ntc = L // 128
        nfc = NF if L == T else NFC
        sfx = "" if L == T else "_c"
        emb = ar.alloc([L], F32, parts=33)
        f1w = ar.alloc([64], F32, parts=33)
        f2w = ar.alloc([64], F32, parts=64)
        f3w = ar.alloc([2048], F32, parts=65)
        vec = ar.alloc([4], F32, parts=64)
        sfb = ar.alloc([2], F32, parts=64)
        h1 = ar.alloc([L], F32, parts=64)
        h2 = ar.alloc([L], F32, parts=65)
        tcol = ar.alloc([ntc], F32)
        dlt = ar.alloc([512], F32)
        wN = ar.alloc([nfc], F32)
        skb = ar.alloc([2, 512], F32)
        g_c = Buf("kfc")
        g_h1 = Buf("h1")
        g_h2 = Buf("h2")
        k.dma("sp", emb, I["embT" + sfx], (), (g_c,))
        k.dma("sp", f1w, I["f1w"][i], (), (g_c,))
        k.dma("sp", f2w, I["f2w"][i], (), (g_c,))
        k.dma("sp", f3w, I["f3wa"][i], (), (g_c,))
        k.dma("sp", vec[:, 0:1], I["f1bT"][i], (), (g_c,))
        k.dma("sp", vec[:, 1:2], I["f2bT"][i], (), (g_c,))
        k.dma("sp", vec[:, 2:3], I["sfT"][i], (), (g_c,))
        k.dma("sp", tcol, I["tcol" + sfx], (), (g_c,))
        k.dma("sp", dlt, I["deltas"].partition_broadcast(128), (), (g_c,))
        k.dma("sp", wN, I["wN" + sfx], (), (g_c,))
        for o in range(2):
            k.dma("sp", skb[:, o, :], I["hy_skip"][i, o].partition_broadcast(128), (), (g_c,))
        k.tt("dve", sfb[:, 0:1], vec[:, 0:1], vec[:, 2:3], ALU.mult, (g_c,), (g_c,))
        k.tt("dve", sfb[:, 1:2], vec[:, 1:2], vec[:, 2:3], ALU.mult, (g_c,), (g_c,))
        k.memset("dve", h2[64:65, :], 1.0, (g_h2,))
        argr = Ring(ar, 2, [512], F32, "arg", parts=64)
        wrr = Ring(ar, 2, [512], F32, "wrp", parts=64)
        cw = min(512, L)

        def sin_layer(src_lhsT, kp, rhs_full, g_rhs, bcol, dst, g_dst):
            for c0 in range(0, L, cw):
                pt, pg = ps()
                k.mm(pt[0:64, 0:cw], src_lhsT, rhs_full[0:kp, c0:c0 + cw], True, True, (g_c, g_rhs), (pg,))
                a, ag = argr.get()
                k.ts("dve", a[:, 0:cw], pt[0:64, 0:cw], vec[:, 2:3], sfb[:, bcol:bcol + 1], ALU.mult, ALU.add,
                     (pg, g_c), (ag,))
                w_, wg_ = wrr.get()
                k.ts("dve", w_[:, 0:cw], a[:, 0:cw], math.pi, -2.0 * math.pi, ALU.is_gt, ALU.mult, (ag,), (wg_,))
                k.tt("dve", a[:, 0:cw], a[:, 0:cw], w_[:, 0:cw], ALU.add, (ag, wg_), (ag,))
                k.ts("dve", w_[:, 0:cw], a[:, 0:cw], -math.pi, 2.0 * math.pi, ALU.is_lt, ALU.mult, (ag,), (wg_,))
                k.tt("dve", a[:, 0:cw], a[:, 0:cw], w_[:, 0:cw], ALU.add, (ag, wg_), (ag,))
                k.act(dst[0:64, c0:c0 + cw], a[:, 0:cw], AF.Sin, (ag,), (g_dst,))
        sin_layer(f1w[0:33, :], 33, emb, g_c, 0, h1, g_h1)
        sin_layer(f2w[0:64, :], 64, h1, g_h1, 1, h2, g_h2)

        PQ = ar.alloc([2, ntc, 512], BF16)
        g_pq = Buf("pq")
        acc = ar.alloc([2, 512], F32)
        g_acc = Buf("acc")
        rs = ar.alloc([512], F32)
        g_rs = Buf("rs")
        dec_r = Ring(ar, 1, [512], F32, "dec")
        hh_r = Ring(ar, 1, [2, 512], F32, "hh")
        sq_r = Ring(ar, 1, [2, 512], F32, "sq")
        kfo_r = Ring(ar, 1, [2, 512], F32, "kfo")
        tbf = fwd_ring(L, 2)
        for o in range(2):
            k.memset("pool", acc, 0.0, (g_acc,))
            for tc in range(ntc):
                dec, dg = dec_r.get()
                k.ts("pool", dec, dlt, tcol[:, tc:tc + 1], None, ALU.mult, None, (g_c,), (dg,))
                k.act(dec, dec, AF.Exp, (dg,), (dg,), scale=-1.0)
                hh, hg_ = hh_r.get()
                for dr in range(2):
                    pt, pg = ps()
                    k.mm(pt, h2[0:65, tc * 128:(tc + 1) * 128], f3w[0:65, o * 1024 + dr * 512:o * 1024 + (dr + 1) * 512],
                         True, True, (g_h2, g_c), (pg,))
                    k.tt("dve", hh[:, dr, :], pt, dec, ALU.mult, (pg, dg), (hg_,))
                if tc == 0:
                    k.memset("dve", hh[0:1, 1, :], 0.0, (hg_,))
                sq, sqg = sq_r.get()
                k.tt("pool", sq, hh, hh, ALU.mult, (hg_,), (sqg,))
                k.tt("pool", acc, acc, sq, ALU.add, (sqg, g_acc), (g_acc,))
                k.tt("dve", PQ[:, 0, tc, :], hh[:, 0, :], hh[:, 1, :], ALU.add, (hg_,), (g_pq,))
                k.tt("dve", PQ[:, 1, tc, :], hh[:, 1, :], hh[:, 0, :], ALU.subtract, (hg_,), (g_pq,))
            k.tt("pool", acc[:, 0, :], acc[:, 0, :], acc[:, 1, :], ALU.add, (g_acc,), (g_acc,))
            pt, pg = ps()
            k.mm(pt, ones_f, acc[:, 0, :], True, True, (g_ones, g_acc), (pg,))
            k.act(rs, pt, AF.Sqrt, (pg, g_cst), (g_rs,), bias=cst[:, 0:1], scale=1.0)
            k.op("dve", lambda eh: eh.reciprocal(out=rs, in_=rs), (g_rs,), (g_rs,))

            def epi(fc, pc, gc, pS, gS, o=o):
                kt, kg = kfo_r.get()
                k.tt("dve", kt[:, 0, :], pc, rs, ALU.mult, (gc, g_rs), (kg,))
                k.tt("dve", kt[:, 0, :], kt[:, 0, :], skb[:, o, :], ALU.add, (kg, g_c), (kg,))
                k.tt("dve", kt[:, 1, :], pS, rs, ALU.mult, (gS, g_rs), (kg,))
                k.ts("pool", kt, kt, wN[:, fc:fc + 1], None, ALU.mult, None, (kg, g_c), (kg,))
                k.dma("sp", KF[li, o, fc], kt, (kg,), (gKF,))
            dft_forward(L, PQ[:, 0], PQ[:, 1], g_pq, epi, tbf)
        k.barrier()
        ar.top = mark

    def even_mixer(l):
        i = l // 2
        run_ctx = l < 3
        def e1():
            mark = ar.top
            wint = WT(ar.alloc([KD, 2560], BF16), KD, 2560)
            win = wint.ap
            load_cast(wint, wview(I["w_in_even"][i]))
            swT = ar.alloc([4, 128], BF16)
            g_sw = Buf("sw")
            st_, sg_ = stage.get()
            k.dma("sp", st_.rearrange("p (g q) -> p g q", g=4), I["sgu_wT"][i].rearrange("g q p -> q g p"), (), (sg_,))
            k.copy("pool", swT, st_.rearrange("p (g q) -> p g q", g=4), (sg_,), (g_sw,))
            lng = ar.alloc([512], F32)
            sbb = ar.alloc([4, 128], F32)
            k.dma("sp", lng, I["sgu_lng"][i].partition_broadcast(128), (), (g_sw,))
            k.dma("sp", sbb, I["sgu_b"][i].rearrange("g p -> (g p)").partition_broadcast(128), (), (g_sw,))
            nr = NormRes()
            xr = Ring(ar, 5, [D], F32, "xe")
            hxr = Ring(ar, 2, [KD, 512], BF16, "hxe")
            zar = Ring(ar, 2, [12, 512], F32, "zae")
            ur = Ring(ar, 2, [4, 512], BF16, "ue")
            vbr = Ring(ar, 2, [512], F32, "vbe")
            vnr = Ring(ar, 2, [512], BF16, "vne")
            str_ = Ring(ar, 4, [4], F32, "ste")
            ybr = Ring(ar, 2, [4, 512], BF16, "ybe")
            tsr = Ring(ar, 2, [512], F32, "tse")
            for (t0, n) in tile_list(512, run_ctx):
                w = 1 if t0 >= T else 0
                xs = []
                for j in range(n // 128):
                    xa, xg = xr.get()
                    load_x("act", xa, xg, "X", False, t0 + j * 128)
                    xs.append((xa, xg))
                hxT, hxg = hxr.get()
                norm_tile(nr, xs, w, 0, hxT, hxg)
                za, zg = zar.get()
                for c in range(12):
                    pt, pg = ps()
                    for kk in range(KD):
                        k.mm(pt[:, 0:n], win[:, kk, c * 128:(c + 1) * 128], hxT[:, kk, 0:n], kk == 0, kk == KD - 1,
                             (hxg,) + wint.gs(kk, c * 128, (c + 1) * 128), (pg,))
                    k.act(za[:, c, 0:n], pt[:, 0:n], AF.Copy, (pg,), (zg,))
                k.dma("act", ZA[:, :, t0:t0 + n], za[:, :, 0:n], (zg,), (gZA,))
                u, ug = ur.get()
                for c in range(4):
                    pt, pg = ps()
                    for kk in range(KD):
                        k.mm(pt[:, 0:n], win[:, kk, 1536 + c * 128:1536 + (c + 1) * 128], hxT[:, kk, 0:n], kk == 0,
                             kk == KD - 1, (hxg,) + wint.gs(kk, 1536 + c * 128, 1536 + (c + 1) * 128), (pg,))
                    k.act(u[:, c, 0:n], pt[:, 0:n], AF.Gelu_apprx_tanh, (pg,), (ug,))
                yb, ybg = ybr.get()
                for j in range(n // 128):
                    pt, pg = ps()
                    for kk in range(KD):
                        k.mm(pt, hxT[:, kk, j * 128:(j + 1) * 128], win[:, kk, 2048:2560], kk == 0, kk == KD - 1,
                             (hxg,) + wint.gs(kk, 2048, 2560), (pg,))
                    vb, vbg = vbr.get()
                    k.act(vb, pt, AF.Gelu_apprx_tanh, (pg,), (vbg,))
                    st_, stg = str_.get()
                    k.op("dve", lambda eh, st_=st_, vb=vb: eh.reduce_sum(out=st_[:, 0:1], in_=vb, axis=AX.X), (vbg,), (stg,))
                    k.ts("dve", st_[:, 1:2], st_[:, 0:1], 1.0 / 512, None, ALU.mult, None, (stg,), (stg,))
                    k.ts("dve", vb, vb, st_[:, 1:2], None, ALU.subtract, None, (vbg, stg), (vbg,))
                    k.memset("pool", st_[:, 2:3], 0.0, (stg,))
                    k.act(nr.junk[:, 0:512], vb, AF.Square, (vbg, stg), (nr.g_junk, stg), accum_out=st_[:, 2:3])
                    k.act(st_[:, 3:4], st_[:, 2:3], AF.Sqrt, (stg, g_cst), (stg,), bias=cst[:, 0:1], scale=1.0 / 512)
                    k.op("dve", lambda eh, st_=st_: eh.reciprocal(out=st_[:, 3:4], in_=st_[:, 3:4]), (stg,), (stg,))
                    vn, vng = vnr.get()
                    k.stt("dve", vn, vb, st_[:, 3:4], lng, ALU.mult, ALU.mult, (vbg, stg, g_sw), (vng,))
                    p2, g2 = ps()
                    for g in range(4):
                        k.mm(p2[:, g * 128:(g + 1) * 128], vn[:, g * 128:(g + 1) * 128], swT[:, g, :], True, True,
                             (vng, g_sw), (g2,))
                    tsb, tsg = tsr.get()
                    k.tt("dve", tsb, p2, sbb.rearrange("p g q -> p (g q)"), ALU.add, (g2, g_sw), (tsg,))
                    k.tt("pool", yb[:, :, j * 128:(j + 1) * 128], tsb.rearrange("p (g q) -> p g q", g=4),
                         u[:, :, j * 128:(j + 1) * 128], ALU.mult, (tsg, ug), (ybg,))
                k.dma("sp", YBd[:, :, t0:t0 + n], yb[:, :, 0:n], (ybg,), (gYB,))
            k.barrier()
            ar.top = mark
        if go('E1'):
            e1()
        if go('kf_gen T'):
            kf_gen(i, T, 0)
        if run_ctx and go('kf_gen TC'):
            kf_gen(i, TC, 1)
        for (L, pos0, li) in ((T, 0, 0), (TC, T, 1)):
            if li == 1 and not run_ctx:
                continue
            if go('hyena_seq %d' % L):
                hyena_seq(i, L, pos0, li)
        if not go('E4'):
            return
        mark = ar.top
        import os
        if os.environ.get('PADWO'):
            ar.alloc([int(os.environ['PADWO']) * 256], F32)
        wot = WT(ar.alloc([KD, D], BF16), KD, D)
        load_cast(wot, wview(I["w_out_even"][i]))
        yr_ = Ring(ar, 2, [KD, 512], BF16, "ycat")

        def yload(t0, n):
            ya, yg = yr_.get()
            k.dma("sp", ya[:, 0:4, 0:n], YAd[:, :, t0:t0 + n], (gYA,), (yg,))
            k.dma("sp", ya[:, 4:8, 0:n], YBd[:, :, t0:t0 + n], (gYB,), (yg,))
            return ya, yg
        outproj_resid(l, yload, [128] * 8, wot, "X", False, "X", run_ctx)
        k.barrier()
        ar.top = mark

    def hyena_seq(i, L, pos0, li):
        ntc = L // 128
        nfc = NF if L == T else NFC
        tw = 512 if L == T else 256
        mark = ar.top
        vT = ar.alloc([ntc, 512], BF16)
        g_vT = Buf("vT")
        cw_ = ar.alloc([12, 3], F32)
        cb_ = ar.alloc([12], F32)
        g_cw = Buf("cw")
        k.dma("sp", cw_, I["hy_cwT"][i], (), (g_cw,))
        k.dma("sp", cb_, I["hy_cbT"][i], (), (g_cw,))
        mark2 = ar.top
        zin_r = Ring(ar, 2, [L + 2], F32, "zin")
        zo_r = Ring(ar, 2, [L], F32, "zo")
        for j in range(2):
            k.memset("pool", zin_r.aps[j][:, 0:1], 0.0, (zin_r.bufs[j],))
            k.memset("pool", zin_r.aps[j][:, L + 1:L + 2], 0.0, (zin_r.bufs[j],))
        for c in range(12):
            zi, zig = zin_r.get()
            k.dma("sp", zi[:, 1:L + 1], ZA[:, c, pos0:pos0 + L], (gZA,), (zig,))
            zo, zog = zo_r.get()
            k.act(zo, zi[:, 1:L + 1], AF.Identity, (zig, g_cw), (zog,), bias=cb_[:, c:c + 1], scale=cw_[:, c, 1:2])
            k.stt("dve", zo, zi[:, 0:L], cw_[:, c, 0:1], zo, ALU.mult, ALU.add, (zig, g_cw, zog), (zog,))
            k.stt("dve", zo, zi[:, 2:L + 2], cw_[:, c, 2:3], zo, ALU.mult, ALU.add, (zig, g_cw, zog), (zog,))
            if c < 4:
                for tc in range(ntc):
                    pt, pg = ps()
                    k.tr(pt[:, 0:128], zo[:, tc * 128:(tc + 1) * 128], ident, (zog, g_ident), (pg,))
                    k.act(vT[:, tc, c * 128:(c + 1) * 128], pt[:, 0:128], AF.Copy, (pg,), (g_vT,))
            elif c < 8:
                k.dma("sp", G1[:, c - 4, pos0:pos0 + L], zo, (zog,), (gG1,))
            else:
                k.dma("sp", G2[:, c - 8, pos0:pos0 + L], zo, (zog,), (gG2,))
        k.barrier()
        ar.top = mark2
        Ysp = ar.alloc([nfc, 2, 512], BF16)
        g_Y = Buf("Ysp")
        kf_r = Ring(ar, 2, [2, 512], F32, "kft")
        t_r = Ring(ar, 2, [4, 512], F32, "ytmp")
        g_r = Ring(ar, 3, [tw], F32, "gt")
        y_r = Ring(ar, 3, [tw], F32, "y1")
        yo_r = Ring(ar, 3, [tw], BF16, "yo")
        tbf = fwd_ring(L, 3)
        tbi = inv_ring(L)
        for o in range(2):
            def epi_f(fc, pc, gc, pS, gS, o=o):
                kt, kg = kf_r.get()
                k.dma("sp", kt, KF[li, o, fc], (gKF,), (kg,))
                tm, tmg = t_r.get()
                k.tt("dve", tm[:, 0, :], pc, kt[:, 0, :], ALU.mult, (gc, kg), (tmg,))
                k.tt("dve", tm[:, 1, :], pS, kt[:, 1, :], ALU.mult, (gS, kg), (tmg,))
                k.tt("dve", tm[:, 2, :], pS, kt[:, 0, :], ALU.mult, (gS, kg), (tmg,))
                k.tt("dve", tm[:, 3, :], pc, kt[:, 1, :], ALU.mult, (gc, kg), (tmg,))
                k.tt("pool", Ysp[:, fc, 0, :], tm[:, 0, :], tm[:, 1, :], ALU.add, (tmg,), (g_Y,))
                k.tt("pool", Ysp[:, fc, 1, :], tm[:, 2, :], tm[:, 3, :], ALU.subtract, (tmg,), (g_Y,))
            dft_forward(L, vT, vT, g_vT, epi_f, tbf)

            def epi_i(cc, tg, pt, pg, o=o):
                gt, gg = g_r.get()
                gsrc = G1 if o == 0 else G2
                k.dma("sp", gt, gsrc[:, cc, pos0 + tg * tw:pos0 + (tg + 1) * tw], (gG1 if o == 0 else gG2,), (gg,))
                if o == 0:
                    y1, y1g = y_r.get()
                    k.tt("dve", y1, pt[:, 0:tw], gt, ALU.mult, (pg, gg), (y1g,))
                    for s in range(tw // 128):
                        p2, g2 = ps()
                        k.tr(p2[:, 0:128], y1[:, s * 128:(s + 1) * 128], ident, (y1g, g_ident), (g2,))
                        k.act(vT[:, tg * (tw // 128) + s, cc * 128:(cc + 1) * 128], p2[:, 0:128], AF.Copy, (g2,), (g_vT,))
                else:
                    yo, yog = yo_r.get()
                    k.tt("dve", yo, pt[:, 0:tw], gt, ALU.mult, (pg, gg), (yog,))
                    k.dma("sp", YAd[:, cc, pos0 + tg * tw:pos0 + (tg + 1) * tw], yo, (yog,), (gYA,))
            dft_inverse(L, Ysp, g_Y, epi_i, tbi)
        k.barrier()
        ar.top = mark

    def odd_mixer(l):
        i = l // 2
        run_ctx = l < 3
        perm = (i % 2 == 1)
        dst = "XP" if perm else "X"
        mark = ar.top
        wint = WT(ar.alloc([KD, 2 * DR], BF16), KD, 2 * DR)
        win = wint.ap
        load_cast(wint, wview(I["w_in_odd"][i]))
        nr = NormRes()
        xr = Ring(ar, 5, [D], F32, "xo1")
        hxr = Ring(ar, 2, [KD, 512], BF16, "hxo")
        gtr = Ring(ar, 1, [NH, 512], BF16, "gto", parts=HD)
        zrr = Ring(ar, 1, [NH, 512], F32, "zro", parts=HD)
        for (t0, n) in tile_list(512, True):
            w = 1 if t0 >= T else 0
            xs = []
            for j in range(n // 128):
                xa, xg = xr.get()
                load_x("act", xa, xg, "X", perm, t0 + j * 128)
                xs.append((xa, xg))
            hxT, hxg = hxr.get()
            norm_tile(nr, xs, w, 0, hxT, hxg)
            gt, gtg = gtr.get()
            zr, zrg = zrr.get()
            for h in range(NH):
                for part in range(2):
                    c0 = part * DR + h * HD
                    pt, pg = ps()
                    for kk in range(KD):
                        k.mm(pt[0:HD, 0:n], win[:, kk, c0:c0 + HD], hxT[:, kk, 0:n], kk == 0, kk == KD - 1,
                             (hxg,) + wint.gs(kk, c0, c0 + HD), (pg,))
                    if part == 0:
                        k.act(gt[:, h, 0:n], pt[0:HD, 0:n], AF.Gelu_apprx_tanh, (pg,), (gtg,))
                    else:
                        k.copy("dve", zr[:, h, 0:n], pt[0:HD, 0:n], (pg,), (zrg,))
            k.dma("act", GTd[:, :, t0:t0 + n], gt[:, :, 0:n], (gtg,), (gGT,))
            k.dma("sp", ZRd[:, :, t0:t0 + n], zr[:, :, 0:n], (zrg,), (gZR,))
        k.barrier()
        ar.top = mark
        wa = ar.alloc([2, 2, NH, HD], BF16, parts=HD)
        g_wa = Buf("wa")
        for d in range(2):
            for ax, nm in enumerate(("rg_wa", "rg_wx")):
                sv = I[nm][i, d].rearrange("h d e -> d h e")
                for h0 in range(0, NH, 4):
                    st_, sg_ = stage.get()
                    sview = st_[0:HD, 0:4 * HD].rearrange("p (h e) -> p h e", h=4)
                    k.dma("sp", sview, sv[:, h0:h0 + 4, :], (), (sg_,))
                    k.copy("pool", wa[:, d, ax, h0:h0 + 4, :], sview, (sg_,), (g_wa,))
        vecs = ar.alloc([2, 3, NH], F32, parts=HD)
        csp = ar.alloc([2, NH], F32, parts=HD)
        cwv = ar.alloc([NH, 4], F32, parts=HD)
        cbv = ar.alloc([NH], F32, parts=HD)
        g_v = Buf("ovec")
        for d in range(2):
            k.dma("sp", vecs[:, d, 0, :], I["rg_baT"][i, d], (), (g_v,))
            k.dma("sp", vecs[:, d, 1, :], I["rg_bxT"][i, d], (), (g_v,))
            k.dma("sp", vecs[:, d, 2, :], I["rg_lamT"][i, d], (), (g_v,))
        k.dma("sp", cwv, I["rg_cwT"][i], (), (g_v,))
        k.dma("sp", cbv, I["rg_cbT"][i], (), (g_v,))
        for d in range(2):
            k.act(csp[:, d, :], vecs[:, d, 2, :], AF.Exp, (g_v,), (g_v,), scale=-1.0)
            k.act(csp[:, d, :], csp[:, d, :], AF.Ln, (g_v, g_cst), (g_v,), bias=cst[0:HD, 1:2], scale=1.0)
            k.ts("dve", csp[:, d, :], csp[:, d, :], -8.0, None, ALU.mult, None, (g_v,), (g_v,))
        LL = (TC, T)
        zl_r = Ring(ar, 2, [T + 3], F32, "zl", parts=HD)
        zc_r = Ring(ar, 2, [TC + 3], F32, "zc", parts=HD)
        for rr, Lx in ((zl_r, T), (zc_r, TC)):
            for j in range(2):
                k.memset("pool", rr.aps[j][:, 0:2], 0.0, (rr.bufs[j],))
                k.memset("pool", rr.aps[j][:, Lx + 2:Lx + 3], 0.0, (rr.bufs[j],))
        xl_r = Ring(ar, 2, [TA], F32, "xl", parts=HD)
        xb_r = Ring(ar, 2, [TA], BF16, "xb", parts=HD)
        hs_r = Ring(ar, 2, [TA], F32, "hs", parts=HD)
        hb_r = Ring(ar, 2, [512], F32, "hb", parts=HD)
        gl_r = Ring(ar, 1, [TA], BF16, "gl", parts=HD)
        yo_r = Ring(ar, 1, [TA], BF16, "yro", parts=HD)
        e_r = [Ring(ar, 2, [512], F32, "e%d" % j, parts=HD) for j in range(5)]
        st_r = Ring(ar, 4, [2], F32, "sto", parts=HD)
        for h in range(NH):
            zl, zlg = zl_r.get()
            zc, zcg = zc_r.get()
            k.dma("sp", zl[:, 2:T + 2], ZRd[:, h, 0:T], (gZR,), (zlg,))
            k.dma("sp", zc[:, 2:TC + 2], ZRd[:, h, T:TA], (gZR,), (zcg,))
            xl, xlg = xl_r.get()
            for (zz, zzg, Lx, o0) in ((zl, zlg, T, 0), (zc, zcg, TC, T)):
                k.act(xl[:, o0:o0 + Lx], zz[:, 2:Lx + 2], AF.Identity, (zzg, g_v), (xlg,), bias=cbv[:, h:h + 1],
                      scale=cwv[:, h, 2:3])
                for (kk, off) in ((0, 0), (1, 1), (3, 3)):
                    k.stt("dve", xl[:, o0:o0 + Lx], zz[:, off:off + Lx], cwv[:, h, kk:kk + 1], xl[:, o0:o0 + Lx],
                          ALU.mult, ALU.add, (zzg, g_v, xlg), (xlg,))
            xb, xbg = xb_r.get()
            k.copy("pool", xb, xl, (xlg,), (xbg,))
            hs, hsg = hs_r.get()
            for d in range(2):
                segs = []
                for (o0, Lx) in ((T, TC), (0, T)):
                    cs = list(range(o0, o0 + Lx, 512))
                    if d == 1:
                        cs = cs[::-1]
                    segs.append([(c0, min(512, Lx)) for c0 in cs])
                prev = None
                for si, seg in enumerate(segs):
                    for (c0, n) in seg:
                        pa, pag = ps()
                        k.mm(pa[0:HD, 0:n], wa[:, d, 0, h, :], xb[:, c0:c0 + n], True, True, (g_wa, xbg), (pag,))
                        px, pxg = ps()
                        k.mm(px[0:HD, 0:n], wa[:, d, 1, h, :], xb[:, c0:c0 + n], True, True, (g_wa, xbg), (pxg,))
                        r_, rg = e_r[0].get()
                        k.act(r_[:, 0:n], pa[0:HD, 0:n], AF.Sigmoid, (pag, g_v), (rg,), bias=vecs[:, d, 0, h:h + 1])
                        gi, gig = e_r[1].get()
                        k.act(gi[:, 0:n], px[0:HD, 0:n], AF.Sigmoid, (pxg, g_v), (gig,), bias=vecs[:, d, 1, h:h + 1])
                        a_, ag = e_r[2].get()
                        k.act(a_[:, 0:n], r_[:, 0:n], AF.Exp, (rg, g_v), (ag,), scale=csp[:, d, h:h + 1])
                        s_, sg_ = e_r[3].get()
                        k.tt("pool", s_[:, 0:n], a_[:, 0:n], a_[:, 0:n], ALU.mult, (ag,), (sg_,))
                        k.act(s_[:, 0:n], s_[:, 0:n], AF.Sqrt, (sg_, g_cst), (sg_,), bias=cst[0:HD, 1:2], scale=-1.0)
                        bx, bxg = e_r[4].get()
                        k.tt("pool", bx[:, 0:n], gi[:, 0:n], xl[:, c0:c0 + n], ALU.mult, (gig, xlg), (bxg,))
                        k.tt("dve", bx[:, 0:n], bx[:, 0:n], s_[:, 0:n], ALU.mult, (bxg, sg_), (bxg,))
                        if d == 0:
                            dst_ap, dst_g = hs[:, c0:c0 + n], hsg
                        else:
                            hbt, hbg = hb_r.get()
                            dst_ap, dst_g = hbt[:, 0:n], hbg
                        init = 0.0 if prev is None else prev[0]
                        rd = (ag, bxg) + (() if prev is None else (prev[1],))
                        if d == 0:
                            k.op("dve", lambda eh, o=dst_ap, a=a_[:, 0:n], b=bx[:, 0:n], ini=init:
                                 eh.tensor_tensor_scan(out=o, data0=a, data1=b, initial=ini, op0=ALU.mult, op1=ALU.add),
                                 rd, (dst_g,))
                            prev = (hs[:, c0 + n - 1:c0 + n], hsg)
                        else:
                            k.op("dve", lambda eh, o=dst_ap, a=a_[:, 0:n], b=bx[:, 0:n], ini=init:
                                 eh.tensor_tensor_scan(out=o[:, ::-1], data0=a[:, ::-1], data1=b[:, ::-1], initial=ini,
                                                       op0=ALU.mult, op1=ALU.add), rd, (dst_g,))
                            st_, stg = st_r.get()
                            k.copy("dve", st_[:, 0:1], dst_ap[:, 0:1], (dst_g,), (stg,))
                            prev = (st_[:, 0:1], stg)
                            k.tt("pool", hs[:, c0:c0 + n], hs[:, c0:c0 + n], dst_ap, ALU.add, (hsg, dst_g), (hsg,))
            gl, glg = gl_r.get()
            k.dma("sp", gl, GTd[:, h, :], (gGT,), (glg,))
            yo, yog = yo_r.get()
            k.tt("pool", yo, hs, gl, ALU.mult, (hsg, glg), (yog,))
            k.dma("sp", YRd[:, h, :], yo, (yog,), (gYR,))
        k.barrier()
        ar.top = mark
        wot = WT(ar.alloc([NH, D], BF16, parts=HD), NH, D)
        load_cast(wot, I["w_out_odd"][i].rearrange("(h e) n -> e h n", e=HD), parts=HD)
        yr_ = Ring(ar, 2, [NH, 512], BF16, "yro3", parts=HD)

        def yload(t0, n):
            ya, yg = yr_.get()
            k.dma("sp", ya[:, :, 0:n], YRd[:, :, t0:t0 + n], (gYR,), (yg,))
            return ya, yg
        outproj_resid(l, yload, [HD] * NH, wot, "X", perm, dst, run_ctx)
        k.barrier()
        ar.top = mark

    def final_norm(src, perm):
        mark = ar.top
        gfb = ar.alloc([D], F32)
        g_gf = Buf("gf")
        k.dma("sp", gfb, I["gfin"].partition_broadcast(128), (), (g_gf,))
        nr = NormRes()
        xr = Ring(ar, 4, [D], F32, "xfn")
        orr = Ring(ar, 3, [D], F32, "ofn")
        for pos0 in range(0, T, 128):
            xa, xg = xr.get()
            load_x("act", xa, xg, src, False, pos0)
            ss, ssg = nr.ss.get()
            k.memset("pool", ss[:, 0:1], 0.0, (ssg,))
            k.act(nr.junk, xa, AF.Square, (xg, ssg), (nr.g_junk, ssg), accum_out=ss[:, 0:1])
            k.act(ss[:, 1:2], ss[:, 0:1], AF.Sqrt, (ssg, g_cst), (ssg,), bias=cst[:, 0:1], scale=1.0 / D)
            k.op("dve", lambda eh, ss=ss: eh.reciprocal(out=ss[:, 2:3], in_=ss[:, 1:2]), (ssg,), (ssg,))
            oa, og = orr.get()
            k.stt("dve", oa, xa, ss[:, 2:3], gfb, ALU.mult, ALU.mult, (xg, ssg, g_gf), (og,))
            for (p0, p1, rows) in xrows(out_d, perm, pos0):
                k.dma("sp", rows, oa[p0:p1, :], (og,), ())
        k.barrier()
        ar.top = mark

    for r0 in range(0, T, 512):
        k.dma("sp", X[r0:r0 + 512, :], I["x"][r0:r0 + 512, :], (), tuple(gX[r0 // 128:r0 // 128 + 4]))
    k.dma("sp", X[T:TA, :], I["ctx"], (), tuple(gX[T // 128:]))
    k.barrier()
    cur = "X"
    for l in range(nl):
        if go('adaln'):
            adaln(l)
        if l % 2 == 0:
            even_mixer(l)
        else:
            odd_mixer(l)
            if (l // 2) % 2 == 1:
                cur = "XP"
        if go('ffn'):
            ffn(l, cur, l < 3)
    if go('final'):
        final_norm(cur, cur == "XP")
    if dbg:
        k.dma("sp", dbg_d[0:T, :], (XP if cur == "XP" else X)[0:T, :], (), ())
        k.dma("sp", dbg_d[T:TA, :], X[T:TA, :], (), ())
    k.finish()
    return nc, k


def _tbl(L):
    N = 2 * L
    nf = (L + 1 + 127) // 128 * 128
    idx = np.arange(nf, dtype=np.int64)
    prod = (idx[:, None] * idx[None, :]) % N
    ang = prod.astype(np.float64) * (2.0 * np.pi / N)
    valid = (idx <= L)
    m = (valid[:, None] & valid[None, :])
    Gc = np.where(m, np.cos(ang), 0.0)
    Gs = np.where(m, np.sin(ang), 0.0)
    nb = nf // 128

    def colblk(G):
        return np.ascontiguousarray(G.reshape(nb, 128, nb, 128).transpose(2, 1, 0, 3)).astype(ml_dtypes.bfloat16)
    wN = np.where(valid, 2.0, 0.0)
    wN[0] = 1.0
    wN[L] = 1.0
    wN = (wN / N).astype(np.float32)
    wNT = np.ascontiguousarray(wN.reshape(nb, 128).T)
    return (colblk(Gc), colblk(Gs), Gc.astype(ml_dtypes.bfloat16), Gs.astype(ml_dtypes.bfloat16), wNT)


def _emb(L):
    t = np.linspace(0.0, 1.0, L, dtype=np.float32)[:, None]
    w = (2.0 * np.float32(math.pi) * np.arange(L, dtype=np.float32)[:, None] / np.float32(L)).astype(np.float32)
    f = np.linspace(1e-4, 15, 16, dtype=np.float32)[None, :]
    emb = np.concatenate([t, np.cos(f * w), -np.sin(f * w)], axis=-1).astype(np.float32)
    tcol = np.ascontiguousarray(t[:, 0].reshape(L // 128, 128).T)
    return np.ascontiguousarray(emb.T), tcol


_CONST = {}


def _consts():
    if _CONST:
        return _CONST
    c = {}
    c["ident"] = np.eye(128, dtype=np.float32)
    c["TBLc"], c["TBLs"], c["GRc"], c["GRs"], c["wN"] = _tbl(T)
    c["TBLc_c"], c["TBLs_c"], c["GRc_c"], c["GRs_c"], c["wN_c"] = _tbl(TC)
    c["embT"], c["tcol"] = _emb(T)
    c["embT_c"], c["tcol_c"] = _emb(TC)
    c["deltas"] = np.abs(np.linspace(math.log(1e-2) / 1.5, math.log(1e-2) / 0.3, 512, dtype=np.float32)).astype(np.float32)
    _CONST.update(c)
    return _CONST


BF = ml_dtypes.bfloat16
INPUT_SPECS = [
    ("x", (T, D), F32), ("ctx", (TC, D), F32), ("ccT", (128, KD, 2), F32),
    ("w_ada", (4, D, 6 * D), F32), ("b_adaT", (4, 128, 48), F32), ("b_ada", (4, 6 * D), F32),
    ("gmT", (4, 128, KD), F32), ("gfT", (4, 128, KD), F32), ("gfin", (D,), F32),
    ("w_in_even", (2, D, 2560), F32), ("w_out_even", (2, D, D), F32),
    ("hy_cwT", (2, 128, 12, 3), F32), ("hy_cbT", (2, 128, 12), F32),
    ("f1w", (2, 33, 64), F32), ("f1bT", (2, 64, 1), F32), ("sfT", (2, 64, 1), F32),
    ("f2w", (2, 64, 64), F32), ("f2bT", (2, 64, 1), F32), ("f3wa", (2, 65, 2048), F32),
    ("hy_skip", (2, 2, 512), F32), ("sgu_lng", (2, 512), F32), ("sgu_wT", (2, 4, 128, 128), F32),
    ("sgu_b", (2, 4, 128), F32),
    ("w_in_odd", (2, D, 2 * DR), F32), ("rg_cwT", (2, HD, NH, 4), F32), ("rg_cbT", (2, HD, NH), F32),
    ("rg_wa", (2, 2, NH, HD, HD), F32), ("rg_wx", (2, 2, NH, HD, HD), F32),
    ("rg_baT", (2, 2, HD, NH), F32), ("rg_bxT", (2, 2, HD, NH), F32), ("rg_lamT", (2, 2, HD, NH), F32),
    ("w_out_odd", (2, DR, D), F32),
    ("w_ffn_in", (4, D, 2 * DFF), F32), ("w_ffn_out", (4, DFF, D), F32),
    ("ident", (128, 128), F32),
    ("TBLc", (NF, 128, NF, 128), BF16), ("TBLs", (NF, 128, NF, 128), BF16),
    ("GRc", (NF * 128, NF * 128), BF16), ("GRs", (NF * 128, NF * 128), BF16), ("wN", (128, NF), F32),
    ("TBLc_c", (NFC, 128, NFC, 128), BF16), ("TBLs_c", (NFC, 128, NFC, 128), BF16),
    ("GRc_c", (NFC * 128, NFC * 128), BF16), ("GRs_c", (NFC * 128, NFC * 128), BF16), ("wN_c", (128, NFC), F32),
    ("embT", (33, T), F32), ("tcol", (128, T // 128), F32),
    ("embT_c", (33, TC), F32), ("tcol_c", (128, TC // 128), F32),
    ("deltas", (512,), F32),
]


def make_in_maps(inp, cores):
    c = _consts()
    f = lambda a: np.ascontiguousarray(np.asarray(a, dtype=np.float32))
    sh = {}
    sh["w_ada"] = f(inp["w_ada"])
    sh["b_adaT"] = f(np.asarray(inp["b_ada"]).reshape(4, 48, 128).transpose(0, 2, 1))
    sh["b_ada"] = f(inp["b_ada"])
    sh["gmT"] = f(np.asarray(inp["norm_mix_g"]).reshape(4, KD, 128).transpose(0, 2, 1))
    sh["gfT"] = f(np.asarray(inp["norm_ffn_g"]).reshape(4, KD, 128).transpose(0, 2, 1))
    sh["gfin"] = f(inp["final_norm_g"])
    sh["w_in_even"] = f(inp["w_in_even"])
    sh["w_out_even"] = f(inp["w_out_even"])
    sh["hy_cwT"] = f(np.asarray(inp["hy_conv_w"]).reshape(2, 3, 12, 128).transpose(0, 3, 2, 1))
    sh["hy_cbT"] = f(np.asarray(inp["hy_conv_b"]).reshape(2, 12, 128).transpose(0, 2, 1))
    sh["f1w"] = f(inp["hy_f1_w"])
    sh["f1bT"] = f(np.asarray(inp["hy_f1_b"]).reshape(2, 64, 1))
    sh["sfT"] = f(np.asarray(inp["hy_sin_freq"]).reshape(2, 64, 1))
    sh["f2w"] = f(inp["hy_f2_w"])
    sh["f2bT"] = f(np.asarray(inp["hy_f2_b"]).reshape(2, 64, 1))
    sh["f3wa"] = f(np.concatenate([np.asarray(inp["hy_f3_w"]), np.asarray(inp["hy_f3_b"])[:, None, :]], axis=1))
    sh["hy_skip"] = f(inp["hy_skip"])
    sh["sgu_lng"] = f(inp["sgu_ln_g"])
    sh["sgu_wT"] = f(np.asarray(inp["sgu_w"]).transpose(0, 1, 3, 2))
    sh["sgu_b"] = f(inp["sgu_b"])
    sh["w_in_odd"] = f(inp["w_in_odd"])
    sh["rg_cwT"] = f(np.asarray(inp["rg_conv_w"]).reshape(2, 4, NH, HD).transpose(0, 3, 2, 1))
    sh["rg_cbT"] = f(np.asarray(inp["rg_conv_b"]).reshape(2, NH, HD).transpose(0, 2, 1))
    sh["rg_wa"] = f(inp["rg_wa"])
    sh["rg_wx"] = f(inp["rg_wx"])
    sh["rg_baT"] = f(np.asarray(inp["rg_ba"]).reshape(2, 2, NH, HD).transpose(0, 1, 3, 2))
    sh["rg_bxT"] = f(np.asarray(inp["rg_bx"]).reshape(2, 2, NH, HD).transpose(0, 1, 3, 2))
    sh["rg_lamT"] = f(np.asarray(inp["rg_lam"]).reshape(2, 2, NH, HD).transpose(0, 1, 3, 2))
    sh["w_out_odd"] = f(inp["w_out_odd"])
    sh["w_ffn_in"] = f(inp["w_ffn_in"])
    sh["w_ffn_out"] = f(inp["w_ffn_out"])
    for kname in ("ident", "TBLc", "TBLs", "GRc", "GRs", "wN", "TBLc_c", "TBLs_c", "GRc_c", "GRs_c", "wN_c",
                  "embT", "tcol", "embT_c", "tcol_c", "deltas"):
        sh[kname] = c[kname]
    maps = []
    x = np.asarray(inp["x"], dtype=np.float32)
    ctx = np.asarray(inp["ctx"], dtype=np.float32)
    cvec = np.asarray(inp["c"], dtype=np.float32)
    ccx = np.asarray(inp["c_ctx"], dtype=np.float32)
    for b in cores:
        m = dict(sh)
        m["x"] = np.ascontiguousarray(x[b])
        m["ctx"] = np.ascontiguousarray(ctx[b])
        cc = np.stack([cvec[b], ccx], axis=-1)
        m["ccT"] = np.ascontiguousarray(cc.reshape(KD, 128, 2).transpose(1, 0, 2))
        maps.append(m)
    return maps


def kernel(**inputs):
    nc, _ = build(4, False)
    maps = make_in_maps(inputs, range(4))
    res = run_bass_kernel_spmd(nc, maps, core_ids=list(range(4)))
    out = np.stack([np.asarray(r["out"], dtype=np.float32) for r in res.results], axis=0)
    return out
```

```python
import contextlib
import math
import numpy as np
import ml_dtypes
import concourse.bass as bass
import concourse.mybir as mybir
from concourse.bass_utils import run_bass_kernel_spmd

F32 = mybir.dt.float32
BF16 = mybir.dt.bfloat16
AF = mybir.ActivationFunctionType
ALU = mybir.AluOpType
AX = mybir.AxisListType

D = 1024
KD = 8
T = 4096
TC = 256
TA = T + TC
DFF = 2816
DR = 1408
NH = 16
HD = 88
EPS = 1e-6
NF = 33
NFC = 3


class Buf:
    __slots__ = ("w", "r", "name")

    def __init__(self, name=""):
        self.w = []
        self.r = []
        self.name = name


class KB:
    def __init__(self):
        self.nc = bass.Bass("TRN2", target_bir_lowering=False)
        self.st = contextlib.ExitStack()
        self.E = ["pe", "act", "dve", "pool", "sp"]
        self.sem = {e: self.st.enter_context(self.nc.semaphore("sem_" + e)) for e in self.E}
        self.cnt = {e: 0 for e in self.E}
        self.ops = {e: [] for e in self.E}
        self.waited = {e: {} for e in self.E}
        self.dq = {}
        self.dqi = {}
        self.dqlast = {}
        for q, n in (("sp", 16), ("pool", 16), ("act", 8)):
            self.dq[q] = [self.st.enter_context(self.nc.semaphore("dq_%s%d" % (q, i))) for i in range(n)]
            self.dqi[q] = 0
        self.nops = 0
        self.dummy = self.st.enter_context(self.nc.sbuf_tensor("kbdummy", [128, 8], F32))
        self.op("dve", lambda eh: eh.memset(self.dummy[:, 0:8], 0.0), (), ())

    def _collect(self, e, reads, writes):
        deps = {}

        def add(lst):
            for (sm, v) in lst:
                if deps.get(sm, 0) < v:
                    deps[sm] = v
        for b in reads:
            add(b.w)
        for b in writes:
            add(b.w)
            add(b.r)
        res = []
        for sm, v in deps.items():
            if e == "pe" and sm == self.sem["pe"]:
                continue
            if self.waited[e].get(sm, 0) >= v:
                continue
            self.waited[e][sm] = v
            res.append((sm, v))
        return res

    def _update(self, dep, reads, writes):
        for b in writes:
            b.w = [dep]
            b.r = []
        for b in reads:
            if b in writes:
                continue
            b.r = [d for d in b.r if d[0] != dep[0]] + [dep]

    def op(self, e, fn, reads=(), writes=()):
        waits = self._collect(e, reads, writes)
        self.cnt[e] += 1
        dep = (self.sem[e], self.cnt[e])
        self.ops[e].append((waits, fn, (self.sem[e], 1)))
        self._update(dep, reads, writes)
        self.nops += 1

    def dma(self, q, out, in_, reads=(), writes=()):
        waits = self._collect(q, reads, writes)
        i = self.dqi[q]
        self.dqi[q] += 1
        n = len(self.dq[q])
        slot = self.dq[q][i % n]
        tgt = 16 * (i // n + 1)
        if i >= n:
            prev = 16 * (i // n)
            if self.waited[q].get(slot, 0) < prev:
                self.waited[q][slot] = prev
                waits.append((slot, prev))
        self.ops[q].append((waits, lambda eh: eh.dma_start(out=out, in_=in_), (slot, 16)))
        self.dqlast[slot] = tgt
        self._update((slot, tgt), reads, writes)
        self.nops += 1

    def barrier(self):
        self._barrier1()
        d = self.dummy
        self.op("act", lambda eh: eh.activation(out=d[:, 1:2], in_=d[:, 0:1], func=AF.Copy), (), ())
        self.op("dve", lambda eh: eh.memset(d[:, 2:3], 0.0), (), ())
        self.op("pool", lambda eh: eh.memset(d[:, 3:4], 0.0), (), ())
        self._barrier1()

    def _barrier1(self):
        deps = [(self.sem[e], self.cnt[e]) for e in self.E if self.cnt[e] > 0]
        deps += list(self.dqlast.items())
        for e in self.E:
            waits = []
            for sm, v in deps:
                if sm == self.sem[e]:
                    continue
                if self.waited[e].get(sm, 0) >= v:
                    continue
                self.waited[e][sm] = v
                waits.append((sm, v))
            if waits:
                self.ops[e].append((waits, None, None))

    def finish(self):
        self.barrier()
        with self.nc.Block() as block:
            def emit(e):
                def f(eh):
                    for waits, fn, inc in self.ops[e]:
                        for sm, v in waits:
                            eh.wait_ge(sm, v)
                        if fn is not None:
                            fn(eh).then_inc(*inc)
                return f
            block.tensor(emit("pe"))
            block.scalar(emit("act"))
            block.vector(emit("dve"))
            block.gpsimd(emit("pool"))
            block.sync(emit("sp"))
        self.st.close()

    def mm(self, out, lhsT, rhs, start, stop, reads, writes):
        self.op("pe", lambda eh: eh.matmul(out, lhsT, rhs, start=start, stop=stop), reads, writes)

    def tr(self, out, in_, ident, reads, writes):
        self.op("pe", lambda eh: eh.transpose(out, in_, ident), reads, writes)

    def act(self, out, in_, func, reads, writes, bias=None, scale=None, accum_out=None):
        kw = {}
        if bias is not None:
            kw["bias"] = bias
        if scale is not None:
            kw["scale"] = scale
        if accum_out is not None:
            kw["accum_out"] = accum_out
        self.op("act", lambda eh: eh.activation(out=out, in_=in_, func=func, **kw), reads, writes)

    def tt(self, e, out, in0, in1, op, reads, writes):
        self.op(e, lambda eh: eh.tensor_tensor(out=out, in0=in0, in1=in1, op=op), reads, writes)

    def ts(self, e, out, in0, s1, s2, op0, op1, reads, writes):
        if s2 is None:
            self.op(e, lambda eh: eh.tensor_scalar(out=out, in0=in0, scalar1=s1, scalar2=None, op0=op0), reads, writes)
        else:
            self.op(e, lambda eh: eh.tensor_scalar(out=out, in0=in0, scalar1=s1, scalar2=s2, op0=op0, op1=op1), reads, writes)

    def stt(self, e, out, in0, scalar, in1, op0, op1, reads, writes):
        self.op(e, lambda eh: eh.scalar_tensor_tensor(out=out, in0=in0, scalar=scalar, in1=in1, op0=op0, op1=op1), reads, writes)

    def copy(self, e, out, in_, reads, writes):
        self.op(e, lambda eh: eh.tensor_copy(out=out, in_=in_), reads, writes)

    def memset(self, e, ap, val, writes):
        self.op(e, lambda eh: eh.memset(ap, val), (), writes)


class Arena:
    def __init__(self, k, nbytes):
        self.k = k
        self.cap = nbytes
        self.t = k.st.enter_context(k.nc.sbuf_tensor("arena", [128, nbytes // 4], F32))
        self.top = 0

    def alloc(self, free_shape, dtype, parts=128):
        es = 4 if dtype == F32 else 2
        n = int(np.prod(free_shape))
        nb = (n * es + 63) // 64 * 64
        off = self.top
        self.top += nb
        assert self.top <= self.cap, "SBUF arena overflow %d > %d" % (self.top, self.cap)
        assert (n * es) % 4 == 0
        ap = self.t[0:parts, off // 4: off // 4 + (n * es) // 4]
        if dtype == BF16:
            ap = ap.bitcast(BF16)
        fs = list(free_shape)
        if len(fs) == 2:
            ap = ap.rearrange("p (a b) -> p a b", a=fs[0])
        elif len(fs) == 3:
            ap = ap.rearrange("p (a b c) -> p a b c", a=fs[0], b=fs[1])
        elif len(fs) == 4:
            ap = ap.rearrange("p (a b c d) -> p a b c d", a=fs[0], b=fs[1], c=fs[2])
        return ap


class Ring:
    def __init__(self, arena, n, free_shape, dtype, name, parts=128):
        self.aps = [arena.alloc(free_shape, dtype, parts) for _ in range(n)]
        self.bufs = [Buf("%s%d" % (name, i)) for i in range(n)]
        self.i = 0

    def get(self):
        j = self.i % len(self.aps)
        self.i += 1
        return self.aps[j], self.bufs[j]


def build(nl=4, dbg=False, stop=10**9):
    k = KB()
    phc = [0]

    def go(name):
        phc[0] += 1
        import os
        skip = [int(v) for v in os.environ.get('SKIPPH', '').split(',') if v]
        ok = phc[0] <= stop and phc[0] not in skip
        if ok:
            print('phase', phc[0], name, flush=True)
        return ok
    nc = k.nc

    def din(name, shape, dt=F32):
        return nc.dram_tensor(name, list(shape), dt, kind="ExternalInput").ap()

    def dscr(name, shape, dt=F32):
        return nc.dram_tensor(name, list(shape), dt).ap()

    I = {}
    for name, shape, dt in INPUT_SPECS:
        I[name] = din(name, shape, dt)
    out_d = nc.dram_tensor("out", [T, D], F32, kind="ExternalOutput").ap()
    if dbg:
        dbg_d = nc.dram_tensor("dbg", [TA, D], F32, kind="ExternalOutput").ap()

    X = dscr("Xs", [TA, D])
    XP = dscr("XPs", [T, D])
    ZA = dscr("ZAs", [128, 12, TA])
    G1 = dscr("G1s", [128, 4, TA])
    G2 = dscr("G2s", [128, 4, TA])
    YAd = dscr("YAs", [128, 4, TA], BF16)
    YBd = dscr("YBs", [128, 4, TA], BF16)
    KF = dscr("KFs", [2, 2, NF, 128, 2, 512])
    ZRd = dscr("ZRs", [HD, NH, TA])
    GTd = dscr("GTs", [HD, NH, TA], BF16)
    YRd = dscr("YRs", [HD, NH, TA], BF16)
    gX = [Buf("X%d" % i) for i in range(TA // 128)]
    gXP = [Buf("XP%d" % i) for i in range(T // 128)]
    gZA, gG1, gG2, gYA, gYB, gKF, gZR, gGT, gYR = (Buf(n) for n in
                                                   ("ZA", "G1", "G2", "YA", "YB", "KF", "ZR", "GT", "YR"))

    ar = Arena(k, 212480)
    ps_t = [k.st.enter_context(nc.psum_tensor("ps%d" % i, [128, 512], F32)) for i in range(8)]
    ps_b = [Buf("ps%d" % i) for i in range(8)]
    ps_i = [0]

    ps_lim = [0, 8]

    def ps():
        lo, hi = ps_lim
        j = lo + ps_i[0] % (hi - lo)
        ps_i[0] += 1
        return ps_t[j][:, :], ps_b[j]

    ident = ar.alloc([128], F32)
    g_ident = Buf("ident")
    k.dma("sp", ident, I["ident"], (), (g_ident,))
    cst = ar.alloc([8], F32)
    g_cst = Buf("cst")
    k.memset("dve", cst[:, 0:1], EPS, (g_cst,))
    k.memset("dve", cst[:, 1:2], 1.0, (g_cst,))
    k.memset("dve", cst[:, 2:3], 0.0, (g_cst,))
    ones_f = ar.alloc([128], F32)
    g_ones = Buf("ones")
    k.memset("dve", ones_f, 1.0, (g_ones,))
    cc_f = ar.alloc([KD, 2], F32)
    scT = ar.alloc([KD, 2], BF16)
    screp = ar.alloc([KD, 2, 128], BF16)
    g_sc = Buf("sc")
    k.dma("sp", cc_f, I["ccT"], (), (g_sc,))
    sc_f = ar.alloc([KD, 2], F32)
    k.act(sc_f, cc_f, AF.Silu, (g_sc,), (g_sc,))
    k.copy("dve", scT, sc_f, (g_sc,), (g_sc,))
    for kk in range(KD):
        for w in range(2):
            k.act(screp[:, kk, w, :], ones_f, AF.Identity, (g_ones, g_sc), (g_sc,), scale=sc_f[:, kk, w:w + 1])
    modf = ar.alloc([2, 6, KD], F32)
    AB = ar.alloc([2, 4, KD], F32)
    gbc = ar.alloc([2, 2, D], F32)
    g_mod = Buf("mod")
    g_gbc = Buf("gbc")
    stage = Ring(ar, 4, [512], F32, "stage")
    persist_top = ar.top

    def xrows(buf_ap, perm, pos0):
        if pos0 >= T or not perm:
            return [(0, 128, buf_ap[pos0:pos0 + 128, :])]
        c0 = pos0 // 64
        v = buf_ap[0:T, :].rearrange("(r c) d -> c r d", c=64)
        return [(0, 64, v[c0]), (64, 128, v[c0 + 1])]

    def xguard(which, pos0):
        return (gXP if which == "XP" else gX)[pos0 // 128]

    def load_x(q, tile_ap, tile_g, src, perm, pos0):
        buf_ap = XP if src == "XP" else X
        for (p0, p1, rows) in xrows(buf_ap, perm, pos0):
            k.dma(q, tile_ap[p0:p1, :], rows, (xguard(src, pos0),) if not perm or pos0 >= T else tuple(gX[:T // 128]), (tile_g,))

    def store_x(q, tile_ap, tile_g, dst, pos0):
        buf_ap = XP if dst == "XP" else X
        k.dma(q, buf_ap[pos0:pos0 + 128, :], tile_ap, (tile_g,), (xguard(dst, pos0),))

    def tile_list(n, with_ctx=True):
        res = [(t0, n) for t0 in range(0, T, n)]
        if with_ctx:
            res += [(T + t0, min(n, TC)) for t0 in range(0, TC, n)]
        return res

    class WT:
        def __init__(self, ap, nk, ncol, step=512):
            self.ap = ap
            self.step = step
            self.nk = nk
            self.ncol = ncol
            self.g = [[Buf("w") for _ in range((ncol + step - 1) // step)] for _ in range(nk)]

        def gs(self, kk, c0, c1):
            return tuple(self.g[kk][j] for j in range(c0 // self.step, (c1 - 1) // self.step + 1))

    cast_rr = [0]

    def load_cast(wt, src_view, parts=128, order=None):
        ng = (wt.ncol + wt.step - 1) // wt.step
        if order is None:
            order = list(range(ng))
        for gi in order:
            c0 = gi * wt.step
            c1 = min(wt.ncol, c0 + wt.step)
            for kk in range(wt.nk):
                st, sg = stage.get()
                k.dma("sp", st[0:parts, 0:c1 - c0], src_view[:, kk, c0:c1], (), (sg,))
                e = ("pool", "dve", "act")[cast_rr[0] % 3]
                cast_rr[0] += 1
                if e == "act":
                    k.act(wt.ap[:, kk, c0:c1], st[0:parts, 0:c1 - c0], AF.Copy, (sg,), (wt.g[kk][gi],))
                else:
                    k.copy(e, wt.ap[:, kk, c0:c1], st[0:parts, 0:c1 - c0], (sg,), (wt.g[kk][gi],))

    def wview(src_ap, p=128):
        return src_ap.rearrange("(k p) n -> p k n", p=p)

    def adaln(l):
        mark = ar.top
        wts = [WT(ar.alloc([KD, D], BF16), KD, D) for _ in range(2)]
        bT = ar.alloc([48], F32)
        bbc = ar.alloc([2, D], F32)
        gm = ar.alloc([2, KD], F32)
        g_b = Buf("adab")
        k.dma("sp", bT, I["b_adaT"][l], (), (g_b,))
        k.dma("sp", gm[:, 0, :], I["gmT"][l], (), (g_b,))
        k.dma("sp", gm[:, 1, :], I["gfT"][l], (), (g_b,))
        for jj, j in enumerate((2, 5)):
            k.dma("sp", bbc[:, jj, :], I["b_ada"][l, j * D:(j + 1) * D].partition_broadcast(128), (), (g_b,))
        wv = I["w_ada"][l].rearrange("(k p) n -> p k n", p=128)
        for j in range(6):
            wtt = wts[j % 2]
            wt = wtt.ap
            load_cast(wtt, wv[:, :, j * D:(j + 1) * D])
            if j in (0, 1, 3, 4):
                pt, pg = ps()
                pv = pt[:, 0:16].rearrange("p (a b) -> p a b", a=KD)
                for oc in range(KD):
                    for kk in range(KD):
                        k.mm(pv[:, oc, :], wt[:, kk, oc * 128:(oc + 1) * 128], scT[:, kk, :], kk == 0, kk == KD - 1,
                             wtt.gs(kk, oc * 128, (oc + 1) * 128) + (g_sc,), (pg,))
                for w in range(2):
                    k.tt("dve", modf[:, w, j, :], pv[:, :, w], bT[:, j * KD:(j + 1) * KD], ALU.add,
                         (pg, g_b), (g_mod,))
            else:
                jj = 0 if j == 2 else 1
                for w in range(2):
                    for hf in range(2):
                        pt, pg = ps()
                        for kk in range(KD):
                            k.mm(pt, screp[:, kk, w, :], wt[:, kk, hf * 512:(hf + 1) * 512], kk == 0, kk == KD - 1,
                                 wtt.gs(kk, hf * 512, (hf + 1) * 512) + (g_sc,), (pg,))
                        k.tt("dve", gbc[:, w, jj, hf * 512:(hf + 1) * 512], pt, bbc[:, jj, hf * 512:(hf + 1) * 512],
                             ALU.add, (pg, g_b), (g_gbc,))
        for w in range(2):
            k.stt("dve", AB[:, w, 0, :], modf[:, w, 1, :], 1.0, gm[:, 0, :], ALU.add, ALU.mult, (g_mod, g_b), (g_mod,))
            k.copy("dve", AB[:, w, 1, :], modf[:, w, 0, :], (g_mod,), (g_mod,))
            k.stt("dve", AB[:, w, 2, :], modf[:, w, 4, :], 1.0, gm[:, 1, :], ALU.add, ALU.mult, (g_mod, g_b), (g_mod,))
            k.copy("dve", AB[:, w, 3, :], modf[:, w, 3, :], (g_mod,), (g_mod,))
        k.barrier()
        ar.top = mark

    class NormRes:
        def __init__(self, nxs=4):
            self.junk = ar.alloc([D], BF16)
            self.g_junk = Buf("junk")
            self.ss = Ring(ar, 8, [4], F32, "ss")
            self.xs = Ring(ar, nxs, [D], F32, "xs")

    def norm_sub(nr, x_ap, x_g, which, ab0, hxT, hx_g, col0):
        ss, ssg = nr.ss.get()
        k.memset("pool", ss[:, 0:1], 0.0, (ssg,))
        k.act(nr.junk, x_ap, AF.Square, (x_g, ssg), (nr.g_junk, ssg), accum_out=ss[:, 0:1])
        k.act(ss[:, 1:2], ss[:, 0:1], AF.Sqrt, (ssg, g_cst), (ssg,), bias=cst[:, 0:1], scale=1.0 / D)
        k.op("dve", lambda eh: eh.reciprocal(out=ss[:, 2:3], in_=ss[:, 1:2]), (ssg,), (ssg,))
        xs, xsg = nr.xs.get()
        k.ts("dve", xs, x_ap, ss[:, 2:3], None, ALU.mult, None, (x_g, ssg), (xsg,))
        return xs, xsg, ss, ssg

    def norm_tile(nr, xsubs, which, ab0, hxT, hx_g):
        ns = len(xsubs)
        normed = []
        for j, (xa, xg) in enumerate(xsubs):
            xs, xsg, _, _ = norm_sub(nr, xa, xg, which, ab0, hxT, hx_g, j * 128)
            normed.append((xs, xsg))
        for kk in range(KD):
            pt, pg = ps()
            for j, (xs, xsg) in enumerate(normed):
                k.tr(pt[:, j * 128:(j + 1) * 128], xs[:, kk * 128:(kk + 1) * 128], ident, (xsg, g_ident), (pg,))
            k.act(hxT[:, kk, 0:ns * 128], pt[:, 0:ns * 128], AF.Identity, (pg, g_mod), (hx_g,),
                  bias=AB[:, which, ab0 + 1, kk:kk + 1], scale=AB[:, which, ab0, kk:kk + 1])

    def outproj_resid(l, yload, kparts, wtt, src, perm, dst, with_ctx):
        xr = Ring(ar, 4, [D], F32, "xo")
        tmpr = Ring(ar, 2, [D], F32, "tmpo")
        nk = len(kparts)
        for (t0, n) in tile_list(512, with_ctx):
            w = 1 if t0 >= T else 0
            ya, yg = yload(t0, n)
            for j in range(n // 128):
                xa, xg = xr.get()
                load_x("act", xa, xg, src, perm, t0 + j * 128)
                tmp, tg = tmpr.get()
                for hf in range(2):
                    pt, pg = ps()
                    for kk in range(nk):
                        kp = kparts[kk]
                        k.mm(pt, ya[0:kp, kk, j * 128:(j + 1) * 128], wtt.ap[0:kp, kk, hf * 512:(hf + 1) * 512],
                             kk == 0, kk == nk - 1, (yg,) + wtt.gs(kk, hf * 512, (hf + 1) * 512), (pg,))
                    k.tt("dve", tmp[:, hf * 512:(hf + 1) * 512], pt, gbc[:, w, 0, hf * 512:(hf + 1) * 512], ALU.mult,
                         (pg, g_gbc), (tg,))
                k.tt("pool", xa, xa, tmp, ALU.add, (xg, tg), (xg,))
                store_x("sp", xa, xg, dst if t0 < T else "X", t0 + j * 128)

    def ffn(l, buf, with_ctx):
        mark = ar.top
        w1t = WT(ar.alloc([KD, 2 * DFF], BF16), KD, 2 * DFF)
        w2t = WT(ar.alloc([22, D], BF16), 22, D)
        w1, w2 = w1t.ap, w2t.ap
        load_cast(w1t, wview(I["w_ffn_in"][l]), order=[0, 5, 1, 6, 2, 7, 3, 8, 4, 9, 10])
        load_cast(w2t, wview(I["w_ffn_out"][l]))
        nr = NormRes(2)
        xr = Ring(ar, 3, [D], F32, "xf")
        hxr = Ring(ar, 2, [KD, 256], BF16, "hxf")
        hr = Ring(ar, 1, [22, 256], BF16, "hf")
        sgr = Ring(ar, 2, [256], F32, "sg")
        tmpr = Ring(ar, 1, [512], F32, "tmpf")
        for (t0, n) in tile_list(256, with_ctx):
            w = 1 if t0 >= T else 0
            src = buf if t0 < T else "X"
            xs = []
            for j in range(n // 128):
                xa, xg = xr.get()
                load_x("act", xa, xg, src, False, t0 + j * 128)
                xs.append((xa, xg))
            hxT, hxg = hxr.get()
            norm_tile(nr, xs, w, 2, hxT, hxg)
            hT, hg = hr.get()
            for c in range(22):
                p1, g1 = ps()
                p2, g2 = ps()
                for kk in range(KD):
                    k.mm(p1[:, 0:n], w1[:, kk, c * 128:(c + 1) * 128], hxT[:, kk, 0:n], kk == 0, kk == KD - 1,
                         (hxg,) + w1t.gs(kk, c * 128, (c + 1) * 128), (g1,))
                for kk in range(KD):
                    k.mm(p2[:, 0:n], w1[:, kk, DFF + c * 128:DFF + (c + 1) * 128], hxT[:, kk, 0:n], kk == 0,
                         kk == KD - 1, (hxg,) + w1t.gs(kk, DFF + c * 128, DFF + (c + 1) * 128), (g2,))
                sg, sgg = sgr.get()
                k.act(sg[:, 0:n], p1[:, 0:n], AF.Silu, (g1,), (sgg,))
                k.tt("dve", hT[:, c, 0:n], p2[:, 0:n], sg[:, 0:n], ALU.mult, (g2, sgg), (hg,))
            for j in range(n // 128):
                xa, xg = xs[j]
                for hf in range(2):
                    pt, pg = ps()
                    for c in range(22):
                        k.mm(pt, hT[:, c, j * 128:(j + 1) * 128], w2[:, c, hf * 512:(hf + 1) * 512], c == 0, c == 21,
                             (hg,) + w2t.gs(c, hf * 512, (hf + 1) * 512), (pg,))
                    tmp, tg = tmpr.get()
                    k.tt("dve", tmp, pt, gbc[:, w, 1, hf * 512:(hf + 1) * 512], ALU.mult, (pg, g_gbc), (tg,))
                    k.tt("pool", xa[:, hf * 512:(hf + 1) * 512], xa[:, hf * 512:(hf + 1) * 512], tmp, ALU.add,
                         (xg, tg), (xg,))
                store_x("sp", xa, xg, src, t0 + j * 128)
        k.barrier()
        ar.top = mark

    def fwd_ring(L, nbuf):
        return Ring(ar, nbuf, [L // 128, 128], BF16, "tbl%d" % L)

    def inv_ring(L):
        return Ring(ar, 6, [2, 512 if L == T else 256], BF16, "gr%d" % L)

    def dft_forward(L, Pin, Qin, g_in, epilogue, tb):
        ntc = L // 128
        nfc = NF if L == T else NFC
        tc_name, ts_name = ("TBLc", "TBLs") if L == T else ("TBLc_c", "TBLs_c")
        for fc in range(nfc):
            ct, cg = tb.get()
            k.dma("sp", ct, I[tc_name][fc][:, 0:ntc, :], (), (cg,))
            st_, sg_ = tb.get()
            k.dma("act" if L == T else "sp", st_, I[ts_name][fc][:, 0:ntc, :], (), (sg_,))
            pc, gc = ps()
            for tc in range(ntc):
                k.mm(pc, ct[:, tc, :], Pin[:, tc, :], tc == 0, tc == ntc - 1, (cg, g_in), (gc,))
            pS, gS = ps()
            for tc in range(ntc):
                k.mm(pS, st_[:, tc, :], Qin[:, tc, :], tc == 0, tc == ntc - 1, (sg_, g_in), (gS,))
            epilogue(fc, pc, gc, pS, gS)

    def dft_inverse(L, Ysp, g_y, epilogue, tb):
        nfc = NF if L == T else NFC
        tw = 512 if L == T else 256
        gc_name, gs_name = ("GRc", "GRs") if L == T else ("GRc_c", "GRs_c")
        for tg in range(L // tw):
            pss = [(ps_t[j][:, :], ps_b[j]) for j in range(4)]
            ps_lim[0] = 4
            for fc in range(nfc):
                tt_, tg_ = tb.get()
                k.dma("sp", tt_[:, 0, :], I[gc_name][fc * 128:(fc + 1) * 128, tg * tw:(tg + 1) * tw], (), (tg_,))
                k.dma("act", tt_[:, 1, :], I[gs_name][fc * 128:(fc + 1) * 128, tg * tw:(tg + 1) * tw], (), (tg_,))
                for cc in range(4):
                    pt, pg = pss[cc]
                    k.mm(pt[:, 0:tw], Ysp[:, fc, 0, cc * 128:(cc + 1) * 128], tt_[:, 0, :], fc == 0, False,
                         (g_y, tg_), (pg,))
                    k.mm(pt[:, 0:tw], Ysp[:, fc, 1, cc * 128:(cc + 1) * 128], tt_[:, 1, :], False, fc == nfc - 1,
                         (g_y, tg_), (pg,))
            for cc in range(4):
                epilogue(cc, tg, pss[cc][0], pss[cc][1])
        ps_lim[0] = 0

    def kf_gen(i, L, li):
        mark = ar.top
        ntc = L // 128
        nfc = NF if L == T else NFC
        sfx = "" if L == T else "_c"
        emb = ar.alloc([L], F32, parts=33)
        f1w = ar.alloc([64], F32, parts=33)
        f2w = ar.alloc([64], F32, parts=64)
        f3w = ar.alloc([2048], F32, parts=65)
        vec = ar.alloc([4], F32, parts=64)
        sfb = ar.alloc([2], F32, parts=64)
        h1 = ar.alloc([L], F32, parts=64)
        h2 = ar.alloc([L], F32, parts=65)
        tcol = ar.alloc([ntc], F32)
        dlt = ar.alloc([512], F32)
        wN = ar.alloc([nfc], F32)
        skb = ar.alloc([2, 512], F32)
        g_c = Buf("kfc")
        g_h1 = Buf("h1")
        g_h2 = Buf("h2")
        k.dma("sp", emb, I["embT" + sfx], (), (g_c,))
        k.dma("sp", f1w, I["f1w"][i], (), (g_c,))
        k.dma("sp", f2w, I["f2w"][i], (), (g_c,))
        k.dma("sp", f3w, I["f3wa"][i], (), (g_c,))
        k.dma("sp", vec[:, 0:1], I["f1bT"][i], (), (g_c,))
        k.dma("sp", vec[:, 1:2], I["f2bT"][i], (), (g_c,))
        k.dma("sp", vec[:, 2:3], I["sfT"][i], (), (g_c,))
        k.dma("sp", tcol, I["tcol" + sfx], (), (g_c,))
        k.dma("sp", dlt, I["deltas"].partition_broadcast(128), (), (g_c,))
        k.dma("sp", wN, I["wN" + sfx], (), (g_c,))
        for o in range(2):
            k.dma("sp", skb[:, o, :], I["hy_skip"][i, o].partition_broadcast(128), (), (g_c,))
        k.tt("dve", sfb[:, 0:1], vec[:, 0:1], vec[:, 2:3], ALU.mult, (g_c,), (g_c,))
        k.tt("dve", sfb[:, 1:2], vec[:, 1:2], vec[:, 2:3], ALU.mult, (g_c,), (g_c,))
        k.memset("dve", h2[64:65, :], 1.0, (g_h2,))
        argr = Ring(ar, 2, [512], F32, "arg", parts=64)
        wrr = Ring(ar, 2, [512], F32, "wrp", parts=64)
        cw = min(512, L)

        def sin_layer(src_lhsT, kp, rhs_full, g_rhs, bcol, dst, g_dst):
            for c0 in range(0, L, cw):
                pt, pg = ps()
                k.mm(pt[0:64, 0:cw], src_lhsT, rhs_full[0:kp, c0:c0 + cw], True, True, (g_c, g_rhs), (pg,))
                a, ag = argr.get()
                k.ts("dve", a[:, 0:cw], pt[0:64, 0:cw], vec[:, 2:3], sfb[:, bcol:bcol + 1], ALU.mult, ALU.add,
                     (pg, g_c), (ag,))
                w_, wg_ = wrr.get()
                k.ts("dve", w_[:, 0:cw], a[:, 0:cw], math.pi, -2.0 * math.pi, ALU.is_gt, ALU.mult, (ag,), (wg_,))
                k.tt("dve", a[:, 0:cw], a[:, 0:cw], w_[:, 0:cw], ALU.add, (ag, wg_), (ag,))
                k.ts("dve", w_[:, 0:cw], a[:, 0:cw], -math.pi, 2.0 * math.pi, ALU.is_lt, ALU.mult, (ag,), (wg_,))
                k.tt("dve", a[:, 0:cw], a[:, 0:cw], w_[:, 0:cw], ALU.add, (ag, wg_), (ag,))
                k.act(dst[0:64, c0:c0 + cw], a[:, 0:cw], AF.Sin, (ag,), (g_dst,))
        sin_layer(f1w[0:33, :], 33, emb, g_c, 0, h1, g_h1)
        sin_layer(f2w[0:64, :], 64, h1, g_h1, 1, h2, g_h2)

        PQ = ar.alloc([2, ntc, 512], BF16)
        g_pq = Buf("pq")
        acc = ar.alloc([2, 512], F32)
        g_acc = Buf("acc")
        rs = ar.alloc([512], F32)
        g_rs = Buf("rs")
        dec_r = Ring(ar, 1, [512], F32, "dec")
        hh_r = Ring(ar, 1, [2, 512], F32, "hh")
        sq_r = Ring(ar, 1, [2, 512], F32, "sq")
        kfo_r = Ring(ar, 1, [2, 512], F32, "kfo")
        tbf = fwd_ring(L, 2)
        for o in range(2):
            k.memset("pool", acc, 0.0, (g_acc,))
            for tc in range(ntc):
                dec, dg = dec_r.get()
                k.ts("pool", dec, dlt, tcol[:, tc:tc + 1], None, ALU.mult, None, (g_c,), (dg,))
                k.act(dec, dec, AF.Exp, (dg,), (dg,), scale=-1.0)
                hh, hg_ = hh_r.get()
                for dr in range(2):
                    pt, pg = ps()
                    k.mm(pt, h2[0:65, tc * 128:(tc + 1) * 128], f3w[0:65, o * 1024 + dr * 512:o * 1024 + (dr + 1) * 512],
                         True, True, (g_h2, g_c), (pg,))
                    k.tt("dve", hh[:, dr, :], pt, dec, ALU.mult, (pg, dg), (hg_,))
                if tc == 0:
                    k.memset("dve", hh[0:1, 1, :], 0.0, (hg_,))
                sq, sqg = sq_r.get()
                k.tt("pool", sq, hh, hh, ALU.mult, (hg_,), (sqg,))
                k.tt("pool", acc, acc, sq, ALU.add, (sqg, g_acc), (g_acc,))
                k.tt("dve", PQ[:, 0, tc, :], hh[:, 0, :], hh[:, 1, :], ALU.add, (hg_,), (g_pq,))
                k.tt("dve", PQ[:, 1, tc, :], hh[:, 1, :], hh[:, 0, :], ALU.subtract, (hg_,), (g_pq,))
            k.tt("pool", acc[:, 0, :], acc[:, 0, :], acc[:, 1, :], ALU.add, (g_acc,), (g_acc,))
            pt, pg = ps()
            k.mm(pt, ones_f, acc[:, 0, :], True, True, (g_ones, g_acc), (pg,))
            k.act(rs, pt, AF.Sqrt, (pg, g_cst), (g_rs,), bias=cst[:, 0:1], scale=1.0)
            k.op("dve", lambda eh: eh.reciprocal(out=rs, in_=rs), (g_rs,), (g_rs,))

            def epi(fc, pc, gc, pS, gS, o=o):
                kt, kg = kfo_r.get()
                k.tt("dve", kt[:, 0, :], pc, rs, ALU.mult, (gc, g_rs), (kg,))
                k.tt("dve", kt[:, 0, :], kt[:, 0, :], skb[:, o, :], ALU.add, (kg, g_c), (kg,))
                k.tt("dve", kt[:, 1, :], pS, rs, ALU.mult, (gS, g_rs), (kg,))
                k.ts("pool", kt, kt, wN[:, fc:fc + 1], None, ALU.mult, None, (kg, g_c), (kg,))
                k.dma("sp", KF[li, o, fc], kt, (kg,), (gKF,))
            dft_forward(L, PQ[:, 0], PQ[:, 1], g_pq, epi, tbf)
        k.barrier()
        ar.top = mark

    def even_mixer(l):
        i = l // 2
        run_ctx = l < 3
        def e1():
            mark = ar.top
            wint = WT(ar.alloc([KD, 2560], BF16), KD, 2560)
            win = wint.ap
            load_cast(wint, wview(I["w_in_even"][i]))
            swT = ar.alloc([4, 128], BF16)
            g_sw = Buf("sw")
            st_, sg_ = stage.get()
            k.dma("sp", st_.rearrange("p (g q) -> p g q", g=4), I["sgu_wT"][i].rearrange("g q p -> q g p"), (), (sg_,))
            k.copy("pool", swT, st_.rearrange("p (g q) -> p g q", g=4), (sg_,), (g_sw,))
            lng = ar.alloc([512], F32)
            sbb = ar.alloc([4, 128], F32)
            k.dma("sp", lng, I["sgu_lng"][i].partition_broadcast(128), (), (g_sw,))
            k.dma("sp", sbb, I["sgu_b"][i].rearrange("g p -> (g p)").partition_broadcast(128), (), (g_sw,))
            nr = NormRes()
            xr = Ring(ar, 5, [D], F32, "xe")
            hxr = Ring(ar, 2, [KD, 512], BF16, "hxe")
            zar = Ring(ar, 2, [12, 512], F32, "zae")
            ur = Ring(ar, 2, [4, 512], BF16, "ue")
            vbr = Ring(ar, 2, [512], F32, "vbe")
            vnr = Ring(ar, 2, [512], BF16, "vne")
            str_ = Ring(ar, 4, [4], F32, "ste")
            ybr = Ring(ar, 2, [4, 512], BF16, "ybe")
            tsr = Ring(ar, 2, [512], F32, "tse")
            for (t0, n) in tile_list(512, run_ctx):
                w = 1 if t0 >= T else 0
                xs = []
                for j in range(n // 128):
                    xa, xg = xr.get()
                    load_x("act", xa, xg, "X", False, t0 + j * 128)
                    xs.append((xa, xg))
                hxT, hxg = hxr.get()
                norm_tile(nr, xs, w, 0, hxT, hxg)
                za, zg = zar.get()
                for c in range(12):
                    pt, pg = ps()
                    for kk in range(KD):
                        k.mm(pt[:, 0:n], win[:, kk, c * 128:(c + 1) * 128], hxT[:, kk, 0:n], kk == 0, kk == KD - 1,
                             (hxg,) + wint.gs(kk, c * 128, (c + 1) * 128), (pg,))
                    k.act(za[:, c, 0:n], pt[:, 0:n], AF.Copy, (pg,), (zg,))
                k.dma("act", ZA[:, :, t0:t0 + n], za[:, :, 0:n], (zg,), (gZA,))
                u, ug = ur.get()
                for c in range(4):
                    pt, pg = ps()
                    for kk in range(KD):
                        k.mm(pt[:, 0:n], win[:, kk, 1536 + c * 128:1536 + (c + 1) * 128], hxT[:, kk, 0:n], kk == 0,
                             kk == KD - 1, (hxg,) + wint.gs(kk, 1536 + c * 128, 1536 + (c + 1) * 128), (pg,))
                    k.act(u[:, c, 0:n], pt[:, 0:n], AF.Gelu_apprx_tanh, (pg,), (ug,))
                yb, ybg = ybr.get()
                for j in range(n // 128):
                    pt, pg = ps()
                    for kk in range(KD):
                        k.mm(pt, hxT[:, kk, j * 128:(j + 1) * 128], win[:, kk, 2048:2560], kk == 0, kk == KD - 1,
                             (hxg,) + wint.gs(kk, 2048, 2560), (pg,))
                    vb, vbg = vbr.get()
                    k.act(vb, pt, AF.Gelu_apprx_tanh, (pg,), (vbg,))
                    st_, stg = str_.get()
                    k.op("dve", lambda eh, st_=st_, vb=vb: eh.reduce_sum(out=st_[:, 0:1], in_=vb, axis=AX.X), (vbg,), (stg,))
                    k.ts("dve", st_[:, 1:2], st_[:, 0:1], 1.0 / 512, None, ALU.mult, None, (stg,), (stg,))
                    k.ts("dve", vb, vb, st_[:, 1:2], None, ALU.subtract, None, (vbg, stg), (vbg,))
                    k.memset("pool", st_[:, 2:3], 0.0, (stg,))
                    k.act(nr.junk[:, 0:512], vb, AF.Square, (vbg, stg), (nr.g_junk, stg), accum_out=st_[:, 2:3])
                    k.act(st_[:, 3:4], st_[:, 2:3], AF.Sqrt, (stg, g_cst), (stg,), bias=cst[:, 0:1], scale=1.0 / 512)
                    k.op("dve", lambda eh, st_=st_: eh.reciprocal(out=st_[:, 3:4], in_=st_[:, 3:4]), (stg,), (stg,))
                    vn, vng = vnr.get()
                    k.stt("dve", vn, vb, st_[:, 3:4], lng, ALU.mult, ALU.mult, (vbg, stg, g_sw), (vng,))
                    p2, g2 = ps()
                    for g in range(4):
                        k.mm(p2[:, g * 128:(g + 1) * 128], vn[:, g * 128:(g + 1) * 128], swT[:, g, :], True, True,
                             (vng, g_sw), (g2,))
                    tsb, tsg = tsr.get()
                    k.tt("dve", tsb, p2, sbb.rearrange("p g q -> p (g q)"), ALU.add, (g2, g_sw), (tsg,))
                    k.tt("pool", yb[:, :, j * 128:(j + 1) * 128], tsb.rearrange("p (g q) -> p g q", g=4),
                         u[:, :, j * 128:(j + 1) * 128], ALU.mult, (tsg, ug), (ybg,))
                k.dma("sp", YBd[:, :, t0:t0 + n], yb[:, :, 0:n], (ybg,), (gYB,))
            k.barrier()
            ar.top = mark
        if go('E1'):
            e1()
        if go('kf_gen T'):
            kf_gen(i, T, 0)
        if run_ctx and go('kf_gen TC'):
            kf_gen(i, TC, 1)
        for (L, pos0, li) in ((T, 0, 0), (TC, T, 1)):
            if li == 1 and not run_ctx:
                continue
            if go('hyena_seq %d' % L):
                hyena_seq(i, L, pos0, li)
        if not go('E4'):
            return
        mark = ar.top
        import os
        if os.environ.get('PADWO'):
            ar.alloc([int(os.environ['PADWO']) * 256], F32)
        wot = WT(ar.alloc([KD, D], BF16), KD, D)
        load_cast(wot, wview(I["w_out_even"][i]))
        yr_ = Ring(ar, 2, [KD, 512], BF16, "ycat")

        def yload(t0, n):
            ya, yg = yr_.get()
            k.dma("sp", ya[:, 0:4, 0:n], YAd[:, :, t0:t0 + n], (gYA,), (yg,))
            k.dma("sp", ya[:, 4:8, 0:n], YBd[:, :, t0:t0 + n], (gYB,), (yg,))
            return ya, yg
        outproj_resid(l, yload, [128] * 8, wot, "X", False, "X", run_ctx)
        k.barrier()
        ar.top = mark

    def hyena_seq(i, L, pos0, li):
        ntc = L // 128
        nfc = NF if L == T else NFC
        tw = 512 if L == T else 256
        mark = ar.top
        vT = ar.alloc([ntc, 512], BF16)
        g_vT = Buf("vT")
        cw_ = ar.alloc([12, 3], F32)
        cb_ = ar.alloc([12], F32)
        g_cw = Buf("cw")
        k.dma("sp", cw_, I["hy_cwT"][i], (), (g_cw,))
        k.dma("sp", cb_, I["hy_cbT"][i], (), (g_cw,))
        mark2 = ar.top
        zin_r = Ring(ar, 2, [L + 2], F32, "zin")
        zo_r = Ring(ar, 2, [L], F32, "zo")
        for j in range(2):
            k.memset("pool", zin_r.aps[j][:, 0:1], 0.0, (zin_r.bufs[j],))
            k.memset("pool", zin_r.aps[j][:, L + 1:L + 2], 0.0, (zin_r.bufs[j],))
        for c in range(12):
            zi, zig = zin_r.get()
            k.dma("sp", zi[:, 1:L + 1], ZA[:, c, pos0:pos0 + L], (gZA,), (zig,))
            zo, zog = zo_r.get()
            k.act(zo, zi[:, 1:L + 1], AF.Identity, (zig, g_cw), (zog,), bias=cb_[:, c:c + 1], scale=cw_[:, c, 1:2])
            k.stt("dve", zo, zi[:, 0:L], cw_[:, c, 0:1], zo, ALU.mult, ALU.add, (zig, g_cw, zog), (zog,))
            k.stt("dve", zo, zi[:, 2:L + 2], cw_[:, c, 2:3], zo, ALU.mult, ALU.add, (zig, g_cw, zog), (zog,))
            if c < 4:
                for tc in range(ntc):
                    pt, pg = ps()
                    k.tr(pt[:, 0:128], zo[:, tc * 128:(tc + 1) * 128], ident, (zog, g_ident), (pg,))
                    k.act(vT[:, tc, c * 128:(c + 1) * 128], pt[:, 0:128], AF.Copy, (pg,), (g_vT,))
            elif c < 8:
                k.dma("sp", G1[:, c - 4, pos0:pos0 + L], zo, (zog,), (gG1,))
            else:
                k.dma("sp", G2[:, c - 8, pos0:pos0 + L], zo, (zog,), (gG2,))
        k.barrier()
        ar.top = mark2
        Ysp = ar.alloc([nfc, 2, 512], BF16)
        g_Y = Buf("Ysp")
        kf_r = Ring(ar, 2, [2, 512], F32, "kft")
        t_r = Ring(ar, 2, [4, 512], F32, "ytmp")
        g_r = Ring(ar, 3, [tw], F32, "gt")
        y_r = Ring(ar, 3, [tw], F32, "y1")
        yo_r = Ring(ar, 3, [tw], BF16, "yo")
        tbf = fwd_ring(L, 3)
        tbi = inv_ring(L)
        for o in range(2):
            def epi_f(fc, pc, gc, pS, gS, o=o):
                kt, kg = kf_r.get()
                k.dma("sp", kt, KF[li, o, fc], (gKF,), (kg,))
                tm, tmg = t_r.get()
                k.tt("dve", tm[:, 0, :], pc, kt[:, 0, :], ALU.mult, (gc, kg), (tmg,))
                k.tt("dve", tm[:, 1, :], pS, kt[:, 1, :], ALU.mult, (gS, kg), (tmg,))
                k.tt("dve", tm[:, 2, :], pS, kt[:, 0, :], ALU.mult, (gS, kg), (tmg,))
                k.tt("dve", tm[:, 3, :], pc, kt[:, 1, :], ALU.mult, (gc, kg), (tmg,))
                k.tt("pool", Ysp[:, fc, 0, :], tm[:, 0, :], tm[:, 1, :], ALU.add, (tmg,), (g_Y,))
                k.tt("pool", Ysp[:, fc, 1, :], tm[:, 2, :], tm[:, 3, :], ALU.subtract, (tmg,), (g_Y,))
            dft_forward(L, vT, vT, g_vT, epi_f, tbf)

            def epi_i(cc, tg, pt, pg, o=o):
                gt, gg = g_r.get()
                gsrc = G1 if o == 0 else G2
                k.dma("sp", gt, gsrc[:, cc, pos0 + tg * tw:pos0 + (tg + 1) * tw], (gG1 if o == 0 else gG2,), (gg,))
                if o == 0:
                    y1, y1g = y_r.get()
                    k.tt("dve", y1, pt[:, 0:tw], gt, ALU.mult, (pg, gg), (y1g,))
                    for s in range(tw // 128):
                        p2, g2 = ps()
                        k.tr(p2[:, 0:128], y1[:, s * 128:(s + 1) * 128], ident, (y1g, g_ident), (g2,))
                        k.act(vT[:, tg * (tw // 128) + s, cc * 128:(cc + 1) * 128], p2[:, 0:128], AF.Copy, (g2,), (g_vT,))
                else:
                    yo, yog = yo_r.get()
                    k.tt("dve", yo, pt[:, 0:tw], gt, ALU.mult, (pg, gg), (yog,))
                    k.dma("sp", YAd[:, cc, pos0 + tg * tw:pos0 + (tg + 1) * tw], yo, (yog,), (gYA,))
            dft_inverse(L, Ysp, g_Y, epi_i, tbi)
        k.barrier()
        ar.top = mark

    def odd_mixer(l):
        i = l // 2
        run_ctx = l < 3
        perm = (i % 2 == 1)
        dst = "XP" if perm else "X"
        mark = ar.top
        wint = WT(ar.alloc([KD, 2 * DR], BF16), KD, 2 * DR)
        win = wint.ap
        load_cast(wint, wview(I["w_in_odd"][i]))
        nr = NormRes()
        xr = Ring(ar, 5, [D], F32, "xo1")
        hxr = Ring(ar, 2, [KD, 512], BF16, "hxo")
        gtr = Ring(ar, 1, [NH, 512], BF16, "gto", parts=HD)
        zrr = Ring(ar, 1, [NH, 512], F32, "zro", parts=HD)
        for (t0, n) in tile_list(512, True):
            w = 1 if t0 >= T else 0
            xs = []
            for j in range(n // 128):
                xa, xg = xr.get()
                load_x("act", xa, xg, "X", perm, t0 + j * 128)
                xs.append((xa, xg))
            hxT, hxg = hxr.get()
            norm_tile(nr, xs, w, 0, hxT, hxg)
            gt, gtg = gtr.get()
            zr, zrg = zrr.get()
            for h in range(NH):
                for part in range(2):
                    c0 = part * DR + h * HD
                    pt, pg = ps()
                    for kk in range(KD):
                        k.mm(pt[0:HD, 0:n], win[:, kk, c0:c0 + HD], hxT[:, kk, 0:n], kk == 0, kk == KD - 1,
                             (hxg,) + wint.gs(kk, c0, c0 + HD), (pg,))
                    if part == 0:
                        k.act(gt[:, h, 0:n], pt[0:HD, 0:n], AF.Gelu_apprx_tanh, (pg,), (gtg,))
                    else:
                        k.copy("dve", zr[:, h, 0:n], pt[0:HD, 0:n], (pg,), (zrg,))
            k.dma("act", GTd[:, :, t0:t0 + n], gt[:, :, 0:n], (gtg,), (gGT,))
            k.dma("sp", ZRd[:, :, t0:t0 + n], zr[:, :, 0:n], (zrg,), (gZR,))
        k.barrier()
        ar.top = mark
        wa = ar.alloc([2, 2, NH, HD], BF16, parts=HD)
        g_wa = Buf("wa")
        for d in range(2):
            for ax, nm in enumerate(("rg_wa", "rg_wx")):
                sv = I[nm][i, d].rearrange("h d e -> d h e")
                for h0 in range(0, NH, 4):
                    st_, sg_ = stage.get()
                    sview = st_[0:HD, 0:4 * HD].rearrange("p (h e) -> p h e", h=4)
                    k.dma("sp", sview, sv[:, h0:h0 + 4, :], (), (sg_,))
                    k.copy("pool", wa[:, d, ax, h0:h0 + 4, :], sview, (sg_,), (g_wa,))
        vecs = ar.alloc([2, 3, NH], F32, parts=HD)
        csp = ar.alloc([2, NH], F32, parts=HD)
        cwv = ar.alloc([NH, 4], F32, parts=HD)
        cbv = ar.alloc([NH], F32, parts=HD)
        g_v = Buf("ovec")
        for d in range(2):
            k.dma("sp", vecs[:, d, 0, :], I["rg_baT"][i, d], (), (g_v,))
            k.dma("sp", vecs[:, d, 1, :], I["rg_bxT"][i, d], (), (g_v,))
            k.dma("sp", vecs[:, d, 2, :], I["rg_lamT"][i, d], (), (g_v,))
        k.dma("sp", cwv, I["rg_cwT"][i], (), (g_v,))
        k.dma("sp", cbv, I["rg_cbT"][i], (), (g_v,))
        for d in range(2):
            k.act(csp[:, d, :], vecs[:, d, 2, :], AF.Exp, (g_v,), (g_v,), scale=-1.0)
            k.act(csp[:, d, :], csp[:, d, :], AF.Ln, (g_v, g_cst), (g_v,), bias=cst[0:HD, 1:2], scale=1.0)
            k.ts("dve", csp[:, d, :], csp[:, d, :], -8.0, None, ALU.mult, None, (g_v,), (g_v,))
        LL = (TC, T)
        zl_r = Ring(ar, 2, [T + 3], F32, "zl", parts=HD)
        zc_r = Ring(ar, 2, [TC + 3], F32, "zc", parts=HD)
        for rr, Lx in ((zl_r, T), (zc_r, TC)):
            for j in range(2):
                k.memset("pool", rr.aps[j][:, 0:2], 0.0, (rr.bufs[j],))
                k.memset("pool", rr.aps[j][:, Lx + 2:Lx + 3], 0.0, (rr.bufs[j],))
        xl_r = Ring(ar, 2, [TA], F32, "xl", parts=HD)
        xb_r = Ring(ar, 2, [TA], BF16, "xb", parts=HD)
        hs_r = Ring(ar, 2, [TA], F32, "hs", parts=HD)
        hb_r = Ring(ar, 2, [512], F32, "hb", parts=HD)
        gl_r = Ring(ar, 1, [TA], BF16, "gl", parts=HD)
        yo_r = Ring(ar, 1, [TA], BF16, "yro", parts=HD)
        e_r = [Ring(ar, 2, [512], F32, "e%d" % j, parts=HD) for j in range(5)]
        st_r = Ring(ar, 4, [2], F32, "sto", parts=HD)
        for h in range(NH):
            zl, zlg = zl_r.get()
            zc, zcg = zc_r.get()
            k.dma("sp", zl[:, 2:T + 2], ZRd[:, h, 0:T], (gZR,), (zlg,))
            k.dma("sp", zc[:, 2:TC + 2], ZRd[:, h, T:TA], (gZR,), (zcg,))
            xl, xlg = xl_r.get()
            for (zz, zzg, Lx, o0) in ((zl, zlg, T, 0), (zc, zcg, TC, T)):
                k.act(xl[:, o0:o0 + Lx], zz[:, 2:Lx + 2], AF.Identity, (zzg, g_v), (xlg,), bias=cbv[:, h:h + 1],
                      scale=cwv[:, h, 2:3])
                for (kk, off) in ((0, 0), (1, 1), (3, 3)):
                    k.stt("dve", xl[:, o0:o0 + Lx], zz[:, off:off + Lx], cwv[:, h, kk:kk + 1], xl[:, o0:o0 + Lx],
                          ALU.mult, ALU.add, (zzg, g_v, xlg), (xlg,))
            xb, xbg = xb_r.get()
            k.copy("pool", xb, xl, (xlg,), (xbg,))
            hs, hsg = hs_r.get()
            for d in range(2):
                segs = []
                for (o0, Lx) in ((T, TC), (0, T)):
                    cs = list(range(o0, o0 + Lx, 512))
                    if d == 1:
                        cs = cs[::-1]
                    segs.append([(c0, min(512, Lx)) for c0 in cs])
                prev = None
                for si, seg in enumerate(segs):
                    for (c0, n) in seg:
                        pa, pag = ps()
                        k.mm(pa[0:HD, 0:n], wa[:, d, 0, h, :], xb[:, c0:c0 + n], True, True, (g_wa, xbg), (pag,))
                        px, pxg = ps()
                        k.mm(px[0:HD, 0:n], wa[:, d, 1, h, :], xb[:, c0:c0 + n], True, True, (g_wa, xbg), (pxg,))
                        r_, rg = e_r[0].get()
                        k.act(r_[:, 0:n], pa[0:HD, 0:n], AF.Sigmoid, (pag, g_v), (rg,), bias=vecs[:, d, 0, h:h + 1])
                        gi, gig = e_r[1].get()
                        k.act(gi[:, 0:n], px[0:HD, 0:n], AF.Sigmoid, (pxg, g_v), (gig,), bias=vecs[:, d, 1, h:h + 1])
                        a_, ag = e_r[2].get()
                        k.act(a_[:, 0:n], r_[:, 0:n], AF.Exp, (rg, g_v), (ag,), scale=csp[:, d, h:h + 1])
                        s_, sg_ = e_r[3].get()
                        k.tt("pool", s_[:, 0:n], a_[:, 0:n], a_[:, 0:n], ALU.mult, (ag,), (sg_,))
                        k.act(s_[:, 0:n], s_[:, 0:n], AF.Sqrt, (sg_, g_cst), (sg_,), bias=cst[0:HD, 1:2], scale=-1.0)
                        bx, bxg = e_r[4].get()
                        k.tt("pool", bx[:, 0:n], gi[:, 0:n], xl[:, c0:c0 + n], ALU.mult, (gig, xlg), (bxg,))
                        k.tt("dve", bx[:, 0:n], bx[:, 0:n], s_[:, 0:n], ALU.mult, (bxg, sg_), (bxg,))
                        if d == 0:
                            dst_ap, dst_g = hs[:, c0:c0 + n], hsg
                        else:
                            hbt, hbg = hb_r.get()
                            dst_ap, dst_g = hbt[:, 0:n], hbg
                        init = 0.0 if prev is None else prev[0]
                        rd = (ag, bxg) + (() if prev is None else (prev[1],))
                        if d == 0:
                            k.op("dve", lambda eh, o=dst_ap, a=a_[:, 0:n], b=bx[:, 0:n], ini=init:
                                 eh.tensor_tensor_scan(out=o, data0=a, data1=b, initial=ini, op0=ALU.mult, op1=ALU.add),
                                 rd, (dst_g,))
                            prev = (hs[:, c0 + n - 1:c0 + n], hsg)
                        else:
                            k.op("dve", lambda eh, o=dst_ap, a=a_[:, 0:n], b=bx[:, 0:n], ini=init:
                                 eh.tensor_tensor_scan(out=o[:, ::-1], data0=a[:, ::-1], data1=b[:, ::-1], initial=ini,
                                                       op0=ALU.mult, op1=ALU.add), rd, (dst_g,))
                            st_, stg = st_r.get()
                            k.copy("dve", st_[:, 0:1], dst_ap[:, 0:1], (dst_g,), (stg,))
                            prev = (st_[:, 0:1], stg)
                            k.tt("pool", hs[:, c0:c0 + n], hs[:, c0:c0 + n], dst_ap, ALU.add, (hsg, dst_g), (hsg,))
            gl, glg = gl_r.get()
            k.dma("sp", gl, GTd[:, h, :], (gGT,), (glg,))
            yo, yog = yo_r.get()
            k.tt("pool", yo, hs, gl, ALU.mult, (hsg, glg), (yog,))
            k.dma("sp", YRd[:, h, :], yo, (yog,), (gYR,))
        k.barrier()
        ar.top = mark
        wot = WT(ar.alloc([NH, D], BF16, parts=HD), NH, D)
        load_cast(wot, I["w_out_odd"][i].rearrange("(h e) n -> e h n", e=HD), parts=HD)
        yr_ = Ring(ar, 2, [NH, 512], BF16, "yro3", parts=HD)

        def yload(t0, n):
            ya, yg = yr_.get()
            k.dma("sp", ya[:, :, 0:n], YRd[:, :, t0:t0 + n], (gYR,), (yg,))
            return ya, yg
        outproj_resid(l, yload, [HD] * NH, wot, "X", perm, dst, run_ctx)
        k.barrier()
        ar.top = mark

    def final_norm(src, perm):
        mark = ar.top
        gfb = ar.alloc([D], F32)
        g_gf = Buf("gf")
        k.dma("sp", gfb, I["gfin"].partition_broadcast(128), (), (g_gf,))
        nr = NormRes()
        xr = Ring(ar, 4, [D], F32, "xfn")
        orr = Ring(ar, 3, [D], F32, "ofn")
        for pos0 in range(0, T, 128):
            xa, xg = xr.get()
            load_x("act", xa, xg, src, False, pos0)
            ss, ssg = nr.ss.get()
            k.memset("pool", ss[:, 0:1], 0.0, (ssg,))
            k.act(nr.junk, xa, AF.Square, (xg, ssg), (nr.g_junk, ssg), accum_out=ss[:, 0:1])
            k.act(ss[:, 1:2], ss[:, 0:1], AF.Sqrt, (ssg, g_cst), (ssg,), bias=cst[:, 0:1], scale=1.0 / D)
            k.op("dve", lambda eh, ss=ss: eh.reciprocal(out=ss[:, 2:3], in_=ss[:, 1:2]), (ssg,), (ssg,))
            oa, og = orr.get()
            k.stt("dve", oa, xa, ss[:, 2:3], gfb, ALU.mult, ALU.mult, (xg, ssg, g_gf), (og,))
            for (p0, p1, rows) in xrows(out_d, perm, pos0):
                k.dma("sp", rows, oa[p0:p1, :], (og,), ())
        k.barrier()
        ar.top = mark

    for r0 in range(0, T, 512):
        k.dma("sp", X[r0:r0 + 512, :], I["x"][r0:r0 + 512, :], (), tuple(gX[r0 // 128:r0 // 128 + 4]))
    k.dma("sp", X[T:TA, :], I["ctx"], (), tuple(gX[T // 128:]))
    k.barrier()
    cur = "X"
    for l in range(nl):
        if go('adaln'):
            adaln(l)
        if l % 2 == 0:
            even_mixer(l)
        else:
            odd_mixer(l)
            if (l // 2) % 2 == 1:
                cur = "XP"
        if go('ffn'):
            ffn(l, cur, l < 3)
    if go('final'):
        final_norm(cur, cur == "XP")
    if dbg:
        k.dma("sp", dbg_d[0:T, :], (XP if cur == "XP" else X)[0:T, :], (), ())
        k.dma("sp", dbg_d[T:TA, :], X[T:TA, :], (), ())
    k.finish()
    return nc, k


def _tbl(L):
    N = 2 * L
    nf = (L + 1 + 127) // 128 * 128
    idx = np.arange(nf, dtype=np.int64)
    prod = (idx[:, None] * idx[None, :]) % N
    ang = prod.astype(np.float64) * (2.0 * np.pi / N)
    valid = (idx <= L)
    m = (valid[:, None] & valid[None, :])
    Gc = np.where(m, np.cos(ang), 0.0)
    Gs = np.where(m, np.sin(ang), 0.0)
    nb = nf // 128

    def colblk(G):
        return np.ascontiguousarray(G.reshape(nb, 128, nb, 128).transpose(2, 1, 0, 3)).astype(ml_dtypes.bfloat16)
    wN = np.where(valid, 2.0, 0.0)
    wN[0] = 1.0
    wN[L] = 1.0
    wN = (wN / N).astype(np.float32)
    wNT = np.ascontiguousarray(wN.reshape(nb, 128).T)
    return (colblk(Gc), colblk(Gs), Gc.astype(ml_dtypes.bfloat16), Gs.astype(ml_dtypes.bfloat16), wNT)


def _emb(L):
    t = np.linspace(0.0, 1.0, L, dtype=np.float32)[:, None]
    w = (2.0 * np.float32(math.pi) * np.arange(L, dtype=np.float32)[:, None] / np.float32(L)).astype(np.float32)
    f = np.linspace(1e-4, 15, 16, dtype=np.float32)[None, :]
    emb = np.concatenate([t, np.cos(f * w), -np.sin(f * w)], axis=-1).astype(np.float32)
    tcol = np.ascontiguousarray(t[:, 0].reshape(L // 128, 128).T)
    return np.ascontiguousarray(emb.T), tcol


_CONST = {}


def _consts():
    if _CONST:
        return _CONST
    c = {}
    c["ident"] = np.eye(128, dtype=np.float32)
    c["TBLc"], c["TBLs"], c["GRc"], c["GRs"], c["wN"] = _tbl(T)
    c["TBLc_c"], c["TBLs_c"], c["GRc_c"], c["GRs_c"], c["wN_c"] = _tbl(TC)
    c["embT"], c["tcol"] = _emb(T)
    c["embT_c"], c["tcol_c"] = _emb(TC)
    c["deltas"] = np.abs(np.linspace(math.log(1e-2) / 1.5, math.log(1e-2) / 0.3, 512, dtype=np.float32)).astype(np.float32)
    _CONST.update(c)
    return _CONST


BF = ml_dtypes.bfloat16
INPUT_SPECS = [
    ("x", (T, D), F32), ("ctx", (TC, D), F32), ("ccT", (128, KD, 2), F32),
    ("w_ada", (4, D, 6 * D), F32), ("b_adaT", (4, 128, 48), F32), ("b_ada", (4, 6 * D), F32),
    ("gmT", (4, 128, KD), F32), ("gfT", (4, 128, KD), F32), ("gfin", (D,), F32),
    ("w_in_even", (2, D, 2560), F32), ("w_out_even", (2, D, D), F32),
    ("hy_cwT", (2, 128, 12, 3), F32), ("hy_cbT", (2, 128, 12), F32),
    ("f1w", (2, 33, 64), F32), ("f1bT", (2, 64, 1), F32), ("sfT", (2, 64, 1), F32),
    ("f2w", (2, 64, 64), F32), ("f2bT", (2, 64, 1), F32), ("f3wa", (2, 65, 2048), F32),
    ("hy_skip", (2, 2, 512), F32), ("sgu_lng", (2, 512), F32), ("sgu_wT", (2, 4, 128, 128), F32),
    ("sgu_b", (2, 4, 128), F32),
    ("w_in_odd", (2, D, 2 * DR), F32), ("rg_cwT", (2, HD, NH, 4), F32), ("rg_cbT", (2, HD, NH), F32),
    ("rg_wa", (2, 2, NH, HD, HD), F32), ("rg_wx", (2, 2, NH, HD, HD), F32),
    ("rg_baT", (2, 2, HD, NH), F32), ("rg_bxT", (2, 2, HD, NH), F32), ("rg_lamT", (2, 2, HD, NH), F32),
    ("w_out_odd", (2, DR, D), F32),
    ("w_ffn_in", (4, D, 2 * DFF), F32), ("w_ffn_out", (4, DFF, D), F32),
    ("ident", (128, 128), F32),
    ("TBLc", (NF, 128, NF, 128), BF16), ("TBLs", (NF, 128, NF, 128), BF16),
    ("GRc", (NF * 128, NF * 128), BF16), ("GRs", (NF * 128, NF * 128), BF16), ("wN", (128, NF), F32),
    ("TBLc_c", (NFC, 128, NFC, 128), BF16), ("TBLs_c", (NFC, 128, NFC, 128), BF16),
    ("GRc_c", (NFC * 128, NFC * 128), BF16), ("GRs_c", (NFC * 128, NFC * 128), BF16), ("wN_c", (128, NFC), F32),
    ("embT", (33, T), F32), ("tcol", (128, T // 128), F32),
    ("embT_c", (33, TC), F32), ("tcol_c", (128, TC // 128), F32),
    ("deltas", (512,), F32),
]


def make_in_maps(inp, cores):
    c = _consts()
    f = lambda a: np.ascontiguousarray(np.asarray(a, dtype=np.float32))
    sh = {}
    sh["w_ada"] = f(inp["w_ada"])
    sh["b_adaT"] = f(np.asarray(inp["b_ada"]).reshape(4, 48, 128).transpose(0, 2, 1))
    sh["b_ada"] = f(inp["b_ada"])
    sh["gmT"] = f(np.asarray(inp["norm_mix_g"]).reshape(4, KD, 128).transpose(0, 2, 1))
    sh["gfT"] = f(np.asarray(inp["norm_ffn_g"]).reshape(4, KD, 128).transpose(0, 2, 1))
    sh["gfin"] = f(inp["final_norm_g"])
    sh["w_in_even"] = f(inp["w_in_even"])
    sh["w_out_even"] = f(inp["w_out_even"])
    sh["hy_cwT"] = f(np.asarray(inp["hy_conv_w"]).reshape(2, 3, 12, 128).transpose(0, 3, 2, 1))
    sh["hy_cbT"] = f(np.asarray(inp["hy_conv_b"]).reshape(2, 12, 128).transpose(0, 2, 1))
    sh["f1w"] = f(inp["hy_f1_w"])
    sh["f1bT"] = f(np.asarray(inp["hy_f1_b"]).reshape(2, 64, 1))
    sh["sfT"] = f(np.asarray(inp["hy_sin_freq"]).reshape(2, 64, 1))
    sh["f2w"] = f(inp["hy_f2_w"])
    sh["f2bT"] = f(np.asarray(inp["hy_f2_b"]).reshape(2, 64, 1))
    sh["f3wa"] = f(np.concatenate([np.asarray(inp["hy_f3_w"]), np.asarray(inp["hy_f3_b"])[:, None, :]], axis=1))
    sh["hy_skip"] = f(inp["hy_skip"])
    sh["sgu_lng"] = f(inp["sgu_ln_g"])
    sh["sgu_wT"] = f(np.asarray(inp["sgu_w"]).transpose(0, 1, 3, 2))
    sh["sgu_b"] = f(inp["sgu_b"])
    sh["w_in_odd"] = f(inp["w_in_odd"])
    sh["rg_cwT"] = f(np.asarray(inp["rg_conv_w"]).reshape(2, 4, NH, HD).transpose(0, 3, 2, 1))
    sh["rg_cbT"] = f(np.asarray(inp["rg_conv_b"]).reshape(2, NH, HD).transpose(0, 2, 1))
    sh["rg_wa"] = f(inp["rg_wa"])
    sh["rg_wx"] = f(inp["rg_wx"])
    sh["rg_baT"] = f(np.asarray(inp["rg_ba"]).reshape(2, 2, NH, HD).transpose(0, 1, 3, 2))
    sh["rg_bxT"] = f(np.asarray(inp["rg_bx"]).reshape(2, 2, NH, HD).transpose(0, 1, 3, 2))
    sh["rg_lamT"] = f(np.asarray(inp["rg_lam"]).reshape(2, 2, NH, HD).transpose(0, 1, 3, 2))
    sh["w_out_odd"] = f(inp["w_out_odd"])
    sh["w_ffn_in"] = f(inp["w_ffn_in"])
    sh["w_ffn_out"] = f(inp["w_ffn_out"])
    for kname in ("ident", "TBLc", "TBLs", "GRc", "GRs", "wN", "TBLc_c", "TBLs_c", "GRc_c", "GRs_c", "wN_c",
                  "embT", "tcol", "embT_c", "tcol_c", "deltas"):
        sh[kname] = c[kname]
    maps = []
    x = np.asarray(inp["x"], dtype=np.float32)
    ctx = np.asarray(inp["ctx"], dtype=np.float32)
    cvec = np.asarray(inp["c"], dtype=np.float32)
    ccx = np.asarray(inp["c_ctx"], dtype=np.float32)
    for b in cores:
        m = dict(sh)
        m["x"] = np.ascontiguousarray(x[b])
        m["ctx"] = np.ascontiguousarray(ctx[b])
        cc = np.stack([cvec[b], ccx], axis=-1)
        m["ccT"] = np.ascontiguousarray(cc.reshape(KD, 128, 2).transpose(1, 0, 2))
        maps.append(m)
    return maps


def kernel(**inputs):
    nc, _ = build(4, False)
    maps = make_in_maps(inputs, range(4))
    res = run_bass_kernel_spmd(nc, maps, core_ids=list(range(4)))
    out = np.stack([np.asarray(r["out"], dtype=np.float32) for r in res.results], axis=0)
    return out
```
